# Optimizing a Trainium2 kernel written in Bass

```python
import jax, jax.numpy as jnp
from jax import lax
import numpy as np

D_MODEL = 4096
BATCH = 4
SEQ = 4096
DEPTH = 2

CTX_LEN = 256
GRID_W = 64
EPS = 1e-6
N_MOD = 6

GLA_HEADS = 4
GLA_DV = (D_MODEL // 2) // GLA_HEADS
GLA_DK = GLA_DV // 2
GLA_KEY_W = GLA_HEADS * GLA_DK
GLA_VAL_W = GLA_HEADS * GLA_DV
GLA_GATE_RANK = 16
GLA_TAU = 16.0
GLA_CHUNK = 64

SGU_GROUPS = 8
SGU_W = D_MODEL // 2
SGU_CHUNK = 128

EVEN_SIZES = (GLA_KEY_W, GLA_KEY_W, GLA_VAL_W, GLA_VAL_W, GLA_GATE_RANK, GLA_GATE_RANK, SGU_W, SGU_W)
EVEN_IN = sum(EVEN_SIZES)
EVEN_MIX_W = GLA_VAL_W + SGU_W

HEAD_DIM = 128
ATT_HEADS = D_MODEL // HEAD_DIM
ATT_KV_HEADS = ATT_HEADS // 4
ATT_GROUP = ATT_HEADS // ATT_KV_HEADS
ODD_Q_W = ATT_HEADS * HEAD_DIM
ODD_KV_W = ATT_KV_HEADS * HEAD_DIM
ODD_IN = ODD_Q_W + 2 * ODD_KV_W
WINDOW = 128
ATT_BLOCK = 128
ROPE_THETA = 10000.0

FFN_HIDDEN = -(-8 * D_MODEL // (3 * 256)) * 256

kernel_name = "hybrid_gla_sgu_swa_prefix_dit"


def rms_norm(x, gain=None):
    xf = x.astype(jnp.float32)
    y = xf * lax.rsqrt(jnp.mean(xf * xf, axis=-1, keepdims=True) + EPS)
    if gain is not None:
        y = y * gain.astype(jnp.float32)
    return y.astype(x.dtype)


def modulate(x, shift, scale):
    return rms_norm(x) * (1.0 + scale) + shift


def swiglu(h, w_gate, w_up, w_down):
    return (jax.nn.silu(h @ w_gate) * (h @ w_up)) @ w_down


def gla_heads(q, k, v, g_f, g_b, w2_f, b2_f, w2_b, b2_b):
    B, L = q.shape[:2]

    def heads(t, d):
        return jnp.swapaxes(t.reshape(B, L, GLA_HEADS, d), 1, 2)

    def log_decay(g, w2, b2):
        return heads(jax.nn.log_sigmoid((g @ w2 + b2).astype(jnp.float32)) / GLA_TAU, GLA_DK)

    return (heads(q, GLA_DK) * GLA_DK ** -0.5, heads(k, GLA_DK), heads(v, GLA_DV),
            log_decay(g_f, w2_f, b2_f), log_decay(g_b, w2_b, b2_b))


def gla_chunk_scan(q, k, v, log_a, s0, strict):
    B, H, L, dk = q.shape
    dv = v.shape[-1]
    n = L // GLA_CHUNK

    def to_chunks(t):
        return jnp.moveaxis(t.reshape(B, H, n, GLA_CHUNK, t.shape[-1]), 2, 0)

    pos = jnp.arange(GLA_CHUNK)
    mask = (pos[:, None] > pos[None, :]) if strict else (pos[:, None] >= pos[None, :])

    def step(s, inp):
        qi, ki, vi, gi = [t.astype(jnp.float32) for t in inp]
        b = jnp.cumsum(gi, axis=-2)
        inter = jnp.einsum('bhtk,bhkv->bhtv', qi * jnp.exp(b), s)
        diff = b[..., :, None, :] - b[..., None, :, :]
        decay = jnp.exp(jnp.where(mask[:, :, None], diff, -jnp.inf))
        att = jnp.einsum('bhtik,bhik->bhti', qi[..., :, None, :] * decay, ki)
        intra = jnp.einsum('bhti,bhiv->bhtv', att, vi)
        b_last = b[..., -1, :]
        s_new = jnp.exp(b_last)[..., None] * s + jnp.einsum(
            'bhik,bhiv->bhkv', ki * jnp.exp(b_last[..., None, :] - b), vi)
        return s_new, inter + intra

    s_fin, o = lax.scan(step, s0, (to_chunks(q), to_chunks(k), to_chunks(v), to_chunks(log_a)))
    o = jnp.moveaxis(o, 0, 2).reshape(B, H, L, dv)
    return o.astype(v.dtype), s_fin


def gla_final_state(k, v, log_a):
    b = jnp.cumsum(log_a, axis=2)
    w = jnp.exp(b[:, :, -1:, :] - b)
    return jnp.einsum('bhlk,bhlv->bhkv', k.astype(jnp.float32) * w, v.astype(jnp.float32))


def gla_out(o, r, gain):
    B, H, L, dv = o.shape
    o = rms_norm(jnp.swapaxes(o, 1, 2), gain).reshape(B, L, H * dv)
    return o * jax.nn.silu(r)


def chunk_sgu(u, v, v_gain, w_s, b_s):
    B, L, _ = u.shape
    n = L // SGU_CHUNK
    gw = SGU_W // SGU_GROUPS
    u = jax.nn.gelu(u)
    vg = rms_norm(jax.nn.gelu(v).reshape(B, n, SGU_CHUNK, SGU_GROUPS, gw), v_gain.reshape(SGU_GROUPS, gw))
    mixed = jnp.einsum('gts,bnsgc->bntgc', w_s, vg) + jnp.swapaxes(b_s, 0, 1)[:, :, None]
    return u * mixed.reshape(B, L, SGU_W)


def split_even(proj):
    offs = np.cumsum(EVEN_SIZES)[:-1].tolist()
    return jnp.split(proj, offs, axis=-1)


def even_mixer(h_l, h_c, w_in, w2_f, b2_f, w2_b, b2_b, gla_gain, sgu_gain, w_s, b_s, w_out, need_ctx_out):
    q_l, k_l, v_l, r_l, gf_l, gb_l, u_l, s_l = split_even(h_l @ w_in)
    q_c, k_c, v_c, r_c, gf_c, gb_c, u_c, s_c = split_even(h_c @ w_in)
    ql, kl, vl, afl, abl = gla_heads(q_l, k_l, v_l, gf_l, gb_l, w2_f, b2_f, w2_b, b2_b)
    qc, kc, vc, afc, abc = gla_heads(q_c, k_c, v_c, gf_c, gb_c, w2_f, b2_f, w2_b, b2_b)
    rev = lambda t: jnp.flip(t, axis=2)
    B = h_l.shape[0]
    if need_ctx_out:
        s0 = jnp.zeros((B, GLA_HEADS, GLA_DK, GLA_DV), jnp.float32)
        oc_f, sc_f = gla_chunk_scan(qc, kc, vc, afc, s0, False)
        oc_b, sc_b = gla_chunk_scan(rev(qc), rev(kc), rev(vc), rev(abc), s0, True)
        o_c = oc_f + rev(oc_b)
    else:
        sc_f = gla_final_state(kc, vc, afc)
        sc_b = gla_final_state(rev(kc), rev(vc), rev(abc))
    ol_f, _ = gla_chunk_scan(ql, kl, vl, afl, sc_f, False)
    ol_b, _ = gla_chunk_scan(rev(ql), rev(kl), rev(vl), rev(abl), sc_b, True)
    o_l = ol_f + rev(ol_b)
    y_l = jnp.concatenate([gla_out(o_l, r_l, gla_gain), chunk_sgu(u_l, s_l, sgu_gain, w_s, b_s)], axis=-1) @ w_out
    if need_ctx_out:
        y_c = jnp.concatenate([gla_out(o_c, r_c, gla_gain), chunk_sgu(u_c, s_c, sgu_gain, w_s, b_s)], axis=-1) @ w_out
        return y_l, y_c
    return y_l, None


def axial_rope(L):
    rows = L // GRID_W
    row = jnp.repeat(jnp.arange(rows, dtype=jnp.float32), GRID_W)
    col = jnp.tile(jnp.arange(GRID_W, dtype=jnp.float32), rows)
    n_freq = HEAD_DIM // 4
    inv_freq = ROPE_THETA ** (-jnp.arange(n_freq, dtype=jnp.float32) / n_freq)
    ang = jnp.concatenate([row[:, None] * inv_freq, col[:, None] * inv_freq], axis=-1)
    return jnp.cos(ang), jnp.sin(ang)


def apply_rope(x, cos, sin):
    x1, x2 = jnp.split(x.astype(jnp.float32), 2, axis=-1)
    cs, sn = cos[:, None], sin[:, None]
    return jnp.concatenate([x1 * cs - x2 * sn, x1 * sn + x2 * cs], axis=-1).astype(x.dtype)


def sink_softmax(scores, sink):
    m = sink
    for s in scores:
        m = jnp.maximum(m, jnp.max(s, axis=-1, keepdims=True))
    ps = [jnp.exp(s - m) for s in scores]
    den = jnp.exp(sink - m)
    for p in ps:
        den = den + jnp.sum(p, axis=-1, keepdims=True)
    return [p / den for p in ps]


def window_attention(q, k, v, k_c, v_c, sink_kg):
    B, L = q.shape[:2]
    nb = L // ATT_BLOCK
    scale = HEAD_DIM ** -0.5
    qb = q.reshape(B, nb, ATT_BLOCK, ATT_KV_HEADS, ATT_GROUP, HEAD_DIM)

    def band(t):
        tp = jnp.pad(t, ((0, 0), (ATT_BLOCK, ATT_BLOCK), (0, 0), (0, 0)))
        tb = tp.reshape(B, nb + 2, ATT_BLOCK, ATT_KV_HEADS, HEAD_DIM)
        return jnp.concatenate([tb[:, :-2], tb[:, 1:-1], tb[:, 2:]], axis=2)

    kb, vb = band(k), band(v)
    qpos = jnp.arange(ATT_BLOCK)[:, None]
    kpos = jnp.arange(3 * ATT_BLOCK)[None, :] - ATT_BLOCK
    abs_k = jnp.arange(nb)[:, None, None] * ATT_BLOCK + kpos[None]
    valid = (jnp.abs(kpos - qpos) <= WINDOW)[None] & (abs_k >= 0) & (abs_k < L)
    s_loc = jnp.einsum('bnqkgd,bnskd->bnqkgs', qb, kb, preferred_element_type=jnp.float32) * scale
    s_loc = jnp.where(valid[None, :, :, None, None, :], s_loc, -jnp.inf)
    s_ctx = jnp.einsum('bnqkgd,bskd->bnqkgs', qb, k_c, preferred_element_type=jnp.float32) * scale
    p_loc, p_ctx = sink_softmax([s_loc, s_ctx], sink_kg)
    o = (jnp.einsum('bnqkgs,bnskd->bnqkgd', p_loc.astype(v.dtype), vb)
         + jnp.einsum('bnqkgs,bskd->bnqkgd', p_ctx.astype(v.dtype), v_c))
    return o.reshape(B, L, ODD_Q_W)


def context_attention(q, k, v, sink_kg):
    B, Lc = q.shape[:2]
    qg = q.reshape(B, Lc, ATT_KV_HEADS, ATT_GROUP, HEAD_DIM)
    s = jnp.einsum('bqkgd,bskd->bqkgs', qg, k, preferred_element_type=jnp.float32) * HEAD_DIM ** -0.5
    (p,) = sink_softmax([s], sink_kg)
    o = jnp.einsum('bqkgs,bskd->bqkgd', p.astype(v.dtype), v)
    return o.reshape(B, Lc, ODD_Q_W)


def odd_mixer(h_l, h_c, w_in, q_gain, k_gain, sink, w_out, need_ctx_out):
    def qkv(h):
        B, L = h.shape[:2]
        q, k, v = jnp.split(h @ w_in, [ODD_Q_W, ODD_Q_W + ODD_KV_W], axis=-1)
        q = rms_norm(q.reshape(B, L, ATT_HEADS, HEAD_DIM), q_gain)
        k = rms_norm(k.reshape(B, L, ATT_KV_HEADS, HEAD_DIM), k_gain)
        return q, k, v.reshape(B, L, ATT_KV_HEADS, HEAD_DIM)

    q_l, k_l, v_l = qkv(h_l)
    cos, sin = axial_rope(h_l.shape[1])
    q_l, k_l = apply_rope(q_l, cos, sin), apply_rope(k_l, cos, sin)
    q_c, k_c, v_c = qkv(h_c)
    sink_kg = sink.astype(jnp.float32).reshape(ATT_KV_HEADS, ATT_GROUP, 1)
    y_l = window_attention(q_l, k_l, v_l, k_c, v_c, sink_kg) @ w_out
    if need_ctx_out:
        return y_l, context_attention(q_c, k_c, v_c, sink_kg) @ w_out
    return y_l, None


def setup_inputs(seed: int = 0) -> dict:
    key = jax.random.key(seed)
    ks = iter(jax.random.split(key, 32))
    nrm = lambda shape, scale: jax.random.normal(next(ks), shape, jnp.float32) * scale
    ne, no = (DEPTH + 1) // 2, DEPTH // 2
    return {
        "x": nrm((BATCH, SEQ, D_MODEL), 1.0),
        "c": nrm((BATCH, D_MODEL), 1.0),
        "ctx": nrm((BATCH, CTX_LEN, D_MODEL), 1.0),
        "c_ctx": nrm((D_MODEL,), 1.0),
        "ada_w": nrm((DEPTH, D_MODEL, N_MOD * D_MODEL), 0.5 * D_MODEL ** -0.5),
        "ada_b": nrm((DEPTH, N_MOD * D_MODEL), 0.02),
        "ffn_w_gate": nrm((DEPTH, D_MODEL, FFN_HIDDEN), D_MODEL ** -0.5),
        "ffn_w_up": nrm((DEPTH, D_MODEL, FFN_HIDDEN), D_MODEL ** -0.5),
        "ffn_w_down": nrm((DEPTH, FFN_HIDDEN, D_MODEL), FFN_HIDDEN ** -0.5),
        "even_w_in": nrm((ne, D_MODEL, EVEN_IN), D_MODEL ** -0.5),
        "even_gate_w2_fwd": nrm((ne, GLA_GATE_RANK, GLA_KEY_W), GLA_GATE_RANK ** -0.5),
        "even_gate_b_fwd": nrm((ne, GLA_KEY_W), 0.1),
        "even_gate_w2_bwd": nrm((ne, GLA_GATE_RANK, GLA_KEY_W), GLA_GATE_RANK ** -0.5),
        "even_gate_b_bwd": nrm((ne, GLA_KEY_W), 0.1),
        "even_gla_norm_gain": 1.0 + nrm((ne, GLA_DV), 0.02),
        "even_sgu_norm_gain": 1.0 + nrm((ne, SGU_W), 0.02),
        "even_sgu_w_s": nrm((ne, SGU_GROUPS, SGU_CHUNK, SGU_CHUNK), SGU_CHUNK ** -0.5),
        "even_sgu_b_s": 1.0 + nrm((ne, SGU_GROUPS, SGU_CHUNK), 0.02),
        "even_w_out": nrm((ne, EVEN_MIX_W, D_MODEL), EVEN_MIX_W ** -0.5),
        "odd_w_in": nrm((no, D_MODEL, ODD_IN), D_MODEL ** -0.5),
        "odd_q_norm_gain": 1.0 + nrm((no, HEAD_DIM), 0.02),
        "odd_k_norm_gain": 1.0 + nrm((no, HEAD_DIM), 0.02),
        "odd_sink": nrm((no, ATT_HEADS), 1.0),
        "odd_w_out": nrm((no, ODD_Q_W, D_MODEL), ODD_Q_W ** -0.5),
    }


def reference(x, c, ctx, c_ctx, ada_w, ada_b, ffn_w_gate, ffn_w_up, ffn_w_down,
              even_w_in, even_gate_w2_fwd, even_gate_b_fwd, even_gate_w2_bwd, even_gate_b_bwd,
              even_gla_norm_gain, even_sgu_norm_gain, even_sgu_w_s, even_sgu_b_s, even_w_out,
              odd_w_in, odd_q_norm_gain, odd_k_norm_gain, odd_sink, odd_w_out):
    for i in range(DEPTH):
        need_ctx = i < DEPTH - 1
        j = i // 2
        sh1, sc1, g1, sh2, sc2, g2 = [m[:, None] for m in jnp.split(jax.nn.silu(c) @ ada_w[i] + ada_b[i], N_MOD, axis=-1)]
        csh1, csc1, cg1, csh2, csc2, cg2 = jnp.split(jax.nn.silu(c_ctx) @ ada_w[i] + ada_b[i], N_MOD, axis=-1)
        h_l = modulate(x, sh1, sc1)
        h_c = modulate(ctx, csh1, csc1)
        if i % 2 == 0:
            y_l, y_c = even_mixer(h_l, h_c, even_w_in[j], even_gate_w2_fwd[j], even_gate_b_fwd[j],
                                  even_gate_w2_bwd[j], even_gate_b_bwd[j], even_gla_norm_gain[j],
                                  even_sgu_norm_gain[j], even_sgu_w_s[j], even_sgu_b_s[j], even_w_out[j], need_ctx)
        else:
            y_l, y_c = odd_mixer(h_l, h_c, odd_w_in[j], odd_q_norm_gain[j], odd_k_norm_gain[j],
                                 odd_sink[j], odd_w_out[j], need_ctx)
        x = x + g1 * y_l
        x = x + g2 * swiglu(modulate(x, sh2, sc2), ffn_w_gate[i], ffn_w_up[i], ffn_w_down[i])
        if need_ctx:
            ctx = ctx + cg1 * y_c
            ctx = ctx + cg2 * swiglu(modulate(ctx, csh2, csc2), ffn_w_gate[i], ffn_w_up[i], ffn_w_down[i])
    return x
```

```python
import numpy as np
from contextlib import ExitStack
import concourse.bass as bass
import concourse.mybir as mybir
from concourse.bass_utils import run_bass_kernel_spmd

F32 = mybir.dt.float32
BF16 = mybir.dt.bfloat16
AF = mybir.ActivationFunctionType
ALU = mybir.AluOpType
AX = mybir.AxisListType

D = 4096
KC = 32
EPS = 1e-6
SOFT_C = 4.0
SAME_ENGINE_SYNC = True


def default_cfg():
    return dict(NT_OWN=16, NT_HALO=1, NT_REST=15, NT_CTX=2, FH=11008, DEBUG=False, ADA=True)


class Buf:
    __slots__ = ("name", "last_w", "readers", "dma_readers")

    def __init__(self, name=""):
        self.name = name
        self.last_w = None
        self.readers = {}
        self.dma_readers = []


class Op:
    __slots__ = ("eng", "fn", "deps", "need_inc", "event", "is_dma", "batch")


class Sched:
    ENG = ("tensor", "vector", "scalar", "gpsimd", "sync")
    NPOOL = 12

    def __init__(self, nc):
        self.nc = nc
        self.pending = []
        self.batch = 0
        self.bar = {}
        self.sems = {e: nc.alloc_semaphore("sc_" + e) for e in self.ENG}
        self.cnt = {e: 0 for e in self.ENG}
        self.seen = {e: {} for e in self.ENG}
        self.pool = {q: [nc.alloc_semaphore("sd_%s%d" % (q, i)) for i in range(self.NPOOL)]
                     for q in ("sync", "gpsimd")}
        self.pool_val = {q: [0] * self.NPOOL for q in ("sync", "gpsimd")}
        self.rr = {q: 0 for q in ("sync", "gpsimd")}
        self.last_on_eng = {}
        self.n_ops = 0

    def add(self, eng, fn, reads=(), writes=(), dma=False):
        o = Op()
        o.eng = eng
        o.fn = fn
        o.is_dma = dma
        o.need_inc = False
        o.event = None
        o.batch = self.batch
        deps = set()
        for b in reads:
            if b.last_w is not None:
                deps.add(b.last_w)
        for b in writes:
            if b.last_w is not None:
                deps.add(b.last_w)
            deps.update(b.readers.values())
            deps.update(b.dma_readers)
        for b in writes:
            b.last_w = o
            b.readers = {}
            b.dma_readers = []
        for b in reads:
            if dma:
                b.dma_readers.append(o)
            else:
                b.readers[eng] = o
        dl = []
        for d in deps:
            if d is o or d.batch != self.batch:
                continue
            if d.eng == eng and not d.is_dma and not dma:
                if eng == "tensor" or not SAME_ENGINE_SYNC:
                    continue
            dl.append(d)
        o.deps = dl
        self.pending.append(o)
        return o

    def _wait(self, eng, sem_key, sem, val):
        if self.seen[eng].get(sem_key, 0) < val:
            getattr(self.nc, eng).wait_ge(sem, val)
            self.seen[eng][sem_key] = val

    def flush(self):
        ops = self.pending
        self.pending = []
        for o in ops:
            for d in o.deps:
                d.need_inc = True
        last = {}
        for o in ops:
            if not o.is_dma:
                last[o.eng] = o
        for o in last.values():
            o.need_inc = True
        nc = self.nc
        first_on_eng = set()
        for o in ops:
            e = o.eng
            h = getattr(nc, e)
            if e not in first_on_eng:
                first_on_eng.add(e)
                for (k, s, v) in self.bar.get(e, ()):
                    self._wait(e, k, s, v)
            for d in o.deps:
                k, s, v = d.event
                self._wait(e, k, s, v)
            if o.is_dma:
                i = self.rr[e]
                self.rr[e] = (i + 1) % self.NPOOL
                s = self.pool[e][i]
                v = self.pool_val[e][i]
                key = (e, i)
                if v > 0:
                    self._wait(e, key, s, v)
                ins = o.fn(h)
                ins.then_inc(s, 16)
                self.pool_val[e][i] = v + 16
                o.event = (key, s, v + 16)
            else:
                ins = o.fn(h)
                if o.need_inc:
                    self.cnt[e] += 1
                    ins.then_inc(self.sems[e], 1)
                    o.event = (e, self.sems[e], self.cnt[e])
            o.fn = None
        self.n_ops += len(ops)
        evs = [(e, self.sems[e], self.cnt[e]) for e in self.ENG if self.cnt[e] > 0]
        for q in ("sync", "gpsimd"):
            for i in range(self.NPOOL):
                if self.pool_val[q][i] > 0:
                    evs.append(((q, i), self.pool[q][i], self.pool_val[q][i]))
        self.bar = {e: evs for e in self.ENG}
        self.batch += 1

    def final_wait(self):
        for e in self.ENG:
            for (k, s, v) in self.bar.get(e, ()):
                self._wait(e, k, s, v)


class T:
    __slots__ = ("t", "b")

    def __init__(self, t, b=None):
        self.t = t
        self.b = b if b is not None else Buf()


class Ctx:
    def __init__(self, nc, cfg):
        self.nc = nc
        self.cfg = cfg
        self.S = Sched(nc)
        self.uid = 0

    def name(self, p):
        self.uid += 1
        return "%s_%d" % (p, self.uid)

    def sb(self, es, shape, dt, name="t"):
        return T(es.enter_context(self.nc.sbuf_tensor(self.name(name), list(shape), dt)))

    def ps(self, es, shape, dt=F32, name="p"):
        return T(es.enter_context(self.nc.psum_tensor(self.name(name), list(shape), dt)))

    def dma(self, q, out, in_, reads=(), writes=()):
        return self.S.add(q, lambda h, o=out, i=in_: h.dma_start(out=o, in_=i), reads, writes, dma=True)

    def mm(self, out, lhsT, rhs, start, stop, reads=(), writes=()):
        return self.S.add("tensor", lambda h, o=out, l=lhsT, r=rhs, a=start, b=stop:
                          h.matmul(o, l, r, start=a, stop=b), reads, writes)

    def tr(self, out, in_, ident, reads=(), writes=()):
        return self.S.add("tensor", lambda h, o=out, i=in_, d=ident: h.transpose(o, i, d), reads, writes)

    def act(self, out, in_, func, reads=(), writes=(), bias=None, scale=None, accum=None, eng="scalar"):
        def fn(h, o=out, i=in_, f=func, b=bias, s=scale, a=accum):
            kw = {}
            if b is not None:
                kw["bias"] = b
            if s is not None:
                kw["scale"] = s
            if a is not None:
                kw["accum_out"] = a
            return h.activation(o, i, f, **kw)
        return self.S.add(eng, fn, reads, writes)

    def tt(self, out, in0, in1, op, reads=(), writes=(), eng="vector"):
        return self.S.add(eng, lambda h, o=out, a=in0, b=in1, p=op: h.tensor_tensor(o, a, b, p), reads, writes)

    def ts(self, out, in0, s1, s2, op0, op1=None, reads=(), writes=(), eng="vector"):
        def fn(h, o=out, a=in0, x=s1, y=s2, p0=op0, p1=op1):
            if p1 is None:
                return h.tensor_scalar(o, a, x, None, p0)
            return h.tensor_scalar(o, a, x, y, p0, p1)
        return self.S.add(eng, fn, reads, writes)

    def stt(self, out, in0, scalar, in1, op0, op1, reads=(), writes=()):
        return self.S.add("vector", lambda h, o=out, a=in0, s=scalar, b=in1, p0=op0, p1=op1:
                          h.scalar_tensor_tensor(o, a, s, b, p0, p1), reads, writes)

    def cp(self, out, in_, reads=(), writes=(), eng="vector"):
        if eng == "scalar":
            return self.S.add(eng, lambda h, o=out, i=in_: h.copy(o, i), reads, writes)
        return self.S.add(eng, lambda h, o=out, i=in_: h.tensor_copy(o, i), reads, writes)

    def recip(self, out, in_, reads=(), writes=()):
        return self.S.add("vector", lambda h, o=out, i=in_: h.reciprocal(o, i), reads, writes)

    def recipf(self, out, in_, reads=(), writes=()):
        return self.S.add("vector", lambda h, o=out, i=in_: h.reciprocal_approx_fast(o, i), reads, writes)

    def memset(self, ap, val, writes=(), eng="vector"):
        return self.S.add(eng, lambda h, a=ap, v=val: h.memset(a, v), (), writes)

    def red(self, out, in_, op, reads=(), writes=()):
        return self.S.add("vector", lambda h, o=out, i=in_, p=op: h.tensor_reduce(o, i, AX.X, p), reads, writes)


class Ring:
    def __init__(self, items):
        self.items = items
        self.i = 0

    def next(self):
        x = self.items[self.i]
        self.i = (self.i + 1) % len(self.items)
        return x


def groups_of(tiles, g=4):
    return [tiles[i:i + g] for i in range(0, len(tiles), g)]


def build_program(cfg):
    NT_OWN, NT_HALO, NT_REST, NT_CTX, FH = cfg["NT_OWN"], cfg["NT_HALO"], cfg["NT_REST"], cfg["NT_CTX"], cfg["FH"]
    DEBUG = cfg["DEBUG"]
    NTF = NT_OWN + NT_HALO
    NF0 = NTF + NT_CTX
    NTT = NF0 + NT_REST
    NTOK = NTT * 128
    CTX0 = NTF
    REST0 = NF0
    FHC = FH // 128
    assert FH % 256 == 0 or FH % 128 == 0
    nc = bass.Bass("TRN2", target_bir_lowering=False)
    C = Ctx(nc, cfg)
    S = C.S

    def din(name, shape, dt=F32):
        return nc.dram_tensor(name, list(shape), dt, kind="ExternalInput").ap()

    skind = "ExternalOutput" if DEBUG else "Internal"

    def dscr(name, shape, dt):
        return nc.dram_tensor(name, list(shape), dt, kind=skind).ap()

    xT_in = din("xT", [D, NTOK])
    cT_in = din("cT", [128, KC, 2])
    if cfg["ADA"]:
        ada_w = din("ada_w", [2, D, 6 * D])
        ada_bT = din("ada_bT", [128, 2, 6, KC])
    else:
        modT_in = din("modT_in", [128, 2 * 6 * KC * 2])
    w_in0 = din("w_in0", [D, 10240])
    w_g = din("w_g", [D, 32])
    w2blk = din("w2blk", [33, 2048])
    gla_gain_bc = din("gla_gain_bc", [128, 512])
    sgainT = din("sgainT", [128, 16])
    wsT = din("wsT", [128, 8, 128])
    bsb = din("bsb", [128, 8, 128])
    w_out0 = din("w_out0", [D, D])
    w_in1 = din("w_in1", [D, 6144])
    qg_bc = din("qg_bc", [128, 128])
    kg_bc = din("kg_bc", [128, 128])
    sink_bc = din("sink_bc", [128, 32])
    w_out1 = din("w_out1", [D, D])
    ffn_g = din("ffn_g", [2, D, FH])
    ffn_u = din("ffn_u", [2, D, FH])
    ffn_d = din("ffn_d", [2, FH, D])
    rope_cs = din("rope_cs", [NTF * 128, 2, 256])
    consts = din("consts", [128, 8, 128])
    yT = nc.dram_tensor("yT", [D, NT_OWN * 128], F32, kind="ExternalOutput").ap()

    qT_s = dscr("qT_s", [1024, NTOK], F32)
    kT_s = dscr("kT_s", [1024, NTOK], F32)
    ktm_s = dscr("ktm_s", [NTOK, 1024], F32)
    v_s = dscr("v_s", [NTOK, 2048], BF16)
    sr_s = dscr("sr_s", [NTOK, 2048], F32)
    gT_s = dscr("gT_s", [32, NTOK], F32)
    guT_s = dscr("guT_s", [2048, NTOK], F32)
    vgn_s = dscr("vgn_s", [NTOK, 2048], BF16)
    qeT_s = [dscr("qeT_s%d" % d, [1024, NTOK], BF16) for d in range(2)]
    keT_s = [dscr("keT_s%d" % d, [1024, NTOK], BF16) for d in range(2)]
    kd_s = [dscr("kd_s%d" % d, [NTOK, 1024], BF16) for d in range(2)]
    ebl_s = [dscr("ebl_s%d" % d, [NTT, 128, 8], F32) for d in range(2)]
    o_s = [dscr("o_s%d" % d, [NTOK, 2048], F32) for d in range(2)]
    mixT_s = dscr("mixT_s", [D, NTOK], BF16)
    x1T_s = dscr("x1T_s", [D, NTOK], F32)
    xhT_s = dscr("xhT_s", [D, NTOK], F32)
    x2T_s = dscr("x2T_s", [D, NTOK], F32)
    qT1_s = dscr("qT1_s", [D, NTOK], BF16)
    kT1_s = dscr("kT1_s", [1024, NTOK], BF16)
    v1_s = dscr("v1_s", [NTOK, 1024], BF16)
    x3T_s = dscr("x3T_s", [D, NTOK], F32)

    with ExitStack() as glob:
        modT = C.sb(glob, [128, 2 * 6 * KC * 2], F32, "modT")
        cst = C.sb(glob, [128, 8, 128], F32, "cst")
        ident_bf = C.sb(glob, [128, 128], BF16, "identb")
        ones_f = C.sb(glob, [128, 128], F32, "onesf")
        ones_bf = C.sb(glob, [128, 128], BF16, "onesb")
        epsc = C.sb(glob, [128, 1], F32, "epsc")

        def mod(l, m, c, j):
            off = ((l * 6 + m) * KC + c) * 2 + j
            return modT.t[:, off:off + 1]

        C.dma("sync", cst.t[:], consts, (), [cst.b])
        C.cp(ident_bf.t[:], cst.t[:, 0, :], [cst.b], [ident_bf.b])
        C.memset(ones_f.t[:], 1.0, [ones_f.b])
        C.memset(ones_bf.t[:], 1.0, [ones_bf.b])
        C.memset(epsc.t[:], EPS, [epsc.b])
        S.flush()

        if not cfg["ADA"]:
            C.dma("sync", modT.t[:], modT_in, (), [modT.b])
            S.flush()
        else:
            with ExitStack() as es:
                cf = C.sb(es, [128, KC, 2], F32, "cf")
                sT = C.sb(es, [128, KC, 2], BF16, "sT")
                abT = C.sb(es, [128, 2, 6, KC], F32, "abT")
                wp = Ring([C.sb(es, [128, KC, 512], BF16, "adw") for _ in range(3)])
                pp = Ring([C.ps(es, [128, 512], F32, "adp") for _ in range(2)])
                C.dma("sync", cf.t[:], cT_in, (), [cf.b])
                C.dma("sync", abT.t[:], ada_bT, (), [abT.b])
                C.act(sT.t[:], cf.t[:], AF.Silu, [cf.b], [sT.b])
                panels = [(l, j) for l in range(2) for j in range(48)]

                def load(idx):
                    l, j = panels[idx]
                    w = wp.next()
                    C.dma("gpsimd", w.t[:], ada_w[l].rearrange("(kc p) n -> p kc n", p=128)[:, :, j * 512:(j + 1) * 512],
                          (), [w.b])
                    return w
                q = [load(0), load(1)]
                for idx, (l, j) in enumerate(panels):
                    if idx + 2 < len(panels):
                        q.append(load(idx + 2))
                    w = q.pop(0)
                    p = pp.next()
                    m, c0 = j // 8, (j % 8) * 4
                    for i in range(4):
                        for kc in range(KC):
                            C.mm(p.t[:, 2 * i:2 * i + 2], w.t[:, kc, i * 128:(i + 1) * 128], sT.t[:, kc, :],
                                 kc == 0, kc == KC - 1, [w.b, sT.b], [p.b])
                    off = ((l * 6 + m) * KC + c0) * 2
                    for jj in range(2):
                        C.tt(modT.t[:, off + jj:off + 8:2], p.t[:, jj:8:2], abT.t[:, l, m, c0:c0 + 4], ALU.add,
                             [p.b, abT.b], [modT.b])
                for l in range(2):
                    for m in (1, 4):
                        off = ((l * 6 + m) * KC) * 2
                        C.ts(modT.t[:, off:off + 2 * KC], modT.t[:, off:off + 2 * KC], 1.0, None, ALU.add,
                             None, [modT.b], [modT.b])
                S.flush()
        if DEBUG:
            modT_o = nc.dram_tensor("modT_o", [128, 2 * 6 * KC * 2], F32, kind="ExternalOutput").ap()
            C.dma("sync", modT_o, modT.t[:], [modT.b], ())
            S.flush()

        def tile_j(t):
            return 1 if CTX0 <= t < CTX0 + NT_CTX else 0

        def segs(tiles):
            out = []
            for i, t in enumerate(tiles):
                j = tile_j(t)
                if out and out[-1][2] == j:
                    out[-1][1] = (i + 1) * 128
                else:
                    out.append([i * 128, (i + 1) * 128, j])
            return out

        def norm_modulate(es, src, tiles, l, m_shift, m_scale, hT, xring, sqring, ss_ps, rstd, tmpring):
            Tn = len(tiles) * 128
            c0 = tiles[0] * 128
            srcv = src[:, c0:c0 + Tn].rearrange("(c p) t -> c p t", p=128)
            for c in range(KC):
                xc = xring.next()
                C.dma("sync", xc.t[:, :Tn], srcv[c], (), [xc.b])
                sq = sqring.next()
                C.act(sq.t[:, :Tn], xc.t[:, :Tn], AF.Square, [xc.b], [sq.b])
                C.mm(ss_ps.t[:, :Tn], ones_f.t[:], sq.t[:, :Tn], c == 0, c == KC - 1, [sq.b, ones_f.b], [ss_ps.b])
            C.act(rstd.t[:, :Tn], ss_ps.t[:, :Tn], AF.Sqrt, [ss_ps.b, epsc.b], [rstd.b], bias=epsc.t[:, 0:1], scale=1.0 / D)
            C.recip(rstd.t[:, :Tn], rstd.t[:, :Tn], [rstd.b], [rstd.b])
            sg = segs(tiles)
            for c in range(KC):
                xc = xring.next()
                C.dma("sync", xc.t[:, :Tn], srcv[c], (), [xc.b])
                tmp = tmpring.next()
                for (a, b, j) in sg:
                    C.stt(tmp.t[:, a:b], xc.t[:, a:b], mod(l, m_scale, c, j), rstd.t[:, a:b], ALU.mult, ALU.mult,
                          [xc.b, rstd.b, modT.b], [tmp.b])
                    C.act(hT.t[:, c, a:b], tmp.t[:, a:b], AF.Identity, [tmp.b, modT.b], [hT.b],
                          bias=mod(l, m_shift, c, j))

        def wview(w, c0, ncols):
            return w.rearrange("(kc p) n -> p kc n", p=128)[:, :, c0:c0 + ncols]

        def phase_l0a():
            with ExitStack() as es:
                hT = C.sb(es, [128, KC, 512], BF16, "hT")
                xring = Ring([C.sb(es, [128, 512], F32, "xc") for _ in range(4)])
                sqring = Ring([C.sb(es, [128, 512], F32, "sq") for _ in range(2)])
                tmpring = Ring([C.sb(es, [128, 512], F32, "tmp") for _ in range(2)])
                rstd = C.sb(es, [128, 512], F32, "rstd")
                ss_ps = C.ps(es, [128, 512], F32, "ssps")
                wp = Ring([C.sb(es, [128, KC, 512], BF16, "wp") for _ in range(3)])
                wgt = C.sb(es, [128, KC, 32], BF16, "wgt")
                pp = Ring([C.ps(es, [128, 512], F32, "pp") for _ in range(6)])
                st32 = Ring([C.sb(es, [128, 512], F32, "st32") for _ in range(4)])
                st16 = Ring([C.sb(es, [128, 512], BF16, "st16") for _ in range(3)])
                sm = Ring([C.sb(es, [128, 4], F32, "sm") for _ in range(4)])
                C.dma("gpsimd", wgt.t[:], w_g.rearrange("(kc p) n -> p kc n", p=128), (), [wgt.b])

                full_groups = groups_of(list(range(NF0)))
                rest_groups = groups_of(list(range(REST0, NTT)))
                for tiles, full in [(g, True) for g in full_groups] + [(g, False) for g in rest_groups]:
                    Tn = len(tiles) * 128
                    c0 = tiles[0] * 128
                    norm_modulate(es, xT_in, tiles, 0, 0, 1, hT, xring, sqring, ss_ps, rstd, tmpring)
                    plist = list(range(20)) if full else list(range(2, 8))
                    p = pp.next()
                    for kc in range(KC):
                        C.mm(p.t[0:32, :Tn], wgt.t[:, kc, :], hT.t[:, kc, :Tn], kc == 0, kc == KC - 1, [wgt.b, hT.b], [p.b])
                    st = st32.next()
                    C.cp(st.t[0:32, :Tn], p.t[0:32, :Tn], [p.b], [st.b], eng="scalar")
                    C.dma("sync", gT_s[:, c0:c0 + Tn], st.t[0:32, :Tn], [st.b], ())

                    def load(pi):
                        w = wp.next()
                        C.dma("gpsimd", w.t[:], wview(w_in0, pi * 512, 512), (), [w.b])
                        return w
                    q = [load(plist[0]), load(plist[1])]
                    for ii, pi in enumerate(plist):
                        if ii + 2 < len(plist):
                            q.append(load(plist[ii + 2]))
                        w = q.pop(0)
                        fm = (pi < 4 and full) or (12 <= pi < 16)
                        tm = (2 <= pi < 12) or pi >= 16
                        if fm:
                            for jj in range(4):
                                p = pp.next()
                                for kc in range(KC):
                                    C.mm(p.t[:, :Tn], w.t[:, kc, jj * 128:(jj + 1) * 128], hT.t[:, kc, :Tn],
                                         kc == 0, kc == KC - 1, [w.b, hT.b], [p.b])
                                st = st32.next()
                                ch = pi * 4 + jj
                                if pi < 2:
                                    C.cp(st.t[:, :Tn], p.t[:, :Tn], [p.b], [st.b], eng="scalar")
                                    C.dma("sync", qT_s[ch * 128:(ch + 1) * 128, c0:c0 + Tn], st.t[:, :Tn], [st.b], ())
                                elif pi < 4:
                                    C.cp(st.t[:, :Tn], p.t[:, :Tn], [p.b], [st.b], eng="scalar")
                                    C.dma("sync", kT_s[(ch - 8) * 128:(ch - 7) * 128, c0:c0 + Tn], st.t[:, :Tn], [st.b], ())
                                else:
                                    C.act(st.t[:, :Tn], p.t[:, :Tn], AF.Gelu_apprx_tanh, [p.b], [st.b])
                                    C.dma("sync", guT_s[(ch - 48) * 128:(ch - 47) * 128, c0:c0 + Tn], st.t[:, :Tn], [st.b], ())
                        if tm:
                            for i, t in enumerate(tiles):
                                p = pp.next()
                                for kc in range(KC):
                                    C.mm(p.t[:], hT.t[:, kc, i * 128:(i + 1) * 128], w.t[:, kc, :],
                                         kc == 0, kc == KC - 1, [w.b, hT.b], [p.b])
                                r0 = t * 128
                                if pi < 4:
                                    st = st32.next()
                                    C.cp(st.t[:], p.t[:], [p.b], [st.b], eng="vector")
                                    C.dma("sync", ktm_s[r0:r0 + 128, (pi - 2) * 512:(pi - 1) * 512], st.t[:], [st.b], ())
                                elif pi < 8:
                                    st = st16.next()
                                    C.cp(st.t[:], p.t[:], [p.b], [st.b], eng="vector")
                                    C.dma("sync", v_s[r0:r0 + 128, (pi - 4) * 512:(pi - 3) * 512], st.t[:], [st.b], ())
                                elif pi < 12:
                                    st = st32.next()
                                    C.act(st.t[:], p.t[:], AF.Silu, [p.b], [st.b])
                                    C.dma("sync", sr_s[r0:r0 + 128, (pi - 8) * 512:(pi - 7) * 512], st.t[:], [st.b], ())
                                else:
                                    st = st32.next()
                                    C.act(st.t[:], p.t[:], AF.Gelu_apprx_tanh, [p.b], [st.b])
                                    s2 = st32.next()
                                    C.tt(s2.t[:], st.t[:], st.t[:], ALU.mult, [st.b], [s2.b])
                                    ssm = sm.next()
                                    C.red(ssm.t[:, 0:2], s2.t[:].rearrange("p (g c) -> p g c", g=2), ALU.add, [s2.b], [ssm.b])
                                    C.act(ssm.t[:, 0:2], ssm.t[:, 0:2], AF.Sqrt, [ssm.b, epsc.b], [ssm.b],
                                          bias=epsc.t[:, 0:1], scale=1.0 / 256)
                                    C.recip(ssm.t[:, 0:2], ssm.t[:, 0:2], [ssm.b], [ssm.b])
                                    so = st16.next()
                                    for gg in range(2):
                                        C.ts(so.t[:, gg * 256:(gg + 1) * 256], st.t[:, gg * 256:(gg + 1) * 256],
                                             ssm.t[:, gg:gg + 1], None, ALU.mult, None, [st.b, ssm.b], [so.b])
                                    C.dma("sync", vgn_s[r0:r0 + 128, (pi - 16) * 512:(pi - 15) * 512], so.t[:], [so.b], ())
                S.flush()

        def phase_l0b():
            with ExitStack() as es:
                w2 = C.sb(es, [33, 2048], F32, "w2")
                C.dma("sync", w2.t[:], w2blk, (), [w2.b])
                gaug = Ring([C.sb(es, [33, 128], F32, "gaug") for _ in range(2)])
                for g in gaug.items:
                    C.memset(g.t[:], 1.0, [g.b])
                zps = Ring([C.ps(es, [128, 512], F32, "zps") for _ in range(4)])
                cps = Ring([C.ps(es, [128, 512], F32, "cps") for _ in range(4)])
                esb = Ring([C.sb(es, [128, 2048], F32, "esb") for _ in range(2)])
                spb = Ring([C.sb(es, [128, 2048], F32, "spb") for _ in range(2)])
                Ep = Ring([C.sb(es, [128, 1024], F32, "Ep") for _ in range(2)])
                Em = Ring([C.sb(es, [128, 1024], F32, "Em") for _ in range(2)])
                Dd = Ring([C.sb(es, [128, 1024], F32, "Dd") for _ in range(2)])
                qin = Ring([C.sb(es, [128, 8, 128], F32, "qin") for _ in range(2)])
                kin = Ring([C.sb(es, [128, 8, 128], F32, "kin") for _ in range(2)])
                ktin = Ring([C.sb(es, [128, 1024], F32, "ktin") for _ in range(2)])
                o16 = Ring([C.sb(es, [128, 1024], BF16, "o16") for _ in range(4)])
                eb = Ring([C.sb(es, [128, 8], F32, "eb") for _ in range(4)])
                for t in range(NTT):
                    full = t < NF0
                    c0 = t * 128
                    ga = gaug.next()
                    C.dma("sync", ga.t[0:32, :], gT_s[:, c0:c0 + 128], (), [ga.b])
                    e_ = esb.next()
                    sp = spb.next()
                    dirs = (0, 1) if full else (1,)
                    for d in dirs:
                        for hh in range(2):
                            z = zps.next()
                            cc = d * 1024 + hh * 512
                            C.mm(z.t[:], ga.t[:], w2.t[:, cc:cc + 512], True, True, [ga.b, w2.b], [z.b])
                            C.act(e_.t[:, cc:cc + 512], z.t[:], AF.Exp, [z.b], [e_.b], scale=-1.0)
                        C.act(sp.t[:, d * 1024:(d + 1) * 1024], e_.t[:, d * 1024:(d + 1) * 1024], AF.Ln, [e_.b], [sp.b],
                              bias=1.0)
                    if full:
                        qi = qin.next()
                        ki = kin.next()
                        C.dma("sync", qi.t[:], qT_s[:, c0:c0 + 128].rearrange("(c p) t -> p c t", p=128), (), [qi.b])
                        C.dma("sync", ki.t[:], kT_s[:, c0:c0 + 128].rearrange("(c p) t -> p c t", p=128), (), [ki.b])
                    kt = ktin.next()
                    C.dma("sync", kt.t[:], ktm_s[c0:c0 + 128, :], (), [kt.b])
                    for d in dirs:
                        ep = Ep.next()
                        em = Em.next()
                        for hh in range(2):
                            cp_ = cps.next()
                            for k4 in range(4):
                                kc = hh * 4 + k4
                                C.mm(cp_.t[:, k4 * 128:(k4 + 1) * 128], sp.t[:, d * 1024 + kc * 128:d * 1024 + (kc + 1) * 128],
                                     cst.t[:, 1 + d, :], True, True, [sp.b, cst.b], [cp_.b])
                            C.act(ep.t[:, hh * 512:(hh + 1) * 512], cp_.t[:], AF.Exp, [cp_.b], [ep.b], scale=-1.0 / 16)
                            if full:
                                C.act(em.t[:, hh * 512:(hh + 1) * 512], cp_.t[:], AF.Exp, [cp_.b], [em.b], scale=1.0 / 16)
                        ebt = eb.next()
                        col = 127 if d == 0 else 0
                        C.cp(ebt.t[:], ep.t[:].rearrange("p (c t) -> p c t", t=128)[:, :, col], [ep.b], [ebt.b], eng="vector")
                        C.dma("gpsimd", ebl_s[d][t], ebt.t[:], [ebt.b], ())
                        if full:
                            oq = o16.next()
                            C.stt(oq.t[:], qi.t[:].rearrange("p c t -> p (c t)"), 0.0625, ep.t[:], ALU.mult, ALU.mult,
                                  [qi.b, ep.b], [oq.b])
                            C.dma("gpsimd", qeT_s[d][:, c0:c0 + 128].rearrange("(c p) t -> p c t", p=128),
                                  oq.t[:].rearrange("p (c t) -> p c t", t=128), [oq.b], ())
                            ok = o16.next()
                            C.tt(ok.t[:], ki.t[:].rearrange("p c t -> p (c t)"), em.t[:], ALU.mult, [ki.b, em.b], [ok.b])
                            C.dma("gpsimd", keT_s[d][:, c0:c0 + 128].rearrange("(c p) t -> p c t", p=128),
                                  ok.t[:].rearrange("p (c t) -> p c t", t=128), [ok.b], ())
                        dd = Dd.next()
                        for hh in range(2):
                            cp_ = cps.next()
                            cc = d * 1024 + hh * 512
                            C.mm(cp_.t[:], cst.t[:, 3 + d, :], sp.t[:, cc:cc + 512], True, True, [sp.b, cst.b], [cp_.b])
                            C.act(dd.t[:, hh * 512:(hh + 1) * 512], cp_.t[:], AF.Exp, [cp_.b], [dd.b], scale=-1.0 / 16)
                        okd = o16.next()
                        C.tt(okd.t[:], kt.t[:], dd.t[:], ALU.mult, [kt.b, dd.b], [okd.b])
                        C.dma("gpsimd", kd_s[d][c0:c0 + 128, :], okd.t[:], [okd.b], ())
                S.flush()

        def phase_l0c():
            with ExitStack() as es:
                S32 = [[C.sb(es, [128, 2, 512], F32, "S32") for h in range(4)] for d in range(2)]
                Sbf = [[C.sb(es, [128, 2, 512], BF16, "Sbf") for h in range(4)] for d in range(2)]
                for d in range(2):
                    for h in range(4):
                        C.memset(S32[d][h].t[:], 0.0, [S32[d][h].b])
                        C.memset(Sbf[d][h].t[:], 0.0, [Sbf[d][h].b])
                mask = [C.sb(es, [128, 128], F32, "mask") for d in range(2)]
                C.cp(mask[0].t[:], cst.t[:, 1, :], [cst.b], [mask[0].b])
                C.cp(mask[1].t[:], cst.t[:, 3, :], [cst.b], [mask[1].b])
                qe = Ring([C.sb(es, [128, 8, 128], BF16, "qe") for _ in range(3)])
                ke = Ring([C.sb(es, [128, 8, 128], BF16, "ke") for _ in range(3)])
                kd = Ring([C.sb(es, [128, 1024], BF16, "kd") for _ in range(3)])
                vv = Ring([C.sb(es, [128, 2048], BF16, "vv") for _ in range(3)])
                ebr = Ring([C.sb(es, [128, 8], F32, "ebr") for _ in range(3)])
                attp = Ring([C.ps(es, [128, 128], F32, "attp") for _ in range(2)])
                op_ = Ring([C.ps(es, [128, 512], F32, "op") for _ in range(2)])
                sup = Ring([C.ps(es, [128, 512], F32, "sup") for _ in range(4)])
                atts = Ring([C.sb(es, [128, 128], BF16, "atts") for _ in range(3)])
                ost = Ring([C.sb(es, [128, 512], F32, "ost") for _ in range(4)])

                def step(d, t, with_out):
                    c0 = t * 128
                    if with_out:
                        q_ = qe.next()
                        k_ = ke.next()
                        C.dma("sync", q_.t[:], qeT_s[d][:, c0:c0 + 128].rearrange("(c p) t -> p c t", p=128), (), [q_.b])
                        C.dma("sync", k_.t[:], keT_s[d][:, c0:c0 + 128].rearrange("(c p) t -> p c t", p=128), (), [k_.b])
                    kd_ = kd.next()
                    v_ = vv.next()
                    eb_ = ebr.next()
                    C.dma("sync", kd_.t[:], kd_s[d][c0:c0 + 128, :], (), [kd_.b])
                    C.dma("sync", v_.t[:], v_s[c0:c0 + 128, :], (), [v_.b])
                    C.dma("sync", eb_.t[:], ebl_s[d][t], (), [eb_.b])
                    for h in range(4):
                        if with_out:
                            ap_ = attp.next()
                            for c in range(2):
                                C.mm(ap_.t[:], k_.t[:, 2 * h + c, :], q_.t[:, 2 * h + c, :], c == 0, c == 1, [k_.b, q_.b], [ap_.b])
                            as_ = atts.next()
                            C.tt(as_.t[:], ap_.t[:], mask[d].t[:], ALU.mult, [ap_.b, mask[d].b], [as_.b])
                            o_ = op_.next()
                            for c in range(2):
                                C.mm(o_.t[:], q_.t[:, 2 * h + c, :], Sbf[d][h].t[:, c, :], c == 0, False,
                                     [q_.b, Sbf[d][h].b], [o_.b])
                            C.mm(o_.t[:], as_.t[:], v_.t[:, h * 512:(h + 1) * 512], False, True, [as_.b, v_.b], [o_.b])
                            os_ = ost.next()
                            C.cp(os_.t[:], o_.t[:], [o_.b], [os_.b], eng="scalar")
                            C.dma("gpsimd", o_s[d][c0:c0 + 128, h * 512:(h + 1) * 512], os_.t[:], [os_.b], ())
                        for c in range(2):
                            su = sup.next()
                            C.mm(su.t[:], kd_.t[:, (2 * h + c) * 128:(2 * h + c + 1) * 128], v_.t[:, h * 512:(h + 1) * 512],
                                 True, True, [kd_.b, v_.b], [su.b])
                            C.stt(S32[d][h].t[:, c, :], S32[d][h].t[:, c, :], eb_.t[:, 2 * h + c:2 * h + c + 1], su.t[:],
                                  ALU.mult, ALU.add, [S32[d][h].b, eb_.b, su.b], [S32[d][h].b])
                            C.cp(Sbf[d][h].t[:, c, :], S32[d][h].t[:, c, :], [S32[d][h].b], [Sbf[d][h].b], eng="scalar")

                ctx_t = list(range(CTX0, CTX0 + NT_CTX))
                chainA = [(0, t, True) for t in ctx_t] + [(0, t, True) for t in range(NTF)]
                chainB = ([(1, t, True) for t in reversed(ctx_t)] + [(1, t, False) for t in reversed(range(REST0, NTT))]
                          + [(1, t, True) for t in reversed(range(NTF))])
                for i in range(max(len(chainA), len(chainB))):
                    if i < len(chainB):
                        step(*chainB[i])
                    if i < len(chainA):
                        step(*chainA[i])
                S.flush()

        def phase_l0d():
            with ExitStack() as es:
                gg = C.sb(es, [128, 512], F32, "gg")
                sgn = C.sb(es, [128, 16], F32, "sgn")
                ws = C.sb(es, [128, 8, 128], F32, "wsf")
                wsb = C.sb(es, [128, 8, 128], BF16, "wsb")
                bs = C.sb(es, [128, 8, 128], F32, "bsb")
                C.dma("sync", gg.t[:], gla_gain_bc, (), [gg.b])
                C.dma("sync", sgn.t[:], sgainT, (), [sgn.b])
                C.dma("sync", ws.t[:], wsT, (), [ws.b])
                C.dma("sync", bs.t[:], bsb, (), [bs.b])
                C.cp(wsb.t[:], ws.t[:], [ws.b], [wsb.b])
                oa = Ring([C.sb(es, [128, 2048], F32, "oa") for _ in range(2)])
                ob = Ring([C.sb(es, [128, 2048], F32, "ob") for _ in range(2)])
                sr = Ring([C.sb(es, [128, 2048], F32, "sr") for _ in range(2)])
                sqj = C.sb(es, [128, 512], F32, "sqj")
                vg = Ring([C.sb(es, [128, 2048], BF16, "vg") for _ in range(2)])
                gu = Ring([C.sb(es, [128, 16, 128], F32, "gu") for _ in range(2)])
                mtm = Ring([C.sb(es, [128, 2048], BF16, "mtm") for _ in range(2)])
                mst = Ring([C.sb(es, [128, 32, 128], BF16, "mst") for _ in range(2)])
                ssm = Ring([C.sb(es, [128, 4], F32, "ssm") for _ in range(2)])
                tps = Ring([C.ps(es, [128, 4, 128], BF16, "tps") for _ in range(2)])
                sps = Ring([C.ps(es, [128, 128], F32, "sps") for _ in range(4)])
                t32 = Ring([C.sb(es, [128, 128], F32, "t32") for _ in range(3)])
                for t in range(NF0):
                    c0 = t * 128
                    a_, b_, r_, v_, u_ = oa.next(), ob.next(), sr.next(), vg.next(), gu.next()
                    C.dma("sync", a_.t[:], o_s[0][c0:c0 + 128, :], (), [a_.b])
                    C.dma("sync", b_.t[:], o_s[1][c0:c0 + 128, :], (), [b_.b])
                    C.dma("sync", r_.t[:], sr_s[c0:c0 + 128, :], (), [r_.b])
                    C.dma("sync", v_.t[:], vgn_s[c0:c0 + 128, :], (), [v_.b])
                    C.dma("sync", u_.t[:], guT_s[:, c0:c0 + 128].rearrange("(c p) t -> p c t", p=128), (), [u_.b])
                    C.tt(a_.t[:], a_.t[:], b_.t[:], ALU.add, [a_.b, b_.b], [a_.b])
                    s_ = ssm.next()
                    for h in range(4):
                        C.act(sqj.t[:], a_.t[:, h * 512:(h + 1) * 512], AF.Square, [a_.b], [sqj.b, s_.b], accum=s_.t[:, h:h + 1])
                    C.act(s_.t[:], s_.t[:], AF.Sqrt, [s_.b, epsc.b], [s_.b], bias=epsc.t[:, 0:1], scale=1.0 / 512)
                    C.recip(s_.t[:], s_.t[:], [s_.b], [s_.b])
                    m_ = mtm.next()
                    for h in range(4):
                        sl = slice(h * 512, (h + 1) * 512)
                        C.stt(a_.t[:, sl], a_.t[:, sl], s_.t[:, h:h + 1], gg.t[:], ALU.mult, ALU.mult, [a_.b, s_.b, gg.b], [a_.b])
                    C.tt(m_.t[:], a_.t[:], r_.t[:], ALU.mult, [a_.b, r_.b], [m_.b])
                    ms = mst.next()
                    for q4 in range(4):
                        tp = tps.next()
                        for i in range(4):
                            ch = q4 * 4 + i
                            C.tr(tp.t[:, i, :], m_.t[:, ch * 128:(ch + 1) * 128], ident_bf.t[:], [m_.b, ident_bf.b], [tp.b])
                        C.cp(ms.t[:, q4 * 4:(q4 + 1) * 4, :], tp.t[:], [tp.b], [ms.b], eng="scalar")
                    for c in range(16):
                        g = c // 2
                        sp_ = sps.next()
                        C.mm(sp_.t[:], v_.t[:, c * 128:(c + 1) * 128], wsb.t[:, g, :], True, True, [v_.b, wsb.b], [sp_.b])
                        tt_ = t32.next()
                        C.stt(tt_.t[:], sp_.t[:], sgn.t[:, c:c + 1], bs.t[:, g, :], ALU.mult, ALU.add, [sp_.b, sgn.b, bs.b], [tt_.b])
                        C.tt(ms.t[:, 16 + c, :], tt_.t[:], u_.t[:, c, :], ALU.mult, [tt_.b, u_.b], [ms.b])
                    C.dma("gpsimd", mixT_s[:, c0:c0 + 128].rearrange("(c p) t -> p c t", p=128), ms.t[:], [ms.b], ())
                S.flush()

        def phase_outproj(w_out, l, tiles_all, src_x, dst_x):
            with ExitStack() as es:
                mT = Ring([C.sb(es, [128, KC, 512], BF16, "mT") for _ in range(2)])
                wp = Ring([C.sb(es, [128, KC, 512], BF16, "wp") for _ in range(3)])
                pp = Ring([C.ps(es, [128, 512], F32, "pp") for _ in range(4)])
                xr = Ring([C.sb(es, [128, 512], F32, "xr") for _ in range(4)])
                xo = Ring([C.sb(es, [128, 512], F32, "xo") for _ in range(4)])
                for tiles in groups_of(tiles_all):
                    Tn = len(tiles) * 128
                    c0 = tiles[0] * 128
                    sg = segs(tiles)
                    m_ = mT.next()
                    C.dma("sync", m_.t[:, :, :Tn], mixT_s[:, c0:c0 + Tn].rearrange("(c p) t -> p c t", p=128), (), [m_.b])

                    def load(pi):
                        w = wp.next()
                        C.dma("gpsimd", w.t[:], wview(w_out, pi * 512, 512), (), [w.b])
                        return w
                    q = [load(0), load(1)]
                    for pi in range(8):
                        if pi + 2 < 8:
                            q.append(load(pi + 2))
                        w = q.pop(0)
                        for jj in range(4):
                            ch = pi * 4 + jj
                            p = pp.next()
                            for kc in range(KC):
                                C.mm(p.t[:, :Tn], w.t[:, kc, jj * 128:(jj + 1) * 128], m_.t[:, kc, :Tn],
                                     kc == 0, kc == KC - 1, [w.b, m_.b], [p.b])
                            x_ = xr.next()
                            C.dma("sync", x_.t[:, :Tn], src_x[ch * 128:(ch + 1) * 128, c0:c0 + Tn], (), [x_.b])
                            o_ = xo.next()
                            for (a, b, j) in sg:
                                C.stt(o_.t[:, a:b], p.t[:, a:b], mod(l, 2, ch, j), x_.t[:, a:b], ALU.mult, ALU.add,
                                      [p.b, x_.b, modT.b], [o_.b])
                            C.dma("sync", dst_x[ch * 128:(ch + 1) * 128, c0:c0 + Tn], o_.t[:, :Tn], [o_.b], ())
                S.flush()

        def phase_ffn(l, tiles_all, src_x, mid_x, dst_x, dst_col_off=0):
            halves = [(0, FHC // 2), (FHC // 2, FHC)] if FHC >= 2 else [(0, FHC)]
            HC = max(b - a for a, b in halves)
            with ExitStack() as es:
                hT = C.sb(es, [128, KC, 512], BF16, "hT")
                AT = C.sb(es, [128, HC, 512], BF16, "AT")
                xring = Ring([C.sb(es, [128, 512], F32, "xc") for _ in range(4)])
                sqring = Ring([C.sb(es, [128, 512], F32, "sq") for _ in range(2)])
                tmpring = Ring([C.sb(es, [128, 512], F32, "tmp") for _ in range(2)])
                rstd = C.sb(es, [128, 512], F32, "rstd")
                ss_ps = C.ps(es, [128, 512], F32, "ssps")
                slots = Ring([C.sb(es, [128, 16384], BF16, "slot") for _ in range(3)])
                gp = Ring([C.ps(es, [128, 512], F32, "gp") for _ in range(2)])
                up = Ring([C.ps(es, [128, 512], F32, "up") for _ in range(2)])
                dp = Ring([C.ps(es, [128, 512], F32, "dp") for _ in range(2)])
                sgr = Ring([C.sb(es, [128, 512], F32, "sgr") for _ in range(2)])
                xo = Ring([C.sb(es, [128, 512], F32, "xo") for _ in range(3)])
                for tiles in groups_of(tiles_all):
                    Tn = len(tiles) * 128
                    c0 = tiles[0] * 128
                    sg = segs(tiles)
                    norm_modulate(es, src_x, tiles, l, 3, 4, hT, xring, sqring, ss_ps, rstd, tmpring)
                    for hi, (ha, hb) in enumerate(halves):
                        nh = hb - ha
                        pans = [(c, min(2, hb - c)) for c in range(ha, hb, 2)]

                        def load_gu(pn):
                            cc, n = pn
                            sl = slots.next()
                            gv = sl.t[:, 0:KC * 256].rearrange("p (k n) -> p k n", n=256)
                            uv = sl.t[:, KC * 256:2 * KC * 256].rearrange("p (k n) -> p k n", n=256)
                            C.dma("gpsimd", gv[:, :, :n * 128], wview(ffn_g[l], cc * 128, n * 128), (), [sl.b])
                            C.dma("gpsimd", uv[:, :, :n * 128], wview(ffn_u[l], cc * 128, n * 128), (), [sl.b])
                            return (sl, gv, uv)
                        q = [load_gu(pans[0])] + ([load_gu(pans[1])] if len(pans) > 1 else [])
                        for ii, (cc, n) in enumerate(pans):
                            if ii + 2 < len(pans):
                                q.append(load_gu(pans[ii + 2]))
                            sl, gv, uv = q.pop(0)
                            for jj in range(n):
                                g_ = gp.next()
                                u_ = up.next()
                                for kc in range(KC):
                                    C.mm(g_.t[:, :Tn], gv[:, kc, jj * 128:(jj + 1) * 128], hT.t[:, kc, :Tn],
                                         kc == 0, kc == KC - 1, [sl.b, hT.b], [g_.b])
                                for kc in range(KC):
                                    C.mm(u_.t[:, :Tn], uv[:, kc, jj * 128:(jj + 1) * 128], hT.t[:, kc, :Tn],
                                         kc == 0, kc == KC - 1, [sl.b, hT.b], [u_.b])
                                s_ = sgr.next()
                                C.act(s_.t[:, :Tn], g_.t[:, :Tn], AF.Silu, [g_.b], [s_.b])
                                C.tt(AT.t[:, cc - ha + jj, :Tn], s_.t[:, :Tn], u_.t[:, :Tn], ALU.mult, [s_.b, u_.b], [AT.b])
                        srcx = src_x if hi == 0 else mid_x
                        last = hi == len(halves) - 1
                        dstx = dst_x if last else mid_x

                        def load_d(j):
                            sl = slots.next()
                            dv = sl.t[:, 0:nh * 128].rearrange("p (k n) -> p k n", n=128)
                            C.dma("gpsimd", dv, ffn_d[l][ha * 128:hb * 128, j * 128:(j + 1) * 128]
                                  .rearrange("(kc p) n -> p kc n", p=128), (), [sl.b])
                            return (sl, dv)
                        q = [load_d(0), load_d(1)]
                        for j in range(KC):
                            if j + 2 < KC:
                                q.append(load_d(j + 2))
                            sl, dv = q.pop(0)
                            p = dp.next()
                            for kc in range(nh):
                                C.mm(p.t[:, :Tn], dv[:, kc, :], AT.t[:, kc, :Tn], kc == 0, kc == nh - 1, [sl.b, AT.b], [p.b])
                            x_ = xring.next()
                            C.dma("sync", x_.t[:, :Tn], srcx[j * 128:(j + 1) * 128, c0:c0 + Tn], (), [x_.b])
                            o_ = xo.next()
                            for (a, b, jx) in sg:
                                C.stt(o_.t[:, a:b], p.t[:, a:b], mod(l, 5, j, jx), x_.t[:, a:b], ALU.mult, ALU.add,
                                      [p.b, x_.b, modT.b], [o_.b])
                            if last:
                                C.dma("sync", dstx[j * 128:(j + 1) * 128, c0 + dst_col_off:c0 + dst_col_off + Tn],
                                      o_.t[:, :Tn], [o_.b], ())
                            else:
                                C.dma("sync", dstx[j * 128:(j + 1) * 128, c0:c0 + Tn], o_.t[:, :Tn], [o_.b], ())
                S.flush()

        def phase_l1a():
            with ExitStack() as es:
                hT = C.sb(es, [128, KC, 512], BF16, "hT")
                xring = Ring([C.sb(es, [128, 512], F32, "xc") for _ in range(4)])
                sqring = Ring([C.sb(es, [128, 512], F32, "sq") for _ in range(2)])
                tmpring = Ring([C.sb(es, [128, 512], F32, "tmp") for _ in range(2)])
                rstd = C.sb(es, [128, 512], F32, "rstd")
                ss_ps = C.ps(es, [128, 512], F32, "ssps")
                wp = Ring([C.sb(es, [128, KC, 512], BF16, "wp") for _ in range(3)])
                pp = Ring([C.ps(es, [128, 512], F32, "pp") for _ in range(4)])
                tps = Ring([C.ps(es, [128, 4, 128], BF16, "tps") for _ in range(2)])
                gq = C.sb(es, [128, 128], F32, "gq")
                gk = C.sb(es, [128, 128], F32, "gk")
                C.dma("sync", gq.t[:], qg_bc, (), [gq.b])
                C.dma("sync", gk.t[:], kg_bc, (), [gk.b])
                grep_ = {}
                for nm, g_ in (("q", gq), ("k", gk)):
                    gf = C.sb(es, [128, 512], F32, "gfr")
                    for h in range(4):
                        C.cp(gf.t[:, h * 128:(h + 1) * 128], g_.t[:], [g_.b], [gf.b])
                    grep_[nm] = (None, None, gf)
                rope = [C.sb(es, [128, 2, 256], F32, "rope") for _ in range(4)]
                sq32 = Ring([C.sb(es, [128, 512], F32, "sq32") for _ in range(2)])
                qn = Ring([C.sb(es, [128, 512], F32, "qn") for _ in range(3)])
                r1 = Ring([C.sb(es, [128, 256], F32, "r1") for _ in range(4)])
                qr = Ring([C.sb(es, [128, 512], BF16, "qr") for _ in range(3)])
                sm = Ring([C.sb(es, [128, 4], F32, "sm") for _ in range(4)])
                st16 = Ring([C.sb(es, [128, 4, 128], BF16, "st16") for _ in range(3)])
                sv16 = Ring([C.sb(es, [128, 512], BF16, "sv16") for _ in range(3)])
                for tiles in groups_of(list(range(NF0))):
                    Tn = len(tiles) * 128
                    norm_modulate(es, x2T_s, tiles, 1, 0, 1, hT, xring, sqring, ss_ps, rstd, tmpring)
                    for i, t in enumerate(tiles):
                        if t < NTF:
                            C.dma("sync", rope[i].t[:], rope_cs[t * 128:(t + 1) * 128], (), [rope[i].b])

                    def load(pi):
                        w = wp.next()
                        C.dma("gpsimd", w.t[:], wview(w_in1, pi * 512, 512), (), [w.b])
                        return w
                    q = [load(0), load(1)]
                    for pi in range(12):
                        if pi + 2 < 12:
                            q.append(load(pi + 2))
                        w = q.pop(0)
                        for i, t in enumerate(tiles):
                            if pi < 8 and t >= NT_OWN:
                                continue
                            p = pp.next()
                            for kc in range(KC):
                                C.mm(p.t[:], hT.t[:, kc, i * 128:(i + 1) * 128], w.t[:, kc, :], kc == 0, kc == KC - 1,
                                     [w.b, hT.b], [p.b])
                            r0 = t * 128
                            if pi >= 10:
                                sv = sv16.next()
                                C.cp(sv.t[:], p.t[:], [p.b], [sv.b], eng="scalar")
                                C.dma("sync", v1_s[r0:r0 + 128, (pi - 10) * 512:(pi - 9) * 512], sv.t[:], [sv.b], ())
                                continue
                            kk = 0 if pi < 8 else 1
                            s2 = sq32.next()
                            C.act(s2.t[:], p.t[:], AF.Square, [p.b], [s2.b])
                            ssm = sm.next()
                            C.red(ssm.t[:], s2.t[:].rearrange("p (h d) -> p h d", h=4), ALU.add, [s2.b], [ssm.b])
                            C.act(ssm.t[:], ssm.t[:], AF.Sqrt, [ssm.b, epsc.b], [ssm.b], bias=epsc.t[:, 0:1], scale=1.0 / 128)
                            C.recip(ssm.t[:], ssm.t[:], [ssm.b], [ssm.b])
                            n_ = qn.next()
                            for h in range(4):
                                sl = slice(h * 128, (h + 1) * 128)
                                C.act(n_.t[:, sl], p.t[:, sl], AF.Copy, [p.b, ssm.b], [n_.b], scale=ssm.t[:, h:h + 1])
                            o_ = qr.next()
                            gf = grep_["q" if pi < 8 else "k"][2]
                            if t < NTF:
                                rp = rope[i]
                                C.tt(n_.t[:], n_.t[:], gf.t[:], ALU.mult, [n_.b, gf.b], [n_.b])
                                nv = n_.t[:].rearrange("p (h d) -> p h d", h=4)
                                ov = o_.t[:].rearrange("p (h d) -> p h d", h=4)
                                cosv = rp.t[:, 0, :].rearrange("p (h d) -> p h d", h=4)
                                sinv = rp.t[:, 1, :].rearrange("p (h d) -> p h d", h=4)
                                v4 = lambda x: x.t[:].rearrange("p (h d) -> p h d", h=4)
                                a1, a2, a3, a4 = r1.next(), r1.next(), r1.next(), r1.next()
                                C.tt(v4(a1), nv[:, :, 0:64], cosv, ALU.mult, [n_.b, rp.b], [a1.b])
                                C.tt(v4(a2), nv[:, :, 64:128], sinv, ALU.mult, [n_.b, rp.b], [a2.b])
                                C.tt(ov[:, :, 0:64], v4(a1), v4(a2), ALU.subtract, [a1.b, a2.b], [o_.b])
                                C.tt(v4(a3), nv[:, :, 0:64], sinv, ALU.mult, [n_.b, rp.b], [a3.b])
                                C.tt(v4(a4), nv[:, :, 64:128], cosv, ALU.mult, [n_.b, rp.b], [a4.b])
                                C.tt(ov[:, :, 64:128], v4(a3), v4(a4), ALU.add, [a3.b, a4.b], [o_.b])
                            else:
                                C.tt(o_.t[:], n_.t[:], gf.t[:], ALU.mult, [n_.b, gf.b], [o_.b])
                            tp = tps.next()
                            for h in range(4):
                                C.tr(tp.t[:, h, :], o_.t[:, h * 128:(h + 1) * 128], ident_bf.t[:], [o_.b, ident_bf.b], [tp.b])
                            so = st16.next()
                            C.cp(so.t[:], tp.t[:], [tp.b], [so.b], eng="scalar")
                            if pi < 8:
                                dst = qT1_s[pi * 512:(pi + 1) * 512, r0:r0 + 128]
                            else:
                                dst = kT1_s[(pi - 8) * 512:(pi - 7) * 512, r0:r0 + 128]
                            C.dma("sync", dst.rearrange("(h d) t -> d h t", d=128), so.t[:], [so.b], ())
                S.flush()

        def phase_l1b():
            with ExitStack() as es:
                kT = C.sb(es, [128, NF0, 8, 128], BF16, "kTall")
                vA = C.sb(es, [128, NF0, 1024], BF16, "vall")
                kb_ = [Buf() for _ in range(NF0)]
                vb_ = [Buf() for _ in range(NF0)]
                for t in range(NF0):
                    C.dma("sync", kT.t[:, t], kT1_s[:, t * 128:(t + 1) * 128].rearrange("(h d) t -> d h t", d=128), (), [kb_[t]])
                    C.dma("sync", vA.t[:, t], v1_s[t * 128:(t + 1) * 128, :], (), [vb_[t]])
                sk = C.sb(es, [128, 32], F32, "sk")
                esk = C.sb(es, [128, 32], F32, "esk")
                negc = C.sb(es, [128, 1], F32, "negc")
                C.memset(negc.t[:], -SOFT_C, [negc.b])
                C.dma("sync", sk.t[:], sink_bc, (), [sk.b])
                C.act(esk.t[:], sk.t[:], AF.Exp, [sk.b, negc.b], [esk.b], bias=negc.t[:, 0:1])
                mk = [C.sb(es, [128, 4, 128], BF16, "mk") for _ in range(2)]
                for h in range(4):
                    C.cp(mk[0].t[:, h, :], cst.t[:, 2, :], [cst.b], [mk[0].b])
                    C.cp(mk[1].t[:, h, :], cst.t[:, 1, :], [cst.b], [mk[1].b])
                qT = Ring([C.sb(es, [128, 32, 128], BF16, "qT") for _ in range(2)])
                sps = Ring([C.ps(es, [128, 512], F32, "sps") for _ in range(4)])
                dps = Ring([C.ps(es, [128, 512], F32, "dps") for _ in range(2)])
                ops_ = Ring([C.ps(es, [128, 512], F32, "ops") for _ in range(2)])
                PT = Ring([C.sb(es, [128, 512], BF16, "PT") for _ in range(4)])
                den = Ring([C.sb(es, [128, 512], F32, "den") for _ in range(2)])
                ast = Ring([C.sb(es, [128, 32, 128], BF16, "ast") for _ in range(2)])
                sc = 128.0 ** -0.5
                LOOK = 3
                for n in range(NT_OWN):
                    q_ = qT.next()
                    C.dma("sync", q_.t[:], qT1_s[:, n * 128:(n + 1) * 128].rearrange("(h d) t -> d h t", d=128), (), [q_.b])
                    blocks = []
                    if n > 0:
                        blocks.append((n - 1, 0))
                    blocks.append((n, None))
                    blocks.append((n + 1, 1))
                    for t in range(CTX0, CTX0 + NT_CTX):
                        blocks.append((t, None))
                    nb = len(blocks)
                    a_ = ast.next()
                    pairs = [(g, bi) for g in range(8) for bi in range(nb)]

                    def score(idx):
                        g, bi = pairs[idx]
                        kt = blocks[bi][0]
                        s_ = sps.next()
                        C.mm(s_.t[:], kT.t[:, kt, g, :], q_.t[:, 4 * g:4 * g + 4, :], True, True, [kb_[kt], q_.b], [s_.b])
                        return s_
                    sq_ = [score(i) for i in range(min(LOOK, len(pairs)))]
                    d_ = o_ = None
                    for idx, (g, bi) in enumerate(pairs):
                        if idx + LOOK < len(pairs):
                            sq_.append(score(idx + LOOK))
                        s_ = sq_.pop(0)
                        kt, mi = blocks[bi]
                        if bi == 0:
                            d_ = dps.next()
                            o_ = ops_.next()
                        p_ = PT.next()
                        C.act(p_.t[:], s_.t[:], AF.Exp, [s_.b, negc.b], [p_.b], bias=negc.t[:, 0:1], scale=sc)
                        if mi is not None:
                            C.tt(p_.t[:], p_.t[:], mk[mi].t[:].rearrange("p h t -> p (h t)"), ALU.mult,
                                 [p_.b, mk[mi].b], [p_.b])
                        C.mm(d_.t[:], ones_bf.t[:], p_.t[:], bi == 0, bi == nb - 1, [p_.b, ones_bf.b], [d_.b])
                        C.mm(o_.t[:], vA.t[:, kt, g * 128:(g + 1) * 128], p_.t[:], bi == 0, bi == nb - 1,
                             [p_.b, vb_[kt]], [o_.b])
                        if bi == nb - 1:
                            dn = den.next()
                            for h in range(4):
                                sl = slice(h * 128, (h + 1) * 128)
                                C.ts(dn.t[:, sl], d_.t[:, sl], esk.t[:, 4 * g + h:4 * g + h + 1], None, ALU.add, None,
                                     [d_.b, esk.b], [dn.b])
                            C.act(dn.t[:], dn.t[:], AF.Ln, [dn.b], [dn.b])
                            C.act(dn.t[:], dn.t[:], AF.Exp, [dn.b], [dn.b], scale=-1.0)
                            C.tt(a_.t[:, 4 * g:4 * g + 4, :].rearrange("p h t -> p (h t)"), o_.t[:], dn.t[:], ALU.mult,
                                 [o_.b, dn.b], [a_.b])
                    C.dma("gpsimd", mixT_s[:, n * 128:(n + 1) * 128].rearrange("(h d) t -> d h t", d=128), a_.t[:], [a_.b], ())
                S.flush()

        ph = cfg.get("PHASES")
        run = lambda name: (ph is None) or (name in ph)
        own = list(range(NT_OWN))
        full0 = list(range(NF0))
        if run("l0a"):
            phase_l0a()
        if run("l0b"):
            phase_l0b()
        if run("l0c"):
            phase_l0c()
        if run("l0d"):
            phase_l0d()
        if run("l0e"):
            phase_outproj(w_out0[:], 0, full0, xT_in, x1T_s)
        if run("l0f"):
            phase_ffn(0, full0, x1T_s, xhT_s, x2T_s)
        if run("l1a"):
            phase_l1a()
        if run("l1b"):
            phase_l1b()
        if run("l1c"):
            phase_outproj(w_out1[:], 1, own, x2T_s, x1T_s)
        if run("l1d"):
            phase_ffn(1, own, x1T_s, xhT_s, yT)
        S.final_wait()
    return nc


def make_consts():
    i = np.arange(128)[:, None]
    t = np.arange(128)[None, :]
    c = np.zeros((128, 8, 128), np.float32)
    c[:, 0] = (i == t)
    c[:, 1] = (i <= t)
    c[:, 2] = (i >= t)
    c[:, 3] = (i > t)
    c[:, 4] = (i < t)
    return c


def rope_tables(seq, grid_w=64, head_dim=128, theta=10000.0):
    rows = seq // grid_w
    row = np.repeat(np.arange(rows, dtype=np.float32), grid_w)
    col = np.tile(np.arange(grid_w, dtype=np.float32), rows)
    n_freq = head_dim // 4
    inv_freq = (np.float32(theta) ** (-np.arange(n_freq, dtype=np.float32) / n_freq)).astype(np.float32)
    ang = np.concatenate([row[:, None] * inv_freq, col[:, None] * inv_freq], axis=-1).astype(np.float32)
    return np.cos(ang).astype(np.float32), np.sin(ang).astype(np.float32)


def host_inputs(cfg, inp, core):
    NT_OWN, NT_HALO, NT_REST, NT_CTX, FH = cfg["NT_OWN"], cfg["NT_HALO"], cfg["NT_REST"], cfg["NT_CTX"], cfg["FH"]
    NTF = NT_OWN + NT_HALO
    b, flip = core // 2, core % 2
    f32 = lambda a: np.ascontiguousarray(a, dtype=np.float32)
    x = np.asarray(inp["x"][b])
    ctx = np.asarray(inp["ctx"][b])
    if flip:
        x = x[::-1]
        ctx = ctx[::-1]
    tok = np.concatenate([x[:NTF * 128], ctx, x[NTF * 128:]], axis=0)
    m = {}
    m["xT"] = f32(tok.T)
    cc = np.stack([np.asarray(inp["c"][b]), np.asarray(inp["c_ctx"])], axis=-1)
    m["cT"] = f32(cc.reshape(KC, 128, 2).transpose(1, 0, 2))
    if cfg["ADA"]:
        m["ada_w"] = inp["_ada_w"]
        m["ada_bT"] = inp["_ada_bT"]
    else:
        m["modT_in"] = inp["_modT"][core]
    m["w_in0"] = inp["_w_in0"]
    wi = np.asarray(inp["even_w_in"][0])
    gf, gb = wi[:, 6144:6160], wi[:, 6160:6176]
    w2f, w2b = np.asarray(inp["even_gate_w2_fwd"][0]), np.asarray(inp["even_gate_w2_bwd"][0])
    b2f, b2b = np.asarray(inp["even_gate_b_fwd"][0]), np.asarray(inp["even_gate_b_bwd"][0])
    if flip:
        gf, gb, w2f, w2b, b2f, b2b = gb, gf, w2b, w2f, b2b, b2f
    m["w_g"] = f32(np.concatenate([gf, gb], axis=1))
    w2 = np.zeros((33, 2048), np.float32)
    w2[0:16, 0:1024] = w2f
    w2[16:32, 1024:2048] = w2b
    w2[32, 0:1024] = b2f
    w2[32, 1024:2048] = b2b
    m["w2blk"] = w2
    m["gla_gain_bc"] = f32(np.broadcast_to(np.asarray(inp["even_gla_norm_gain"][0])[None, :], (128, 512)))
    m["sgainT"] = f32(np.asarray(inp["even_sgu_norm_gain"][0]).reshape(16, 128).T)
    ws = np.asarray(inp["even_sgu_w_s"][0])
    bs = np.asarray(inp["even_sgu_b_s"][0])
    if flip:
        ws = ws[:, ::-1, ::-1]
        bs = bs[:, ::-1]
    m["wsT"] = f32(ws.transpose(2, 0, 1))
    m["bsb"] = f32(np.broadcast_to(bs[None], (128, 8, 128)))
    m["w_out0"] = inp["_w_out0"]
    m["w_in1"] = inp["_w_in1"]
    m["qg_bc"] = f32(np.broadcast_to(np.asarray(inp["odd_q_norm_gain"][0])[None, :], (128, 128)))
    m["kg_bc"] = f32(np.broadcast_to(np.asarray(inp["odd_k_norm_gain"][0])[None, :], (128, 128)))
    m["sink_bc"] = f32(np.broadcast_to(np.asarray(inp["odd_sink"][0])[None, :], (128, 32)))
    m["w_out1"] = inp["_w_out1"]
    m["ffn_g"] = inp["_ffn_g"]
    m["ffn_u"] = inp["_ffn_u"]
    m["ffn_d"] = inp["_ffn_d"]
    cos, sin = inp["_rope"]
    if flip:
        cos, sin = cos[::-1], sin[::-1]
    cs = np.stack([np.tile(cos[:NTF * 128], (1, 4)), np.tile(sin[:NTF * 128], (1, 4))], axis=1)
    m["rope_cs"] = f32(cs)
    m["consts"] = inp["_consts"]
    return m


def prep_shared(cfg, inputs):
    inp = dict(inputs)
    f32 = lambda a: np.ascontiguousarray(a, dtype=np.float32)
    wi = np.asarray(inputs["even_w_in"][0])
    inp["_w_in0"] = f32(np.concatenate([wi[:, 0:6144], wi[:, 6176:10272]], axis=1))
    inp["_w_out0"] = f32(inputs["even_w_out"][0])
    inp["_w_in1"] = f32(inputs["odd_w_in"][0])
    inp["_w_out1"] = f32(inputs["odd_w_out"][0])
    inp["_ffn_g"] = f32(inputs["ffn_w_gate"])
    inp["_ffn_u"] = f32(inputs["ffn_w_up"])
    inp["_ffn_d"] = f32(inputs["ffn_w_down"])
    if cfg["ADA"]:
        inp["_ada_w"] = f32(inputs["ada_w"])
        inp["_ada_bT"] = f32(np.asarray(inputs["ada_b"]).reshape(2, 6, KC, 128).transpose(3, 0, 1, 2))
    seq = (cfg["NT_OWN"] + cfg["NT_HALO"] + cfg["NT_REST"]) * 128
    inp["_rope"] = rope_tables(seq)
    inp["_consts"] = make_consts()
    return inp


def kernel(**inputs):
    cfg = default_cfg()
    B = inputs["x"].shape[0]
    n_cores = 2 * B
    inp = prep_shared(cfg, inputs)
    nc = build_program(cfg)
    in_maps = [host_inputs(cfg, inp, c) for c in range(n_cores)]
    res = run_bass_kernel_spmd(nc, in_maps, core_ids=list(range(n_cores)))
    L = inputs["x"].shape[1]
    half = cfg["NT_OWN"] * 128
    out = np.empty((B, L, D), np.float32)
    for c in range(n_cores):
        y = np.asarray(res.results[c]["yT"]).T
        b, flip = c // 2, c % 2
        if flip:
            out[b, L - half:] = y[::-1]
        else:
            out[b, :half] = y
    return out
```

```python
import numpy as np
from contextlib import ExitStack
import concourse.bass as bass
import concourse.mybir as mybir
from concourse.bass_utils import run_bass_kernel_spmd

F32 = mybir.dt.float32
BF16 = mybir.dt.bfloat16
AF = mybir.ActivationFunctionType
ALU = mybir.AluOpType
AX = mybir.AxisListType

D = 4096
KC = 32
EPS = 1e-6
SOFT_C = 4.0
SAME_ENGINE_SYNC = True
ADA_INTERLEAVE = False


def default_cfg():
    return dict(NT_OWN=16, NT_HALO=1, NT_REST=15, NT_CTX=2, FH=11008, DEBUG=False, ADA=True)


class Buf:
    __slots__ = ("name", "last_w", "readers", "dma_readers")

    def __init__(self, name=""):
        self.name = name
        self.last_w = None
        self.readers = {}
        self.dma_readers = []


class Op:
    __slots__ = ("eng", "fn", "deps", "need_inc", "event", "is_dma", "batch")


class Sched:
    ENG = ("tensor", "vector", "scalar", "gpsimd", "sync")
    NPOOL = 12

    def __init__(self, nc):
        self.nc = nc
        self.pending = []
        self.batch = 0
        self.bar = {}
        self.sems = {e: nc.alloc_semaphore("sc_" + e) for e in self.ENG}
        self.cnt = {e: 0 for e in self.ENG}
        self.seen = {e: {} for e in self.ENG}
        self.pool = {q: [nc.alloc_semaphore("sd_%s%d" % (q, i)) for i in range(self.NPOOL)]
                     for q in ("sync", "gpsimd")}
        self.pool_val = {q: [0] * self.NPOOL for q in ("sync", "gpsimd")}
        self.rr = {q: 0 for q in ("sync", "gpsimd")}
        self.last_on_eng = {}
        self.n_ops = 0

    def add(self, eng, fn, reads=(), writes=(), dma=False):
        o = Op()
        o.eng = eng
        o.fn = fn
        o.is_dma = dma
        o.need_inc = False
        o.event = None
        o.batch = self.batch
        deps = set()
        for b in reads:
            if b.last_w is not None:
                deps.add(b.last_w)
        for b in writes:
            if b.last_w is not None:
                deps.add(b.last_w)
            deps.update(b.readers.values())
            deps.update(b.dma_readers)
        for b in writes:
            b.last_w = o
            b.readers = {}
            b.dma_readers = []
        for b in reads:
            if dma:
                b.dma_readers.append(o)
            else:
                b.readers[eng] = o
        dl = []
        for d in deps:
            if d is o or d.batch != self.batch:
                continue
            if d.eng == eng and not d.is_dma and not dma:
                if eng == "tensor" or not SAME_ENGINE_SYNC:
                    continue
            dl.append(d)
        o.deps = dl
        self.pending.append(o)
        return o

    def _wait(self, eng, sem_key, sem, val):
        if self.seen[eng].get(sem_key, 0) < val:
            getattr(self.nc, eng).wait_ge(sem, val)
            self.seen[eng][sem_key] = val

    def flush(self):
        ops = self.pending
        self.pending = []
        for o in ops:
            for d in o.deps:
                d.need_inc = True
        last = {}
        for o in ops:
            if not o.is_dma:
                last[o.eng] = o
        for o in last.values():
            o.need_inc = True
        nc = self.nc
        first_on_eng = set()
        for o in ops:
            e = o.eng
            h = getattr(nc, e)
            if e not in first_on_eng:
                first_on_eng.add(e)
                for (k, s, v) in self.bar.get(e, ()):
                    self._wait(e, k, s, v)
            for d in o.deps:
                k, s, v = d.event
                self._wait(e, k, s, v)
            if o.is_dma:
                i = self.rr[e]
                self.rr[e] = (i + 1) % self.NPOOL
                s = self.pool[e][i]
                v = self.pool_val[e][i]
                key = (e, i)
                if v > 0:
                    self._wait(e, key, s, v)
                ins = o.fn(h)
                ins.then_inc(s, 16)
                self.pool_val[e][i] = v + 16
                o.event = (key, s, v + 16)
            else:
                ins = o.fn(h)
                if o.need_inc:
                    self.cnt[e] += 1
                    ins.then_inc(self.sems[e], 1)
                    o.event = (e, self.sems[e], self.cnt[e])
            o.fn = None
        self.n_ops += len(ops)
        evs = [(e, self.sems[e], self.cnt[e]) for e in self.ENG if self.cnt[e] > 0]
        for q in ("sync", "gpsimd"):
            for i in range(self.NPOOL):
                if self.pool_val[q][i] > 0:
                    evs.append(((q, i), self.pool[q][i], self.pool_val[q][i]))
        self.bar = {e: evs for e in self.ENG}
        self.batch += 1

    def final_wait(self):
        for e in self.ENG:
            for (k, s, v) in self.bar.get(e, ()):
                self._wait(e, k, s, v)


class T:
    __slots__ = ("t", "b")

    def __init__(self, t, b=None):
        self.t = t
        self.b = b if b is not None else Buf()


class Ctx:
    def __init__(self, nc, cfg):
        self.nc = nc
        self.cfg = cfg
        self.S = Sched(nc)
        self.uid = 0

    def name(self, p):
        self.uid += 1
        return "%s_%d" % (p, self.uid)

    def sb(self, es, shape, dt, name="t"):
        return T(es.enter_context(self.nc.sbuf_tensor(self.name(name), list(shape), dt)))

    def ps(self, es, shape, dt=F32, name="p"):
        return T(es.enter_context(self.nc.psum_tensor(self.name(name), list(shape), dt)))

    def dma(self, q, out, in_, reads=(), writes=()):
        return self.S.add(q, lambda h, o=out, i=in_: h.dma_start(out=o, in_=i), reads, writes, dma=True)

    def mm(self, out, lhsT, rhs, start, stop, reads=(), writes=()):
        return self.S.add("tensor", lambda h, o=out, l=lhsT, r=rhs, a=start, b=stop:
                          h.matmul(o, l, r, start=a, stop=b), reads, writes)

    def tr(self, out, in_, ident, reads=(), writes=()):
        return self.S.add("tensor", lambda h, o=out, i=in_, d=ident: h.transpose(o, i, d), reads, writes)

    def act(self, out, in_, func, reads=(), writes=(), bias=None, scale=None, accum=None, eng="scalar"):
        def fn(h, o=out, i=in_, f=func, b=bias, s=scale, a=accum):
            kw = {}
            if b is not None:
                kw["bias"] = b
            if s is not None:
                kw["scale"] = s
            if a is not None:
                kw["accum_out"] = a
            return h.activation(o, i, f, **kw)
        return self.S.add(eng, fn, reads, writes)

    def tt(self, out, in0, in1, op, reads=(), writes=(), eng="vector"):
        return self.S.add(eng, lambda h, o=out, a=in0, b=in1, p=op: h.tensor_tensor(o, a, b, p), reads, writes)

    def ts(self, out, in0, s1, s2, op0, op1=None, reads=(), writes=(), eng="vector"):
        def fn(h, o=out, a=in0, x=s1, y=s2, p0=op0, p1=op1):
            if p1 is None:
                return h.tensor_scalar(o, a, x, None, p0)
            return h.tensor_scalar(o, a, x, y, p0, p1)
        return self.S.add(eng, fn, reads, writes)

    def stt(self, out, in0, scalar, in1, op0, op1, reads=(), writes=()):
        return self.S.add("vector", lambda h, o=out, a=in0, s=scalar, b=in1, p0=op0, p1=op1:
                          h.scalar_tensor_tensor(o, a, s, b, p0, p1), reads, writes)

    def cp(self, out, in_, reads=(), writes=(), eng="vector"):
        if eng == "scalar":
            return self.S.add(eng, lambda h, o=out, i=in_: h.copy(o, i), reads, writes)
        return self.S.add(eng, lambda h, o=out, i=in_: h.tensor_copy(o, i), reads, writes)

    def recip(self, out, in_, reads=(), writes=()):
        return self.S.add("vector", lambda h, o=out, i=in_: h.reciprocal(o, i), reads, writes)

    def recipf(self, out, in_, reads=(), writes=()):
        return self.S.add("vector", lambda h, o=out, i=in_: h.reciprocal_approx_fast(o, i), reads, writes)

    def memset(self, ap, val, writes=(), eng="vector"):
        return self.S.add(eng, lambda h, a=ap, v=val: h.memset(a, v), (), writes)

    def red(self, out, in_, op, reads=(), writes=()):
        return self.S.add("vector", lambda h, o=out, i=in_, p=op: h.tensor_reduce(o, i, AX.X, p), reads, writes)


class Ring:
    def __init__(self, items):
        self.items = items
        self.i = 0

    def next(self):
        x = self.items[self.i]
        self.i = (self.i + 1) % len(self.items)
        return x


def groups_of(tiles, g=4):
    return [tiles[i:i + g] for i in range(0, len(tiles), g)]


def build_program(cfg):
    NT_OWN, NT_HALO, NT_REST, NT_CTX, FH = cfg["NT_OWN"], cfg["NT_HALO"], cfg["NT_REST"], cfg["NT_CTX"], cfg["FH"]
    DEBUG = cfg["DEBUG"]
    NTF = NT_OWN + NT_HALO
    NF0 = NTF + NT_CTX
    NTT = NF0 + NT_REST
    NTOK = NTT * 128
    CTX0 = NTF
    REST0 = NF0
    FHC = FH // 128
    assert FH % 256 == 0 or FH % 128 == 0
    nc = bass.Bass("TRN2", target_bir_lowering=False)
    C = Ctx(nc, cfg)
    S = C.S

    def din(name, shape, dt=F32):
        return nc.dram_tensor(name, list(shape), dt, kind="ExternalInput").ap()

    skind = "ExternalOutput" if DEBUG else "Internal"

    def dscr(name, shape, dt):
        return nc.dram_tensor(name, list(shape), dt, kind=skind).ap()

    xT_in = din("xT", [D, NTOK])
    cT_in = din("cT", [128, KC, 2])
    if cfg["ADA"]:
        ada_w = din("ada_w", [2, D, 6 * D])
        ada_bT = din("ada_bT", [128, 2, 6, KC])
    else:
        modT_in = din("modT_in", [128, 2 * 6 * KC * 2])
    w_in0 = din("w_in0", [D, 10240])
    w_g = din("w_g", [D, 32])
    w2blk = din("w2blk", [33, 2048])
    gla_gain_bc = din("gla_gain_bc", [128, 512])
    sgainT = din("sgainT", [128, 16])
    wsT = din("wsT", [128, 8, 128])
    bsb = din("bsb", [128, 8, 128])
    w_out0 = din("w_out0", [D, D])
    w_in1 = din("w_in1", [D, 6144])
    qg_bc = din("qg_bc", [128, 128])
    kg_bc = din("kg_bc", [128, 128])
    sink_bc = din("sink_bc", [128, 32])
    w_out1 = din("w_out1", [D, D])
    ffn_g = din("ffn_g", [2, D, FH])
    ffn_u = din("ffn_u", [2, D, FH])
    ffn_d = din("ffn_d", [2, FH, D])
    rope_cs = din("rope_cs", [NTF * 128, 2, 256])
    consts = din("consts", [128, 8, 128])
    yT = nc.dram_tensor("yT", [D, NT_OWN * 128], F32, kind="ExternalOutput").ap()

    qT_s = dscr("qT_s", [1024, NTOK], F32)
    kT_s = dscr("kT_s", [1024, NTOK], F32)
    ktm_s = dscr("ktm_s", [NTOK, 1024], F32)
    v_s = dscr("v_s", [NTOK, 2048], BF16)
    sr_s = dscr("sr_s", [NTOK, 2048], F32)
    gT_s = dscr("gT_s", [32, NTOK], F32)
    guT_s = dscr("guT_s", [2048, NTOK], F32)
    vgn_s = dscr("vgn_s", [NTOK, 2048], BF16)
    qeT_s = [dscr("qeT_s%d" % d, [1024, NTOK], BF16) for d in range(2)]
    keT_s = [dscr("keT_s%d" % d, [1024, NTOK], BF16) for d in range(2)]
    kd_s = [dscr("kd_s%d" % d, [NTOK, 1024], BF16) for d in range(2)]
    ebl_s = [dscr("ebl_s%d" % d, [NTT, 128, 8], F32) for d in range(2)]
    o_s = [dscr("o_s%d" % d, [NTOK, 2048], F32) for d in range(2)]
    mixT_s = dscr("mixT_s", [D, NTOK], BF16)
    x1T_s = dscr("x1T_s", [D, NTOK], F32)
    xhT_s = dscr("xhT_s", [D, NTOK], F32)
    x2T_s = dscr("x2T_s", [D, NTOK], F32)
    qT1_s = dscr("qT1_s", [D, NTOK], BF16)
    kT1_s = dscr("kT1_s", [1024, NTOK], BF16)
    v1_s = dscr("v1_s", [NTOK, 1024], BF16)
    x3T_s = dscr("x3T_s", [D, NTOK], F32)

    with ExitStack() as glob:
        modT = C.sb(glob, [128, 2 * 6 * KC * 2], F32, "modT")
        cst = C.sb(glob, [128, 8, 128], F32, "cst")
        ident_bf = C.sb(glob, [128, 128], BF16, "identb")
        ones_f = C.sb(glob, [128, 128], F32, "onesf")
        ones_bf = C.sb(glob, [128, 128], BF16, "onesb")
        epsc = C.sb(glob, [128, 1], F32, "epsc")

        def mod(l, m, c, j):
            off = ((l * 6 + m) * KC + c) * 2 + j
            return modT.t[:, off:off + 1]

        C.dma("sync", cst.t[:], consts, (), [cst.b])
        C.cp(ident_bf.t[:], cst.t[:, 0, :], [cst.b], [ident_bf.b])
        C.memset(ones_f.t[:], 1.0, [ones_f.b])
        C.memset(ones_bf.t[:], 1.0, [ones_bf.b])
        C.memset(epsc.t[:], EPS, [epsc.b])
        S.flush()

        class AdaStream:
            def __init__(self, es, panels):
                self.cf = C.sb(es, [128, KC, 2], F32, "cf")
                self.sT = C.sb(es, [128, KC, 2], BF16, "sT")
                self.abT = C.sb(es, [128, 2, 6, KC], F32, "abT")
                self.wp = Ring([C.sb(es, [128, KC, 256], BF16, "adw") for _ in range(3)])
                pt = C.ps(es, [128, 8], F32, "adp")
                self.pp = Ring([(pt.t, 0, Buf()), (pt.t, 4, Buf())])
                C.dma("sync", self.cf.t[:], cT_in, (), [self.cf.b])
                C.dma("sync", self.abT.t[:], ada_bT, (), [self.abT.b])
                C.act(self.sT.t[:], self.cf.t[:], AF.Silu, [self.cf.b], [self.sT.b])
                self.panels = panels
                self.nl = 0
                self.ncmp = 0
                self.q = []

            def _load(self):
                l, j = self.panels[self.nl]
                self.nl += 1
                w = self.wp.next()
                C.dma("gpsimd", w.t[:], ada_w[l].rearrange("(kc p) n -> p kc n", p=128)[:, :, j * 256:(j + 1) * 256],
                      (), [w.b])
                self.q.append(w)

            def step(self, n=1):
                for _ in range(n):
                    if self.ncmp >= len(self.panels):
                        return
                    while self.nl < min(len(self.panels), self.ncmp + 3):
                        self._load()
                    l, j = self.panels[self.ncmp]
                    self.ncmp += 1
                    w = self.q.pop(0)
                    pt_, po, pb = self.pp.next()
                    m, c0 = j // 16, (j % 16) * 2
                    for i in range(2):
                        for kc in range(KC):
                            C.mm(pt_[:, po + 2 * i:po + 2 * i + 2], w.t[:, kc, i * 128:(i + 1) * 128], self.sT.t[:, kc, :],
                                 kc == 0, kc == KC - 1, [w.b, self.sT.b], [pb])
                    off = ((l * 6 + m) * KC + c0) * 2
                    for jj in range(2):
                        C.tt(modT.t[:, off + jj:off + 4:2], pt_[:, po + jj:po + 4:2], self.abT.t[:, l, m, c0:c0 + 2], ALU.add,
                             [pb, self.abT.b], [modT.b])

            def finish(self, plus_one):
                self.step(len(self.panels))
                for (l, m) in plus_one:
                    off = ((l * 6 + m) * KC) * 2
                    C.ts(modT.t[:, off:off + 2 * KC], modT.t[:, off:off + 2 * KC], 1.0, None, ALU.add,
                         None, [modT.b], [modT.b])

        ada_late = None
        ada_es = ExitStack()
        if not cfg["ADA"]:
            C.dma("sync", modT.t[:], modT_in, (), [modT.b])
            S.flush()
        else:
            with ExitStack() as es:
                if ADA_INTERLEAVE:
                    first = AdaStream(es, [(0, j) for j in range(32)])
                    first.finish([(0, 1)])
                else:
                    first = AdaStream(es, [(l, j) for l in range(2) for j in range(96)])
                    first.finish([(0, 1), (0, 4), (1, 1), (1, 4)])
                S.flush()

        def tile_j(t):
            return 1 if CTX0 <= t < CTX0 + NT_CTX else 0

        def segs(tiles):
            out = []
            for i, t in enumerate(tiles):
                j = tile_j(t)
                if out and out[-1][2] == j:
                    out[-1][1] = (i + 1) * 128
                else:
                    out.append([i * 128, (i + 1) * 128, j])
            return out

        def norm_modulate(es, src, tiles, l, m_shift, m_scale, hT, xring, sqring, ss_ps, rstd, tmpring):
            Tn = len(tiles) * 128
            c0 = tiles[0] * 128
            srcv = src[:, c0:c0 + Tn].rearrange("(c p) t -> c p t", p=128)
            for c in range(KC):
                xc = xring.next()
                C.dma("sync", xc.t[:, :Tn], srcv[c], (), [xc.b])
                sq = sqring.next()
                C.act(sq.t[:, :Tn], xc.t[:, :Tn], AF.Square, [xc.b], [sq.b])
                C.mm(ss_ps.t[:, :Tn], ones_f.t[:], sq.t[:, :Tn], c == 0, c == KC - 1, [sq.b, ones_f.b], [ss_ps.b])
            C.act(rstd.t[:, :Tn], ss_ps.t[:, :Tn], AF.Sqrt, [ss_ps.b, epsc.b], [rstd.b], bias=epsc.t[:, 0:1], scale=1.0 / D)
            C.recip(rstd.t[:, :Tn], rstd.t[:, :Tn], [rstd.b], [rstd.b])
            sg = segs(tiles)
            for c in range(KC):
                xc = xring.next()
                C.dma("sync", xc.t[:, :Tn], srcv[c], (), [xc.b])
                tmp = tmpring.next()
                for (a, b, j) in sg:
                    C.stt(tmp.t[:, a:b], xc.t[:, a:b], mod(l, m_scale, c, j), rstd.t[:, a:b], ALU.mult, ALU.mult,
                          [xc.b, rstd.b, modT.b], [tmp.b])
                    C.act(hT.t[:, c, a:b], tmp.t[:, a:b], AF.Identity, [tmp.b, modT.b], [hT.b],
                          bias=mod(l, m_shift, c, j))

        def wview(w, c0, ncols):
            return w.rearrange("(kc p) n -> p kc n", p=128)[:, :, c0:c0 + ncols]

        def phase_l0a():
            with ExitStack() as es:
                hT = C.sb(es, [128, KC, 512], BF16, "hT")
                xring = Ring([C.sb(es, [128, 512], F32, "xc") for _ in range(4)])
                sqring = Ring([C.sb(es, [128, 512], F32, "sq") for _ in range(2)])
                tmpring = Ring([C.sb(es, [128, 512], F32, "tmp") for _ in range(2)])
                rstd = C.sb(es, [128, 512], F32, "rstd")
                ss_ps = C.ps(es, [128, 512], F32, "ssps")
                wp = Ring([C.sb(es, [128, KC, 512], BF16, "wp") for _ in range(3)])
                wgt = C.sb(es, [128, KC, 32], BF16, "wgt")
                pp = Ring([C.ps(es, [128, 512], F32, "pp") for _ in range(6)])
                st32 = Ring([C.sb(es, [128, 512], F32, "st32") for _ in range(4)])
                st16 = Ring([C.sb(es, [128, 512], BF16, "st16") for _ in range(3)])
                sm = Ring([C.sb(es, [128, 4], F32, "sm") for _ in range(4)])
                C.dma("gpsimd", wgt.t[:], w_g.rearrange("(kc p) n -> p kc n", p=128), (), [wgt.b])

                full_groups = groups_of(list(range(NF0)))
                rest_groups = groups_of(list(range(REST0, NTT)))
                for tiles, full in [(g, True) for g in full_groups] + [(g, False) for g in rest_groups]:
                    Tn = len(tiles) * 128
                    c0 = tiles[0] * 128
                    norm_modulate(es, xT_in, tiles, 0, 0, 1, hT, xring, sqring, ss_ps, rstd, tmpring)
                    plist = list(range(20)) if full else list(range(2, 8))
                    p = pp.next()
                    for kc in range(KC):
                        C.mm(p.t[0:32, :Tn], wgt.t[:, kc, :], hT.t[:, kc, :Tn], kc == 0, kc == KC - 1, [wgt.b, hT.b], [p.b])
                    st = st32.next()
                    C.cp(st.t[0:32, :Tn], p.t[0:32, :Tn], [p.b], [st.b], eng="scalar")
                    C.dma("sync", gT_s[:, c0:c0 + Tn], st.t[0:32, :Tn], [st.b], ())

                    def load(pi):
                        w = wp.next()
                        C.dma("gpsimd", w.t[:], wview(w_in0, pi * 512, 512), (), [w.b])
                        return w
                    q = [load(plist[0]), load(plist[1])]
                    for ii, pi in enumerate(plist):
                        if ii + 2 < len(plist):
                            q.append(load(plist[ii + 2]))
                        w = q.pop(0)
                        fm = (pi < 4 and full) or (12 <= pi < 16)
                        tm = (2 <= pi < 12) or pi >= 16
                        if fm:
                            for jj in range(4):
                                p = pp.next()
                                for kc in range(KC):
                                    C.mm(p.t[:, :Tn], w.t[:, kc, jj * 128:(jj + 1) * 128], hT.t[:, kc, :Tn],
                                         kc == 0, kc == KC - 1, [w.b, hT.b], [p.b])
                                st = st32.next()
                                ch = pi * 4 + jj
                                if pi < 2:
                                    C.cp(st.t[:, :Tn], p.t[:, :Tn], [p.b], [st.b], eng="scalar")
                                    C.dma("sync", qT_s[ch * 128:(ch + 1) * 128, c0:c0 + Tn], st.t[:, :Tn], [st.b], ())
                                elif pi < 4:
                                    C.cp(st.t[:, :Tn], p.t[:, :Tn], [p.b], [st.b], eng="scalar")
                                    C.dma("sync", kT_s[(ch - 8) * 128:(ch - 7) * 128, c0:c0 + Tn], st.t[:, :Tn], [st.b], ())
                                else:
                                    C.act(st.t[:, :Tn], p.t[:, :Tn], AF.Gelu_apprx_tanh, [p.b], [st.b])
                                    C.dma("sync", guT_s[(ch - 48) * 128:(ch - 47) * 128, c0:c0 + Tn], st.t[:, :Tn], [st.b], ())
                        if tm:
                            for i, t in enumerate(tiles):
                                p = pp.next()
                                for kc in range(KC):
                                    C.mm(p.t[:], hT.t[:, kc, i * 128:(i + 1) * 128], w.t[:, kc, :],
                                         kc == 0, kc == KC - 1, [w.b, hT.b], [p.b])
                                r0 = t * 128
                                if pi < 4:
                                    st = st32.next()
                                    C.cp(st.t[:], p.t[:], [p.b], [st.b], eng="vector")
                                    C.dma("sync", ktm_s[r0:r0 + 128, (pi - 2) * 512:(pi - 1) * 512], st.t[:], [st.b], ())
                                elif pi < 8:
                                    st = st16.next()
                                    C.cp(st.t[:], p.t[:], [p.b], [st.b], eng="vector")
                                    C.dma("sync", v_s[r0:r0 + 128, (pi - 4) * 512:(pi - 3) * 512], st.t[:], [st.b], ())
                                elif pi < 12:
                                    st = st32.next()
                                    C.act(st.t[:], p.t[:], AF.Silu, [p.b], [st.b])
                                    C.dma("sync", sr_s[r0:r0 + 128, (pi - 8) * 512:(pi - 7) * 512], st.t[:], [st.b], ())
                                else:
                                    st = st32.next()
                                    C.act(st.t[:], p.t[:], AF.Gelu_apprx_tanh, [p.b], [st.b])
                                    s2 = st32.next()
                                    C.tt(s2.t[:], st.t[:], st.t[:], ALU.mult, [st.b], [s2.b])
                                    ssm = sm.next()
                                    C.red(ssm.t[:, 0:2], s2.t[:].rearrange("p (g c) -> p g c", g=2), ALU.add, [s2.b], [ssm.b])
                                    C.act(ssm.t[:, 0:2], ssm.t[:, 0:2], AF.Sqrt, [ssm.b, epsc.b], [ssm.b],
                                          bias=epsc.t[:, 0:1], scale=1.0 / 256)
                                    C.recip(ssm.t[:, 0:2], ssm.t[:, 0:2], [ssm.b], [ssm.b])
                                    so = st16.next()
                                    for gg in range(2):
                                        C.ts(so.t[:, gg * 256:(gg + 1) * 256], st.t[:, gg * 256:(gg + 1) * 256],
                                             ssm.t[:, gg:gg + 1], None, ALU.mult, None, [st.b, ssm.b], [so.b])
                                    C.dma("sync", vgn_s[r0:r0 + 128, (pi - 16) * 512:(pi - 15) * 512], so.t[:], [so.b], ())
                S.flush()

        def phase_l0b():
            with ExitStack() as es:
                w2 = C.sb(es, [33, 2048], F32, "w2")
                C.dma("sync", w2.t[:], w2blk, (), [w2.b])
                gaug = Ring([C.sb(es, [33, 128], F32, "gaug") for _ in range(2)])
                for g in gaug.items:
                    C.memset(g.t[:], 1.0, [g.b])
                zps = Ring([C.ps(es, [128, 512], F32, "zps") for _ in range(4)])
                cps = Ring([C.ps(es, [128, 512], F32, "cps") for _ in range(3)])
                esb = Ring([C.sb(es, [128, 2048], F32, "esb") for _ in range(2)])
                spb = Ring([C.sb(es, [128, 2048], F32, "spb") for _ in range(2)])
                Ep = Ring([C.sb(es, [128, 1024], F32, "Ep") for _ in range(2)])
                Em = Ring([C.sb(es, [128, 1024], F32, "Em") for _ in range(2)])
                Dd = Ring([C.sb(es, [128, 1024], F32, "Dd") for _ in range(2)])
                qin = Ring([C.sb(es, [128, 8, 128], F32, "qin") for _ in range(2)])
                kin = Ring([C.sb(es, [128, 8, 128], F32, "kin") for _ in range(2)])
                ktin = Ring([C.sb(es, [128, 1024], F32, "ktin") for _ in range(2)])
                o16 = Ring([C.sb(es, [128, 1024], BF16, "o16") for _ in range(4)])
                eb = Ring([C.sb(es, [128, 8], F32, "eb") for _ in range(4)])
                for t in range(NTT):
                    if ada_late is not None:
                        ada_late.step(2)
                    full = t < NF0
                    c0 = t * 128
                    ga = gaug.next()
                    C.dma("sync", ga.t[0:32, :], gT_s[:, c0:c0 + 128], (), [ga.b])
                    e_ = esb.next()
                    sp = spb.next()
                    dirs = (0, 1) if full else (1,)
                    for d in dirs:
                        for hh in range(2):
                            z = zps.next()
                            cc = d * 1024 + hh * 512
                            C.mm(z.t[:], ga.t[:], w2.t[:, cc:cc + 512], True, True, [ga.b, w2.b], [z.b])
                            C.act(e_.t[:, cc:cc + 512], z.t[:], AF.Exp, [z.b], [e_.b], scale=-1.0)
                        C.act(sp.t[:, d * 1024:(d + 1) * 1024], e_.t[:, d * 1024:(d + 1) * 1024], AF.Ln, [e_.b], [sp.b],
                              bias=1.0)
                    if full:
                        qi = qin.next()
                        ki = kin.next()
                        C.dma("sync", qi.t[:], qT_s[:, c0:c0 + 128].rearrange("(c p) t -> p c t", p=128), (), [qi.b])
                        C.dma("sync", ki.t[:], kT_s[:, c0:c0 + 128].rearrange("(c p) t -> p c t", p=128), (), [ki.b])
                    kt = ktin.next()
                    C.dma("sync", kt.t[:], ktm_s[c0:c0 + 128, :], (), [kt.b])
                    for d in dirs:
                        ep = Ep.next()
                        em = Em.next()
                        for hh in range(2):
                            cp_ = cps.next()
                            for k4 in range(4):
                                kc = hh * 4 + k4
                                C.mm(cp_.t[:, k4 * 128:(k4 + 1) * 128], sp.t[:, d * 1024 + kc * 128:d * 1024 + (kc + 1) * 128],
                                     cst.t[:, 1 + d, :], True, True, [sp.b, cst.b], [cp_.b])
                            C.act(ep.t[:, hh * 512:(hh + 1) * 512], cp_.t[:], AF.Exp, [cp_.b], [ep.b], scale=-1.0 / 16)
                            if full:
                                C.act(em.t[:, hh * 512:(hh + 1) * 512], cp_.t[:], AF.Exp, [cp_.b], [em.b], scale=1.0 / 16)
                        ebt = eb.next()
                        col = 127 if d == 0 else 0
                        C.cp(ebt.t[:], ep.t[:].rearrange("p (c t) -> p c t", t=128)[:, :, col], [ep.b], [ebt.b], eng="vector")
                        C.dma("gpsimd", ebl_s[d][t], ebt.t[:], [ebt.b], ())
                        if full:
                            oq = o16.next()
                            C.stt(oq.t[:], qi.t[:].rearrange("p c t -> p (c t)"), 0.0625, ep.t[:], ALU.mult, ALU.mult,
                                  [qi.b, ep.b], [oq.b])
                            C.dma("gpsimd", qeT_s[d][:, c0:c0 + 128].rearrange("(c p) t -> p c t", p=128),
                                  oq.t[:].rearrange("p (c t) -> p c t", t=128), [oq.b], ())
                            ok = o16.next()
                            C.tt(ok.t[:], ki.t[:].rearrange("p c t -> p (c t)"), em.t[:], ALU.mult, [ki.b, em.b], [ok.b])
                            C.dma("gpsimd", keT_s[d][:, c0:c0 + 128].rearrange("(c p) t -> p c t", p=128),
                                  ok.t[:].rearrange("p (c t) -> p c t", t=128), [ok.b], ())
                        dd = Dd.next()
                        for hh in range(2):
                            cp_ = cps.next()
                            cc = d * 1024 + hh * 512
                            C.mm(cp_.t[:], cst.t[:, 3 + d, :], sp.t[:, cc:cc + 512], True, True, [sp.b, cst.b], [cp_.b])
                            C.act(dd.t[:, hh * 512:(hh + 1) * 512], cp_.t[:], AF.Exp, [cp_.b], [dd.b], scale=-1.0 / 16)
                        okd = o16.next()
                        C.tt(okd.t[:], kt.t[:], dd.t[:], ALU.mult, [kt.b, dd.b], [okd.b])
                        C.dma("gpsimd", kd_s[d][c0:c0 + 128, :], okd.t[:], [okd.b], ())
                S.flush()

        def phase_l0c():
            with ExitStack() as es:
                S32 = [[C.sb(es, [128, 2, 512], F32, "S32") for h in range(4)] for d in range(2)]
                Sbf = [[C.sb(es, [128, 2, 512], BF16, "Sbf") for h in range(4)] for d in range(2)]
                for d in range(2):
                    for h in range(4):
                        C.memset(S32[d][h].t[:], 0.0, [S32[d][h].b])
                        C.memset(Sbf[d][h].t[:], 0.0, [Sbf[d][h].b])
                mask = [C.sb(es, [128, 128], F32, "mask") for d in range(2)]
                C.cp(mask[0].t[:], cst.t[:, 1, :], [cst.b], [mask[0].b])
                C.cp(mask[1].t[:], cst.t[:, 3, :], [cst.b], [mask[1].b])
                qe = Ring([C.sb(es, [128, 8, 128], BF16, "qe") for _ in range(3)])
                ke = Ring([C.sb(es, [128, 8, 128], BF16, "ke") for _ in range(3)])
                kd = Ring([C.sb(es, [128, 1024], BF16, "kd") for _ in range(3)])
                vv = Ring([C.sb(es, [128, 2048], BF16, "vv") for _ in range(3)])
                ebr = Ring([C.sb(es, [128, 8], F32, "ebr") for _ in range(3)])
                attp = Ring([C.ps(es, [128, 128], F32, "attp") for _ in range(2)])
                op_ = Ring([C.ps(es, [128, 512], F32, "op") for _ in range(2)])
                sup = Ring([C.ps(es, [128, 512], F32, "sup") for _ in range(3)])
                atts = Ring([C.sb(es, [128, 128], BF16, "atts") for _ in range(3)])
                ost = Ring([C.sb(es, [128, 512], F32, "ost") for _ in range(4)])

                def step(d, t, with_out):
                    c0 = t * 128
                    if with_out:
                        q_ = qe.next()
                        k_ = ke.next()
                        C.dma("sync", q_.t[:], qeT_s[d][:, c0:c0 + 128].rearrange("(c p) t -> p c t", p=128), (), [q_.b])
                        C.dma("sync", k_.t[:], keT_s[d][:, c0:c0 + 128].rearrange("(c p) t -> p c t", p=128), (), [k_.b])
                    kd_ = kd.next()
                    v_ = vv.next()
                    eb_ = ebr.next()
                    C.dma("sync", kd_.t[:], kd_s[d][c0:c0 + 128, :], (), [kd_.b])
                    C.dma("sync", v_.t[:], v_s[c0:c0 + 128, :], (), [v_.b])
                    C.dma("sync", eb_.t[:], ebl_s[d][t], (), [eb_.b])
                    for h in range(4):
                        if with_out:
                            ap_ = attp.next()
                            for c in range(2):
                                C.mm(ap_.t[:], k_.t[:, 2 * h + c, :], q_.t[:, 2 * h + c, :], c == 0, c == 1, [k_.b, q_.b], [ap_.b])
                            as_ = atts.next()
                            C.tt(as_.t[:], ap_.t[:], mask[d].t[:], ALU.mult, [ap_.b, mask[d].b], [as_.b])
                            o_ = op_.next()
                            for c in range(2):
                                C.mm(o_.t[:], q_.t[:, 2 * h + c, :], Sbf[d][h].t[:, c, :], c == 0, False,
                                     [q_.b, Sbf[d][h].b], [o_.b])
                            C.mm(o_.t[:], as_.t[:], v_.t[:, h * 512:(h + 1) * 512], False, True, [as_.b, v_.b], [o_.b])
                            os_ = ost.next()
                            C.cp(os_.t[:], o_.t[:], [o_.b], [os_.b], eng="scalar")
                            C.dma("gpsimd", o_s[d][c0:c0 + 128, h * 512:(h + 1) * 512], os_.t[:], [os_.b], ())
                        for c in range(2):
                            su = sup.next()
                            C.mm(su.t[:], kd_.t[:, (2 * h + c) * 128:(2 * h + c + 1) * 128], v_.t[:, h * 512:(h + 1) * 512],
                                 True, True, [kd_.b, v_.b], [su.b])
                            C.stt(S32[d][h].t[:, c, :], S32[d][h].t[:, c, :], eb_.t[:, 2 * h + c:2 * h + c + 1], su.t[:],
                                  ALU.mult, ALU.add, [S32[d][h].b, eb_.b, su.b], [S32[d][h].b])
                            C.cp(Sbf[d][h].t[:, c, :], S32[d][h].t[:, c, :], [S32[d][h].b], [Sbf[d][h].b], eng="scalar")

                ctx_t = list(range(CTX0, CTX0 + NT_CTX))
                chainA = [(0, t, True) for t in ctx_t] + [(0, t, True) for t in range(NTF)]
                chainB = ([(1, t, True) for t in reversed(ctx_t)] + [(1, t, False) for t in reversed(range(REST0, NTT))]
                          + [(1, t, True) for t in reversed(range(NTF))])
                for i in range(max(len(chainA), len(chainB))):
                    if ada_late is not None:
                        ada_late.step(2)
                    if i < len(chainB):
                        step(*chainB[i])
                    if i < len(chainA):
                        step(*chainA[i])
                S.flush()

        def phase_l0d():
            with ExitStack() as es:
                gg = C.sb(es, [128, 512], F32, "gg")
                sgn = C.sb(es, [128, 16], F32, "sgn")
                ws = C.sb(es, [128, 8, 128], F32, "wsf")
                wsb = C.sb(es, [128, 8, 128], BF16, "wsb")
                bs = C.sb(es, [128, 8, 128], F32, "bsb")
                C.dma("sync", gg.t[:], gla_gain_bc, (), [gg.b])
                C.dma("sync", sgn.t[:], sgainT, (), [sgn.b])
                C.dma("sync", ws.t[:], wsT, (), [ws.b])
                C.dma("sync", bs.t[:], bsb, (), [bs.b])
                C.cp(wsb.t[:], ws.t[:], [ws.b], [wsb.b])
                oa = Ring([C.sb(es, [128, 2048], F32, "oa") for _ in range(2)])
                ob = Ring([C.sb(es, [128, 2048], F32, "ob") for _ in range(2)])
                sr = Ring([C.sb(es, [128, 2048], F32, "sr") for _ in range(2)])
                sqj = C.sb(es, [128, 512], F32, "sqj")
                vg = Ring([C.sb(es, [128, 2048], BF16, "vg") for _ in range(2)])
                gu = Ring([C.sb(es, [128, 16, 128], F32, "gu") for _ in range(2)])
                mtm = Ring([C.sb(es, [128, 2048], BF16, "mtm") for _ in range(2)])
                mst = Ring([C.sb(es, [128, 32, 128], BF16, "mst") for _ in range(2)])
                ssm = Ring([C.sb(es, [128, 4], F32, "ssm") for _ in range(2)])
                tps = Ring([C.ps(es, [128, 4, 128], BF16, "tps") for _ in range(2)])
                sps = Ring([C.ps(es, [128, 128], F32, "sps") for _ in range(4)])
                t32 = Ring([C.sb(es, [128, 128], F32, "t32") for _ in range(3)])
                for t in range(NF0):
                    if ada_late is not None:
                        ada_late.step(2)
                    c0 = t * 128
                    a_, b_, r_, v_, u_ = oa.next(), ob.next(), sr.next(), vg.next(), gu.next()
                    C.dma("sync", a_.t[:], o_s[0][c0:c0 + 128, :], (), [a_.b])
                    C.dma("sync", b_.t[:], o_s[1][c0:c0 + 128, :], (), [b_.b])
                    C.dma("sync", r_.t[:], sr_s[c0:c0 + 128, :], (), [r_.b])
                    C.dma("sync", v_.t[:], vgn_s[c0:c0 + 128, :], (), [v_.b])
                    C.dma("sync", u_.t[:], guT_s[:, c0:c0 + 128].rearrange("(c p) t -> p c t", p=128), (), [u_.b])
                    C.tt(a_.t[:], a_.t[:], b_.t[:], ALU.add, [a_.b, b_.b], [a_.b])
                    s_ = ssm.next()
                    for h in range(4):
                        C.act(sqj.t[:], a_.t[:, h * 512:(h + 1) * 512], AF.Square, [a_.b], [sqj.b, s_.b], accum=s_.t[:, h:h + 1])
                    C.act(s_.t[:], s_.t[:], AF.Sqrt, [s_.b, epsc.b], [s_.b], bias=epsc.t[:, 0:1], scale=1.0 / 512)
                    C.recip(s_.t[:], s_.t[:], [s_.b], [s_.b])
                    m_ = mtm.next()
                    for h in range(4):
                        sl = slice(h * 512, (h + 1) * 512)
                        C.stt(a_.t[:, sl], a_.t[:, sl], s_.t[:, h:h + 1], gg.t[:], ALU.mult, ALU.mult, [a_.b, s_.b, gg.b], [a_.b])
                    C.tt(m_.t[:], a_.t[:], r_.t[:], ALU.mult, [a_.b, r_.b], [m_.b])
                    ms = mst.next()
                    for q4 in range(4):
                        tp = tps.next()
                        for i in range(4):
                            ch = q4 * 4 + i
                            C.tr(tp.t[:, i, :], m_.t[:, ch * 128:(ch + 1) * 128], ident_bf.t[:], [m_.b, ident_bf.b], [tp.b])
                        C.cp(ms.t[:, q4 * 4:(q4 + 1) * 4, :], tp.t[:], [tp.b], [ms.b], eng="scalar")
                    for c in range(16):
                        g = c // 2
                        sp_ = sps.next()
                        C.mm(sp_.t[:], v_.t[:, c * 128:(c + 1) * 128], wsb.t[:, g, :], True, True, [v_.b, wsb.b], [sp_.b])
                        tt_ = t32.next()
                        C.stt(tt_.t[:], sp_.t[:], sgn.t[:, c:c + 1], bs.t[:, g, :], ALU.mult, ALU.add, [sp_.b, sgn.b, bs.b], [tt_.b])
                        C.tt(ms.t[:, 16 + c, :], tt_.t[:], u_.t[:, c, :], ALU.mult, [tt_.b, u_.b], [ms.b])
                    C.dma("gpsimd", mixT_s[:, c0:c0 + 128].rearrange("(c p) t -> p c t", p=128), ms.t[:], [ms.b], ())
                S.flush()

        def phase_outproj(w_out, l, tiles_all, src_x, dst_x):
            with ExitStack() as es:
                mT = Ring([C.sb(es, [128, KC, 512], BF16, "mT") for _ in range(2)])
                wp = Ring([C.sb(es, [128, KC, 512], BF16, "wp") for _ in range(3)])
                pp = Ring([C.ps(es, [128, 512], F32, "pp") for _ in range(4)])
                xr = Ring([C.sb(es, [128, 512], F32, "xr") for _ in range(4)])
                xo = Ring([C.sb(es, [128, 512], F32, "xo") for _ in range(4)])
                for tiles in groups_of(tiles_all):
                    Tn = len(tiles) * 128
                    c0 = tiles[0] * 128
                    sg = segs(tiles)
                    m_ = mT.next()
                    C.dma("sync", m_.t[:, :, :Tn], mixT_s[:, c0:c0 + Tn].rearrange("(c p) t -> p c t", p=128), (), [m_.b])

                    def load(pi):
                        w = wp.next()
                        C.dma("gpsimd", w.t[:], wview(w_out, pi * 512, 512), (), [w.b])
                        return w
                    q = [load(0), load(1)]
                    for pi in range(8):
                        if pi + 2 < 8:
                            q.append(load(pi + 2))
                        w = q.pop(0)
                        for jj in range(4):
                            ch = pi * 4 + jj
                            p = pp.next()
                            for kc in range(KC):
                                C.mm(p.t[:, :Tn], w.t[:, kc, jj * 128:(jj + 1) * 128], m_.t[:, kc, :Tn],
                                     kc == 0, kc == KC - 1, [w.b, m_.b], [p.b])
                            x_ = xr.next()
                            C.dma("sync", x_.t[:, :Tn], src_x[ch * 128:(ch + 1) * 128, c0:c0 + Tn], (), [x_.b])
                            o_ = xo.next()
                            for (a, b, j) in sg:
                                C.stt(o_.t[:, a:b], p.t[:, a:b], mod(l, 2, ch, j), x_.t[:, a:b], ALU.mult, ALU.add,
                                      [p.b, x_.b, modT.b], [o_.b])
                            C.dma("sync", dst_x[ch * 128:(ch + 1) * 128, c0:c0 + Tn], o_.t[:, :Tn], [o_.b], ())
                S.flush()

        def phase_ffn(l, tiles_all, src_x, mid_x, dst_x, dst_col_off=0):
            halves = [(0, FHC // 2), (FHC // 2, FHC)] if FHC >= 2 else [(0, FHC)]
            HC = max(b - a for a, b in halves)
            with ExitStack() as es:
                hT = C.sb(es, [128, KC, 512], BF16, "hT")
                AT = C.sb(es, [128, HC, 512], BF16, "AT")
                xring = Ring([C.sb(es, [128, 512], F32, "xc") for _ in range(4)])
                sqring = Ring([C.sb(es, [128, 512], F32, "sq") for _ in range(2)])
                tmpring = Ring([C.sb(es, [128, 512], F32, "tmp") for _ in range(2)])
                rstd = C.sb(es, [128, 512], F32, "rstd")
                ss_ps = C.ps(es, [128, 512], F32, "ssps")
                slots = Ring([C.sb(es, [128, 16384], BF16, "slot") for _ in range(3)])
                gp = Ring([C.ps(es, [128, 512], F32, "gp") for _ in range(2)])
                up = Ring([C.ps(es, [128, 512], F32, "up") for _ in range(2)])
                dp = Ring([C.ps(es, [128, 512], F32, "dp") for _ in range(2)])
                sgr = Ring([C.sb(es, [128, 512], F32, "sgr") for _ in range(2)])
                xo = Ring([C.sb(es, [128, 512], F32, "xo") for _ in range(3)])
                for tiles in groups_of(tiles_all):
                    Tn = len(tiles) * 128
                    c0 = tiles[0] * 128
                    sg = segs(tiles)
                    norm_modulate(es, src_x, tiles, l, 3, 4, hT, xring, sqring, ss_ps, rstd, tmpring)
                    for hi, (ha, hb) in enumerate(halves):
                        nh = hb - ha
                        pans = [(c, min(2, hb - c)) for c in range(ha, hb, 2)]

                        def load_gu(pn):
                            cc, n = pn
                            sl = slots.next()
                            gv = sl.t[:, 0:KC * 256].rearrange("p (k n) -> p k n", n=256)
                            uv = sl.t[:, KC * 256:2 * KC * 256].rearrange("p (k n) -> p k n", n=256)
                            C.dma("gpsimd", gv[:, :, :n * 128], wview(ffn_g[l], cc * 128, n * 128), (), [sl.b])
                            C.dma("gpsimd", uv[:, :, :n * 128], wview(ffn_u[l], cc * 128, n * 128), (), [sl.b])
                            return (sl, gv, uv)
                        q = [load_gu(pans[0])] + ([load_gu(pans[1])] if len(pans) > 1 else [])
                        for ii, (cc, n) in enumerate(pans):
                            if ii + 2 < len(pans):
                                q.append(load_gu(pans[ii + 2]))
                            sl, gv, uv = q.pop(0)
                            for jj in range(n):
                                g_ = gp.next()
                                u_ = up.next()
                                for kc in range(KC):
                                    C.mm(g_.t[:, :Tn], gv[:, kc, jj * 128:(jj + 1) * 128], hT.t[:, kc, :Tn],
                                         kc == 0, kc == KC - 1, [sl.b, hT.b], [g_.b])
                                for kc in range(KC):
                                    C.mm(u_.t[:, :Tn], uv[:, kc, jj * 128:(jj + 1) * 128], hT.t[:, kc, :Tn],
                                         kc == 0, kc == KC - 1, [sl.b, hT.b], [u_.b])
                                s_ = sgr.next()
                                C.act(s_.t[:, :Tn], g_.t[:, :Tn], AF.Silu, [g_.b], [s_.b])
                                C.tt(AT.t[:, cc - ha + jj, :Tn], s_.t[:, :Tn], u_.t[:, :Tn], ALU.mult, [s_.b, u_.b], [AT.b])
                        srcx = src_x if hi == 0 else mid_x
                        last = hi == len(halves) - 1
                        dstx = dst_x if last else mid_x

                        def load_d(j):
                            sl = slots.next()
                            dv = sl.t[:, 0:nh * 128].rearrange("p (k n) -> p k n", n=128)
                            C.dma("gpsimd", dv, ffn_d[l][ha * 128:hb * 128, j * 128:(j + 1) * 128]
                                  .rearrange("(kc p) n -> p kc n", p=128), (), [sl.b])
                            return (sl, dv)
                        q = [load_d(0), load_d(1)]
                        for j in range(KC):
                            if j + 2 < KC:
                                q.append(load_d(j + 2))
                            sl, dv = q.pop(0)
                            p = dp.next()
                            for kc in range(nh):
                                C.mm(p.t[:, :Tn], dv[:, kc, :], AT.t[:, kc, :Tn], kc == 0, kc == nh - 1, [sl.b, AT.b], [p.b])
                            x_ = xring.next()
                            C.dma("sync", x_.t[:, :Tn], srcx[j * 128:(j + 1) * 128, c0:c0 + Tn], (), [x_.b])
                            o_ = xo.next()
                            for (a, b, jx) in sg:
                                C.stt(o_.t[:, a:b], p.t[:, a:b], mod(l, 5, j, jx), x_.t[:, a:b], ALU.mult, ALU.add,
                                      [p.b, x_.b, modT.b], [o_.b])
                            if last:
                                C.dma("sync", dstx[j * 128:(j + 1) * 128, c0 + dst_col_off:c0 + dst_col_off + Tn],
                                      o_.t[:, :Tn], [o_.b], ())
                            else:
                                C.dma("sync", dstx[j * 128:(j + 1) * 128, c0:c0 + Tn], o_.t[:, :Tn], [o_.b], ())
                S.flush()

        def phase_l1a():
            with ExitStack() as es:
                hT = C.sb(es, [128, KC, 512], BF16, "hT")
                xring = Ring([C.sb(es, [128, 512], F32, "xc") for _ in range(4)])
                sqring = Ring([C.sb(es, [128, 512], F32, "sq") for _ in range(2)])
                tmpring = Ring([C.sb(es, [128, 512], F32, "tmp") for _ in range(2)])
                rstd = C.sb(es, [128, 512], F32, "rstd")
                ss_ps = C.ps(es, [128, 512], F32, "ssps")
                wp = Ring([C.sb(es, [128, KC, 512], BF16, "wp") for _ in range(3)])
                pp = Ring([C.ps(es, [128, 512], F32, "pp") for _ in range(4)])
                tps = Ring([C.ps(es, [128, 4, 128], BF16, "tps") for _ in range(2)])
                gq = C.sb(es, [128, 128], F32, "gq")
                gk = C.sb(es, [128, 128], F32, "gk")
                C.dma("sync", gq.t[:], qg_bc, (), [gq.b])
                C.dma("sync", gk.t[:], kg_bc, (), [gk.b])
                grep_ = {}
                for nm, g_ in (("q", gq), ("k", gk)):
                    gf = C.sb(es, [128, 512], F32, "gfr")
                    for h in range(4):
                        C.cp(gf.t[:, h * 128:(h + 1) * 128], g_.t[:], [g_.b], [gf.b])
                    grep_[nm] = (None, None, gf)
                rope = [C.sb(es, [128, 2, 256], F32, "rope") for _ in range(4)]
                sq32 = Ring([C.sb(es, [128, 512], F32, "sq32") for _ in range(2)])
                qn = Ring([C.sb(es, [128, 512], F32, "qn") for _ in range(3)])
                r1 = Ring([C.sb(es, [128, 256], F32, "r1") for _ in range(4)])
                qr = Ring([C.sb(es, [128, 512], BF16, "qr") for _ in range(4)])
                sm = Ring([C.sb(es, [128, 4], F32, "sm") for _ in range(4)])
                st16 = Ring([C.sb(es, [128, 4, 128], BF16, "st16") for _ in range(3)])
                sv16 = Ring([C.sb(es, [128, 512], BF16, "sv16") for _ in range(3)])
                fifo = []

                def stage2(o_, pi, r0):
                    tp = tps.next()
                    for h in range(4):
                        C.tr(tp.t[:, h, :], o_.t[:, h * 128:(h + 1) * 128], ident_bf.t[:], [o_.b, ident_bf.b], [tp.b])
                    so = st16.next()
                    C.cp(so.t[:], tp.t[:], [tp.b], [so.b], eng="scalar")
                    if pi < 8:
                        dst = qT1_s[pi * 512:(pi + 1) * 512, r0:r0 + 128]
                    else:
                        dst = kT1_s[(pi - 8) * 512:(pi - 7) * 512, r0:r0 + 128]
                    C.dma("sync", dst.rearrange("(h d) t -> d h t", d=128), so.t[:], [so.b], ())

                for tiles in groups_of(list(range(NF0))):
                    Tn = len(tiles) * 128
                    norm_modulate(es, x2T_s, tiles, 1, 0, 1, hT, xring, sqring, ss_ps, rstd, tmpring)
                    for i, t in enumerate(tiles):
                        if t < NTF:
                            C.dma("sync", rope[i].t[:], rope_cs[t * 128:(t + 1) * 128], (), [rope[i].b])

                    def load(pi):
                        w = wp.next()
                        C.dma("gpsimd", w.t[:], wview(w_in1, pi * 512, 512), (), [w.b])
                        return w
                    q = [load(0), load(1)]
                    for pi in range(12):
                        if pi + 2 < 12:
                            q.append(load(pi + 2))
                        w = q.pop(0)
                        for i, t in enumerate(tiles):
                            if pi < 8 and t >= NT_OWN:
                                continue
                            p = pp.next()
                            for kc in range(KC):
                                C.mm(p.t[:], hT.t[:, kc, i * 128:(i + 1) * 128], w.t[:, kc, :], kc == 0, kc == KC - 1,
                                     [w.b, hT.b], [p.b])
                            r0 = t * 128
                            if pi >= 10:
                                sv = sv16.next()
                                C.cp(sv.t[:], p.t[:], [p.b], [sv.b], eng="scalar")
                                C.dma("sync", v1_s[r0:r0 + 128, (pi - 10) * 512:(pi - 9) * 512], sv.t[:], [sv.b], ())
                                continue
                            kk = 0 if pi < 8 else 1
                            s2 = sq32.next()
                            C.act(s2.t[:], p.t[:], AF.Square, [p.b], [s2.b])
                            ssm = sm.next()
                            C.red(ssm.t[:], s2.t[:].rearrange("p (h d) -> p h d", h=4), ALU.add, [s2.b], [ssm.b])
                            C.act(ssm.t[:], ssm.t[:], AF.Sqrt, [ssm.b, epsc.b], [ssm.b], bias=epsc.t[:, 0:1], scale=1.0 / 128)
                            C.recip(ssm.t[:], ssm.t[:], [ssm.b], [ssm.b])
                            n_ = qn.next()
                            for h in range(4):
                                sl = slice(h * 128, (h + 1) * 128)
                                C.act(n_.t[:, sl], p.t[:, sl], AF.Copy, [p.b, ssm.b], [n_.b], scale=ssm.t[:, h:h + 1])
                            o_ = qr.next()
                            gf = grep_["q" if pi < 8 else "k"][2]
                            if t < NTF:
                                rp = rope[i]
                                C.tt(n_.t[:], n_.t[:], gf.t[:], ALU.mult, [n_.b, gf.b], [n_.b])
                                nv = n_.t[:].rearrange("p (h d) -> p h d", h=4)
                                ov = o_.t[:].rearrange("p (h d) -> p h d", h=4)
                                cosv = rp.t[:, 0, :].rearrange("p (h d) -> p h d", h=4)
                                sinv = rp.t[:, 1, :].rearrange("p (h d) -> p h d", h=4)
                                v4 = lambda x: x.t[:].rearrange("p (h d) -> p h d", h=4)
                                a1, a2, a3, a4 = r1.next(), r1.next(), r1.next(), r1.next()
                                C.tt(v4(a1), nv[:, :, 0:64], cosv, ALU.mult, [n_.b, rp.b], [a1.b])
                                C.tt(v4(a2), nv[:, :, 64:128], sinv, ALU.mult, [n_.b, rp.b], [a2.b])
                                C.tt(ov[:, :, 0:64], v4(a1), v4(a2), ALU.subtract, [a1.b, a2.b], [o_.b])
                                C.tt(v4(a3), nv[:, :, 0:64], sinv, ALU.mult, [n_.b, rp.b], [a3.b])
                                C.tt(v4(a4), nv[:, :, 64:128], cosv, ALU.mult, [n_.b, rp.b], [a4.b])
                                C.tt(ov[:, :, 64:128], v4(a3), v4(a4), ALU.add, [a3.b, a4.b], [o_.b])
                            else:
                                C.tt(o_.t[:], n_.t[:], gf.t[:], ALU.mult, [n_.b, gf.b], [o_.b])
                            fifo.append((o_, pi, r0))
                            if len(fifo) > 2:
                                stage2(*fifo.pop(0))
                    while fifo:
                        stage2(*fifo.pop(0))
                S.flush()

        def phase_l1b():
            with ExitStack() as es:
                kT = C.sb(es, [128, NF0, 8, 128], BF16, "kTall")
                vA = C.sb(es, [128, NF0, 1024], BF16, "vall")
                kb_ = [Buf() for _ in range(NF0)]
                vb_ = [Buf() for _ in range(NF0)]
                for t in range(NF0):
                    C.dma("sync", kT.t[:, t], kT1_s[:, t * 128:(t + 1) * 128].rearrange("(h d) t -> d h t", d=128), (), [kb_[t]])
                    C.dma("sync", vA.t[:, t], v1_s[t * 128:(t + 1) * 128, :], (), [vb_[t]])
                sk = C.sb(es, [128, 32], F32, "sk")
                esk = C.sb(es, [128, 32], F32, "esk")
                negc = C.sb(es, [128, 1], F32, "negc")
                C.memset(negc.t[:], -SOFT_C, [negc.b])
                C.dma("sync", sk.t[:], sink_bc, (), [sk.b])
                C.act(esk.t[:], sk.t[:], AF.Exp, [sk.b, negc.b], [esk.b], bias=negc.t[:, 0:1])
                mk = [C.sb(es, [128, 4, 128], BF16, "mk") for _ in range(2)]
                for h in range(4):
                    C.cp(mk[0].t[:, h, :], cst.t[:, 2, :], [cst.b], [mk[0].b])
                    C.cp(mk[1].t[:, h, :], cst.t[:, 1, :], [cst.b], [mk[1].b])
                qT = Ring([C.sb(es, [128, 32, 128], BF16, "qT") for _ in range(2)])
                sps = Ring([C.ps(es, [128, 512], F32, "sps") for _ in range(4)])
                dps = Ring([C.ps(es, [128, 512], F32, "dps") for _ in range(2)])
                ops_ = Ring([C.ps(es, [128, 512], F32, "ops") for _ in range(2)])
                PT = Ring([C.sb(es, [128, 512], BF16, "PT") for _ in range(4)])
                den = Ring([C.sb(es, [128, 512], F32, "den") for _ in range(2)])
                ast = Ring([C.sb(es, [128, 32, 128], BF16, "ast") for _ in range(2)])
                sc = 128.0 ** -0.5
                LOOK = 3
                for n in range(NT_OWN):
                    q_ = qT.next()
                    C.dma("sync", q_.t[:], qT1_s[:, n * 128:(n + 1) * 128].rearrange("(h d) t -> d h t", d=128), (), [q_.b])
                    blocks = []
                    if n > 0:
                        blocks.append((n - 1, 0))
                    blocks.append((n, None))
                    blocks.append((n + 1, 1))
                    for t in range(CTX0, CTX0 + NT_CTX):
                        blocks.append((t, None))
                    nb = len(blocks)
                    a_ = ast.next()
                    pairs = [(g, bi) for g in range(8) for bi in range(nb)]

                    def score(idx):
                        g, bi = pairs[idx]
                        kt = blocks[bi][0]
                        s_ = sps.next()
                        C.mm(s_.t[:], kT.t[:, kt, g, :], q_.t[:, 4 * g:4 * g + 4, :], True, True, [kb_[kt], q_.b], [s_.b])
                        return s_
                    sq_ = [score(i) for i in range(min(LOOK, len(pairs)))]
                    d_ = o_ = None
                    for idx, (g, bi) in enumerate(pairs):
                        if idx + LOOK < len(pairs):
                            sq_.append(score(idx + LOOK))
                        s_ = sq_.pop(0)
                        kt, mi = blocks[bi]
                        if bi == 0:
                            d_ = dps.next()
                            o_ = ops_.next()
                        p_ = PT.next()
                        C.act(p_.t[:], s_.t[:], AF.Exp, [s_.b, negc.b], [p_.b], bias=negc.t[:, 0:1], scale=sc)
                        if mi is not None:
                            C.tt(p_.t[:], p_.t[:], mk[mi].t[:].rearrange("p h t -> p (h t)"), ALU.mult,
                                 [p_.b, mk[mi].b], [p_.b])
                        C.mm(d_.t[:], ones_bf.t[:], p_.t[:], bi == 0, bi == nb - 1, [p_.b, ones_bf.b], [d_.b])
                        C.mm(o_.t[:], vA.t[:, kt, g * 128:(g + 1) * 128], p_.t[:], bi == 0, bi == nb - 1,
                             [p_.b, vb_[kt]], [o_.b])
                        if bi == nb - 1:
                            dn = den.next()
                            for h in range(4):
                                sl = slice(h * 128, (h + 1) * 128)
                                C.ts(dn.t[:, sl], d_.t[:, sl], esk.t[:, 4 * g + h:4 * g + h + 1], None, ALU.add, None,
                                     [d_.b, esk.b], [dn.b])
                            C.act(dn.t[:], dn.t[:], AF.Ln, [dn.b], [dn.b])
                            C.act(dn.t[:], dn.t[:], AF.Exp, [dn.b], [dn.b], scale=-1.0)
                            C.tt(a_.t[:, 4 * g:4 * g + 4, :].rearrange("p h t -> p (h t)"), o_.t[:], dn.t[:], ALU.mult,
                                 [o_.b, dn.b], [a_.b])
                    C.dma("gpsimd", mixT_s[:, n * 128:(n + 1) * 128].rearrange("(h d) t -> d h t", d=128), a_.t[:], [a_.b], ())
                S.flush()

        ph = cfg.get("PHASES")
        run = lambda name: (ph is None) or (name in ph)
        own = list(range(NT_OWN))
        full0 = list(range(NF0))
        if run("l0a"):
            phase_l0a()
        if cfg["ADA"] and ADA_INTERLEAVE:
            ada_late = AdaStream(ada_es, [(0, j) for j in range(32, 96)] + [(1, j) for j in range(96)])
        if run("l0b"):
            phase_l0b()
        if run("l0c"):
            phase_l0c()
        if run("l0d"):
            phase_l0d()
        if ada_late is not None:
            ada_late.finish([(0, 4), (1, 1), (1, 4)])
            S.flush()
        ada_es.close()
        if DEBUG:
            modT_o = nc.dram_tensor("modT_o", [128, 2 * 6 * KC * 2], F32, kind="ExternalOutput").ap()
            C.dma("sync", modT_o, modT.t[:], [modT.b], ())
            S.flush()
        if run("l0e"):
            phase_outproj(w_out0[:], 0, full0, xT_in, x1T_s)
        if run("l0f"):
            phase_ffn(0, full0, x1T_s, xhT_s, x2T_s)
        if run("l1a"):
            phase_l1a()
        if run("l1b"):
            phase_l1b()
        if run("l1c"):
            phase_outproj(w_out1[:], 1, own, x2T_s, x1T_s)
        if run("l1d"):
            phase_ffn(1, own, x1T_s, xhT_s, yT)
        S.final_wait()
    return nc


def make_consts():
    i = np.arange(128)[:, None]
    t = np.arange(128)[None, :]
    c = np.zeros((128, 8, 128), np.float32)
    c[:, 0] = (i == t)
    c[:, 1] = (i <= t)
    c[:, 2] = (i >= t)
    c[:, 3] = (i > t)
    c[:, 4] = (i < t)
    return c


def rope_tables(seq, grid_w=64, head_dim=128, theta=10000.0):
    rows = seq // grid_w
    row = np.repeat(np.arange(rows, dtype=np.float32), grid_w)
    col = np.tile(np.arange(grid_w, dtype=np.float32), rows)
    n_freq = head_dim // 4
    inv_freq = (np.float32(theta) ** (-np.arange(n_freq, dtype=np.float32) / n_freq)).astype(np.float32)
    ang = np.concatenate([row[:, None] * inv_freq, col[:, None] * inv_freq], axis=-1).astype(np.float32)
    return np.cos(ang).astype(np.float32), np.sin(ang).astype(np.float32)


def host_inputs(cfg, inp, core):
    NT_OWN, NT_HALO, NT_REST, NT_CTX, FH = cfg["NT_OWN"], cfg["NT_HALO"], cfg["NT_REST"], cfg["NT_CTX"], cfg["FH"]
    NTF = NT_OWN + NT_HALO
    b, flip = core // 2, core % 2
    f32 = lambda a: np.ascontiguousarray(a, dtype=np.float32)
    x = np.asarray(inp["x"][b])
    ctx = np.asarray(inp["ctx"][b])
    if flip:
        x = x[::-1]
        ctx = ctx[::-1]
    tok = np.concatenate([x[:NTF * 128], ctx, x[NTF * 128:]], axis=0)
    m = {}
    m["xT"] = f32(tok.T)
    cc = np.stack([np.asarray(inp["c"][b]), np.asarray(inp["c_ctx"])], axis=-1)
    m["cT"] = f32(cc.reshape(KC, 128, 2).transpose(1, 0, 2))
    if cfg["ADA"]:
        m["ada_w"] = inp["_ada_w"]
        m["ada_bT"] = inp["_ada_bT"]
    else:
        m["modT_in"] = inp["_modT"][core]
    m["w_in0"] = inp["_w_in0"]
    wi = np.asarray(inp["even_w_in"][0])
    gf, gb = wi[:, 6144:6160], wi[:, 6160:6176]
    w2f, w2b = np.asarray(inp["even_gate_w2_fwd"][0]), np.asarray(inp["even_gate_w2_bwd"][0])
    b2f, b2b = np.asarray(inp["even_gate_b_fwd"][0]), np.asarray(inp["even_gate_b_bwd"][0])
    if flip:
        gf, gb, w2f, w2b, b2f, b2b = gb, gf, w2b, w2f, b2b, b2f
    m["w_g"] = f32(np.concatenate([gf, gb], axis=1))
    w2 = np.zeros((33, 2048), np.float32)
    w2[0:16, 0:1024] = w2f
    w2[16:32, 1024:2048] = w2b
    w2[32, 0:1024] = b2f
    w2[32, 1024:2048] = b2b
    m["w2blk"] = w2
    m["gla_gain_bc"] = f32(np.broadcast_to(np.asarray(inp["even_gla_norm_gain"][0])[None, :], (128, 512)))
    m["sgainT"] = f32(np.asarray(inp["even_sgu_norm_gain"][0]).reshape(16, 128).T)
    ws = np.asarray(inp["even_sgu_w_s"][0])
    bs = np.asarray(inp["even_sgu_b_s"][0])
    if flip:
        ws = ws[:, ::-1, ::-1]
        bs = bs[:, ::-1]
    m["wsT"] = f32(ws.transpose(2, 0, 1))
    m["bsb"] = f32(np.broadcast_to(bs[None], (128, 8, 128)))
    m["w_out0"] = inp["_w_out0"]
    m["w_in1"] = inp["_w_in1"]
    m["qg_bc"] = f32(np.broadcast_to(np.asarray(inp["odd_q_norm_gain"][0])[None, :], (128, 128)))
    m["kg_bc"] = f32(np.broadcast_to(np.asarray(inp["odd_k_norm_gain"][0])[None, :], (128, 128)))
    m["sink_bc"] = f32(np.broadcast_to(np.asarray(inp["odd_sink"][0])[None, :], (128, 32)))
    m["w_out1"] = inp["_w_out1"]
    m["ffn_g"] = inp["_ffn_g"]
    m["ffn_u"] = inp["_ffn_u"]
    m["ffn_d"] = inp["_ffn_d"]
    cos, sin = inp["_rope"]
    if flip:
        cos, sin = cos[::-1], sin[::-1]
    cs = np.stack([np.tile(cos[:NTF * 128], (1, 4)), np.tile(sin[:NTF * 128], (1, 4))], axis=1)
    m["rope_cs"] = f32(cs)
    m["consts"] = inp["_consts"]
    return m


def prep_shared(cfg, inputs):
    inp = dict(inputs)
    f32 = lambda a: np.ascontiguousarray(a, dtype=np.float32)
    wi = np.asarray(inputs["even_w_in"][0])
    inp["_w_in0"] = f32(np.concatenate([wi[:, 0:6144], wi[:, 6176:10272]], axis=1))
    inp["_w_out0"] = f32(inputs["even_w_out"][0])
    inp["_w_in1"] = f32(inputs["odd_w_in"][0])
    inp["_w_out1"] = f32(inputs["odd_w_out"][0])
    inp["_ffn_g"] = f32(inputs["ffn_w_gate"])
    inp["_ffn_u"] = f32(inputs["ffn_w_up"])
    inp["_ffn_d"] = f32(inputs["ffn_w_down"])
    if cfg["ADA"]:
        inp["_ada_w"] = f32(inputs["ada_w"])
        inp["_ada_bT"] = f32(np.asarray(inputs["ada_b"]).reshape(2, 6, KC, 128).transpose(3, 0, 1, 2))
    seq = (cfg["NT_OWN"] + cfg["NT_HALO"] + cfg["NT_REST"]) * 128
    inp["_rope"] = rope_tables(seq)
    inp["_consts"] = make_consts()
    return inp


def kernel(**inputs):
    cfg = default_cfg()
    B = inputs["x"].shape[0]
    n_cores = 2 * B
    inp = prep_shared(cfg, inputs)
    nc = build_program(cfg)
    in_maps = [host_inputs(cfg, inp, c) for c in range(n_cores)]
    res = run_bass_kernel_spmd(nc, in_maps, core_ids=list(range(n_cores)))
    L = inputs["x"].shape[1]
    half = cfg["NT_OWN"] * 128
    out = np.empty((B, L, D), np.float32)
    for c in range(n_cores):
        y = np.asarray(res.results[c]["yT"]).T
        b, flip = c // 2, c % 2
        if flip:
            out[b, L - half:] = y[::-1]
        else:
            out[b, :half] = y
    return out
```

```python
import numpy as np
from contextlib import ExitStack
import concourse.bass as bass
import concourse.mybir as mybir
from concourse.bass_utils import run_bass_kernel_spmd

F32 = mybir.dt.float32
BF16 = mybir.dt.bfloat16
AF = mybir.ActivationFunctionType
ALU = mybir.AluOpType
AX = mybir.AxisListType

D = 4096
KC = 32
EPS = 1e-6
SOFT_C = 4.0
SAME_ENGINE_SYNC = True
ADA_INTERLEAVE = False


def default_cfg():
    return dict(NT_OWN=16, NT_HALO=1, NT_REST=15, NT_CTX=2, FH=11008, DEBUG=False, ADA=True)


class Buf:
    __slots__ = ("name", "last_w", "readers", "dma_readers")

    def __init__(self, name=""):
        self.name = name
        self.last_w = None
        self.readers = {}
        self.dma_readers = []


class Op:
    __slots__ = ("eng", "fn", "deps", "need_inc", "event", "is_dma", "batch")


class Sched:
    ENG = ("tensor", "vector", "scalar", "gpsimd", "sync")
    NPOOL = 12

    def __init__(self, nc):
        self.nc = nc
        self.pending = []
        self.batch = 0
        self.bar = {}
        self.sems = {e: nc.alloc_semaphore("sc_" + e) for e in self.ENG}
        self.cnt = {e: 0 for e in self.ENG}
        self.seen = {e: {} for e in self.ENG}
        self.pool = {q: [nc.alloc_semaphore("sd_%s%d" % (q, i)) for i in range(self.NPOOL)]
                     for q in ("sync", "gpsimd")}
        self.pool_val = {q: [0] * self.NPOOL for q in ("sync", "gpsimd")}
        self.rr = {q: 0 for q in ("sync", "gpsimd")}
        self.last_on_eng = {}
        self.n_ops = 0

    def add(self, eng, fn, reads=(), writes=(), dma=False):
        o = Op()
        o.eng = eng
        o.fn = fn
        o.is_dma = dma
        o.need_inc = False
        o.event = None
        o.batch = self.batch
        deps = set()
        for b in reads:
            if b.last_w is not None:
                deps.add(b.last_w)
        for b in writes:
            if b.last_w is not None:
                deps.add(b.last_w)
            deps.update(b.readers.values())
            deps.update(b.dma_readers)
        for b in writes:
            b.last_w = o
            b.readers = {}
            b.dma_readers = []
        for b in reads:
            if dma:
                b.dma_readers.append(o)
            else:
                b.readers[eng] = o
        dl = []
        for d in deps:
            if d is o or d.batch != self.batch:
                continue
            if d.eng == eng and not d.is_dma and not dma:
                if eng == "tensor" or not SAME_ENGINE_SYNC:
                    continue
            dl.append(d)
        o.deps = dl
        self.pending.append(o)
        return o

    def _wait(self, eng, sem_key, sem, val):
        if self.seen[eng].get(sem_key, 0) < val:
            getattr(self.nc, eng).wait_ge(sem, val)
            self.seen[eng][sem_key] = val

    def flush(self):
        ops = self.pending
        self.pending = []
        for o in ops:
            for d in o.deps:
                d.need_inc = True
        last = {}
        for o in ops:
            if not o.is_dma:
                last[o.eng] = o
        for o in last.values():
            o.need_inc = True
        nc = self.nc
        first_on_eng = set()
        for o in ops:
            e = o.eng
            h = getattr(nc, e)
            if e not in first_on_eng:
                first_on_eng.add(e)
                for (k, s, v) in self.bar.get(e, ()):
                    self._wait(e, k, s, v)
            for d in o.deps:
                k, s, v = d.event
                self._wait(e, k, s, v)
            if o.is_dma:
                i = self.rr[e]
                self.rr[e] = (i + 1) % self.NPOOL
                s = self.pool[e][i]
                v = self.pool_val[e][i]
                key = (e, i)
                if v > 0:
                    self._wait(e, key, s, v)
                ins = o.fn(h)
                ins.then_inc(s, 16)
                self.pool_val[e][i] = v + 16
                o.event = (key, s, v + 16)
            else:
                ins = o.fn(h)
                if o.need_inc:
                    self.cnt[e] += 1
                    ins.then_inc(self.sems[e], 1)
                    o.event = (e, self.sems[e], self.cnt[e])
            o.fn = None
        self.n_ops += len(ops)
        evs = [(e, self.sems[e], self.cnt[e]) for e in self.ENG if self.cnt[e] > 0]
        for q in ("sync", "gpsimd"):
            for i in range(self.NPOOL):
                if self.pool_val[q][i] > 0:
                    evs.append(((q, i), self.pool[q][i], self.pool_val[q][i]))
        self.bar = {e: evs for e in self.ENG}
        self.batch += 1

    def final_wait(self):
        for e in self.ENG:
            for (k, s, v) in self.bar.get(e, ()):
                self._wait(e, k, s, v)


class T:
    __slots__ = ("t", "b")

    def __init__(self, t, b=None):
        self.t = t
        self.b = b if b is not None else Buf()


class Ctx:
    def __init__(self, nc, cfg):
        self.nc = nc
        self.cfg = cfg
        self.S = Sched(nc)
        self.uid = 0

    def name(self, p):
        self.uid += 1
        return "%s_%d" % (p, self.uid)

    def sb(self, es, shape, dt, name="t"):
        return T(es.enter_context(self.nc.sbuf_tensor(self.name(name), list(shape), dt)))

    def ps(self, es, shape, dt=F32, name="p"):
        return T(es.enter_context(self.nc.psum_tensor(self.name(name), list(shape), dt)))

    def dma(self, q, out, in_, reads=(), writes=()):
        return self.S.add(q, lambda h, o=out, i=in_: h.dma_start(out=o, in_=i), reads, writes, dma=True)

    def mm(self, out, lhsT, rhs, start, stop, reads=(), writes=()):
        return self.S.add("tensor", lambda h, o=out, l=lhsT, r=rhs, a=start, b=stop:
                          h.matmul(o, l, r, start=a, stop=b), reads, writes)

    def tr(self, out, in_, ident, reads=(), writes=()):
        return self.S.add("tensor", lambda h, o=out, i=in_, d=ident: h.transpose(o, i, d), reads, writes)

    def act(self, out, in_, func, reads=(), writes=(), bias=None, scale=None, accum=None, eng="scalar"):
        def fn(h, o=out, i=in_, f=func, b=bias, s=scale, a=accum):
            kw = {}
            if b is not None:
                kw["bias"] = b
            if s is not None:
                kw["scale"] = s
            if a is not None:
                kw["accum_out"] = a
            return h.activation(o, i, f, **kw)
        return self.S.add(eng, fn, reads, writes)

    def tt(self, out, in0, in1, op, reads=(), writes=(), eng="vector"):
        return self.S.add(eng, lambda h, o=out, a=in0, b=in1, p=op: h.tensor_tensor(o, a, b, p), reads, writes)

    def ts(self, out, in0, s1, s2, op0, op1=None, reads=(), writes=(), eng="vector"):
        def fn(h, o=out, a=in0, x=s1, y=s2, p0=op0, p1=op1):
            if p1 is None:
                return h.tensor_scalar(o, a, x, None, p0)
            return h.tensor_scalar(o, a, x, y, p0, p1)
        return self.S.add(eng, fn, reads, writes)

    def stt(self, out, in0, scalar, in1, op0, op1, reads=(), writes=()):
        return self.S.add("vector", lambda h, o=out, a=in0, s=scalar, b=in1, p0=op0, p1=op1:
                          h.scalar_tensor_tensor(o, a, s, b, p0, p1), reads, writes)

    def cp(self, out, in_, reads=(), writes=(), eng="vector"):
        if eng == "scalar":
            return self.S.add(eng, lambda h, o=out, i=in_: h.copy(o, i), reads, writes)
        return self.S.add(eng, lambda h, o=out, i=in_: h.tensor_copy(o, i), reads, writes)

    def recip(self, out, in_, reads=(), writes=()):
        return self.S.add("vector", lambda h, o=out, i=in_: h.reciprocal(o, i), reads, writes)

    def recipf(self, out, in_, reads=(), writes=()):
        return self.S.add("vector", lambda h, o=out, i=in_: h.reciprocal_approx_fast(o, i), reads, writes)

    def memset(self, ap, val, writes=(), eng="vector"):
        return self.S.add(eng, lambda h, a=ap, v=val: h.memset(a, v), (), writes)

    def red(self, out, in_, op, reads=(), writes=()):
        return self.S.add("vector", lambda h, o=out, i=in_, p=op: h.tensor_reduce(o, i, AX.X, p), reads, writes)


class Ring:
    def __init__(self, items):
        self.items = items
        self.i = 0

    def next(self):
        x = self.items[self.i]
        self.i = (self.i + 1) % len(self.items)
        return x


def groups_of(tiles, g=4):
    return [tiles[i:i + g] for i in range(0, len(tiles), g)]


def build_program(cfg):
    NT_OWN, NT_HALO, NT_REST, NT_CTX, FH = cfg["NT_OWN"], cfg["NT_HALO"], cfg["NT_REST"], cfg["NT_CTX"], cfg["FH"]
    DEBUG = cfg["DEBUG"]
    NTF = NT_OWN + NT_HALO
    NF0 = NTF + NT_CTX
    NTT = NF0 + NT_REST
    NTOK = NTT * 128
    CTX0 = NTF
    REST0 = NF0
    FHC = FH // 128
    assert FH % 256 == 0 or FH % 128 == 0
    nc = bass.Bass("TRN2", target_bir_lowering=False)
    C = Ctx(nc, cfg)
    S = C.S

    def din(name, shape, dt=F32):
        return nc.dram_tensor(name, list(shape), dt, kind="ExternalInput").ap()

    skind = "ExternalOutput" if DEBUG else "Internal"

    def dscr(name, shape, dt):
        return nc.dram_tensor(name, list(shape), dt, kind=skind).ap()

    xT_in = din("xT", [D, NTOK])
    cT_in = din("cT", [128, KC, 2])
    if cfg["ADA"]:
        ada_w = din("ada_w", [2, D, 6 * D])
        ada_bT = din("ada_bT", [128, 2, 6, KC])
    else:
        modT_in = din("modT_in", [128, 2 * 6 * KC * 2])
    w_in0 = din("w_in0", [D, 10240])
    w_g = din("w_g", [D, 32])
    w2blk = din("w2blk", [33, 2048])
    gla_gain_bc = din("gla_gain_bc", [128, 512])
    sgainT = din("sgainT", [128, 16])
    wsT = din("wsT", [128, 8, 128])
    bsb = din("bsb", [128, 8, 128])
    w_out0 = din("w_out0", [D, D])
    w_in1 = din("w_in1", [D, 6144])
    qg_bc = din("qg_bc", [128, 128])
    kg_bc = din("kg_bc", [128, 128])
    sink_bc = din("sink_bc", [128, 32])
    w_out1 = din("w_out1", [D, D])
    ffn_g = din("ffn_g", [2, D, FH])
    ffn_u = din("ffn_u", [2, D, FH])
    ffn_d = din("ffn_d", [2, FH, D])
    rope_cs = din("rope_cs", [NTF * 128, 2, 256])
    consts = din("consts", [128, 8, 128])
    yT = nc.dram_tensor("yT", [D, NT_OWN * 128], F32, kind="ExternalOutput").ap()

    qT_s = dscr("qT_s", [1024, NTOK], F32)
    kT_s = dscr("kT_s", [1024, NTOK], F32)
    ktm_s = dscr("ktm_s", [NTOK, 1024], F32)
    v_s = dscr("v_s", [NTOK, 2048], BF16)
    sr_s = dscr("sr_s", [NTOK, 2048], F32)
    gT_s = dscr("gT_s", [32, NTOK], F32)
    guT_s = dscr("guT_s", [2048, NTOK], F32)
    vgn_s = dscr("vgn_s", [NTOK, 2048], BF16)
    qeT_s = [dscr("qeT_s%d" % d, [1024, NTOK], BF16) for d in range(2)]
    keT_s = [dscr("keT_s%d" % d, [1024, NTOK], BF16) for d in range(2)]
    kd_s = [dscr("kd_s%d" % d, [NTOK, 1024], BF16) for d in range(2)]
    ebl_s = [dscr("ebl_s%d" % d, [NTT, 128, 8], F32) for d in range(2)]
    o_s = [dscr("o_s%d" % d, [NTOK, 2048], F32) for d in range(2)]
    mixT_s = dscr("mixT_s", [D, NTOK], BF16)
    x1T_s = dscr("x1T_s", [D, NTOK], F32)
    xhT_s = dscr("xhT_s", [D, NTOK], F32)
    x2T_s = dscr("x2T_s", [D, NTOK], F32)
    qT1_s = dscr("qT1_s", [D, NTOK], BF16)
    kT1_s = dscr("kT1_s", [1024, NTOK], BF16)
    v1_s = dscr("v1_s", [NTOK, 1024], BF16)
    x3T_s = dscr("x3T_s", [D, NTOK], F32)

    with ExitStack() as glob:
        modT = C.sb(glob, [128, 2 * 6 * KC * 2], F32, "modT")
        cst = C.sb(glob, [128, 8, 128], F32, "cst")
        ident_bf = C.sb(glob, [128, 128], BF16, "identb")
        ones_f = C.sb(glob, [128, 128], F32, "onesf")
        ones_bf = C.sb(glob, [128, 128], BF16, "onesb")
        epsc = C.sb(glob, [128, 1], F32, "epsc")

        def mod(l, m, c, j):
            off = ((l * 6 + m) * KC + c) * 2 + j
            return modT.t[:, off:off + 1]

        C.dma("sync", cst.t[:], consts, (), [cst.b])
        C.cp(ident_bf.t[:], cst.t[:, 0, :], [cst.b], [ident_bf.b])
        C.memset(ones_f.t[:], 1.0, [ones_f.b])
        C.memset(ones_bf.t[:], 1.0, [ones_bf.b])
        C.memset(epsc.t[:], EPS, [epsc.b])
        S.flush()

        class AdaStream:
            def __init__(self, es, panels):
                self.cf = C.sb(es, [128, KC, 2], F32, "cf")
                self.sT = C.sb(es, [128, KC, 2], BF16, "sT")
                self.abT = C.sb(es, [128, 2, 6, KC], F32, "abT")
                self.wp = Ring([C.sb(es, [128, KC, 512], BF16, "adw") for _ in range(3)])
                pt = C.ps(es, [128, 16], F32, "adp")
                self.pp = Ring([(pt.t, 0, Buf()), (pt.t, 8, Buf())])
                C.dma("sync", self.cf.t[:], cT_in, (), [self.cf.b])
                C.dma("sync", self.abT.t[:], ada_bT, (), [self.abT.b])
                C.act(self.sT.t[:], self.cf.t[:], AF.Silu, [self.cf.b], [self.sT.b])
                self.panels = panels
                self.nl = 0
                self.ncmp = 0
                self.q = []

            def _load(self):
                l, j = self.panels[self.nl]
                self.nl += 1
                w = self.wp.next()
                C.dma("gpsimd", w.t[:], ada_w[l].rearrange("(kc p) n -> p kc n", p=128)[:, :, j * 512:(j + 1) * 512],
                      (), [w.b])
                self.q.append(w)

            def step(self, n=1):
                for _ in range(n):
                    if self.ncmp >= len(self.panels):
                        return
                    while self.nl < min(len(self.panels), self.ncmp + 3):
                        self._load()
                    l, j = self.panels[self.ncmp]
                    self.ncmp += 1
                    w = self.q.pop(0)
                    pt_, po, pb = self.pp.next()
                    m, c0 = j // 8, (j % 8) * 4
                    for i in range(4):
                        for kc in range(KC):
                            C.mm(pt_[:, po + 2 * i:po + 2 * i + 2], w.t[:, kc, i * 128:(i + 1) * 128], self.sT.t[:, kc, :],
                                 kc == 0, kc == KC - 1, [w.b, self.sT.b], [pb])
                    off = ((l * 6 + m) * KC + c0) * 2
                    for jj in range(2):
                        C.tt(modT.t[:, off + jj:off + 8:2], pt_[:, po + jj:po + 8:2], self.abT.t[:, l, m, c0:c0 + 4], ALU.add,
                             [pb, self.abT.b], [modT.b])

            def finish(self, plus_one):
                self.step(len(self.panels))
                for (l, m) in plus_one:
                    off = ((l * 6 + m) * KC) * 2
                    C.ts(modT.t[:, off:off + 2 * KC], modT.t[:, off:off + 2 * KC], 1.0, None, ALU.add,
                         None, [modT.b], [modT.b])

        ada_late = None
        ada_es = ExitStack()
        if not cfg["ADA"]:
            C.dma("sync", modT.t[:], modT_in, (), [modT.b])
            S.flush()
        else:
            with ExitStack() as es:
                if ADA_INTERLEAVE:
                    first = AdaStream(es, [(0, j) for j in range(16)])
                    first.finish([(0, 1)])
                else:
                    first = AdaStream(es, [(l, j) for l in range(2) for j in range(48)])
                    first.finish([(0, 1), (0, 4), (1, 1), (1, 4)])
                S.flush()

        def tile_j(t):
            return 1 if CTX0 <= t < CTX0 + NT_CTX else 0

        def segs(tiles):
            out = []
            for i, t in enumerate(tiles):
                j = tile_j(t)
                if out and out[-1][2] == j:
                    out[-1][1] = (i + 1) * 128
                else:
                    out.append([i * 128, (i + 1) * 128, j])
            return out

        def norm_modulate(es, src, tiles, l, m_shift, m_scale, hT, xring, sqring, ss_ps, rstd, tmpring):
            Tn = len(tiles) * 128
            c0 = tiles[0] * 128
            srcv = src[:, c0:c0 + Tn].rearrange("(c p) t -> c p t", p=128)
            for c in range(KC):
                xc = xring.next()
                C.dma("sync", xc.t[:, :Tn], srcv[c], (), [xc.b])
                sq = sqring.next()
                C.act(sq.t[:, :Tn], xc.t[:, :Tn], AF.Square, [xc.b], [sq.b])
                C.mm(ss_ps.t[:, :Tn], ones_f.t[:], sq.t[:, :Tn], c == 0, c == KC - 1, [sq.b, ones_f.b], [ss_ps.b])
            C.act(rstd.t[:, :Tn], ss_ps.t[:, :Tn], AF.Sqrt, [ss_ps.b, epsc.b], [rstd.b], bias=epsc.t[:, 0:1], scale=1.0 / D)
            C.recip(rstd.t[:, :Tn], rstd.t[:, :Tn], [rstd.b], [rstd.b])
            sg = segs(tiles)
            for c in range(KC):
                xc = xring.next()
                C.dma("sync", xc.t[:, :Tn], srcv[c], (), [xc.b])
                tmp = tmpring.next()
                for (a, b, j) in sg:
                    C.stt(tmp.t[:, a:b], xc.t[:, a:b], mod(l, m_scale, c, j), rstd.t[:, a:b], ALU.mult, ALU.mult,
                          [xc.b, rstd.b, modT.b], [tmp.b])
                    C.act(hT.t[:, c, a:b], tmp.t[:, a:b], AF.Identity, [tmp.b, modT.b], [hT.b],
                          bias=mod(l, m_shift, c, j))

        def wview(w, c0, ncols):
            return w.rearrange("(kc p) n -> p kc n", p=128)[:, :, c0:c0 + ncols]

        def phase_l0a():
            with ExitStack() as es:
                hT = C.sb(es, [128, KC, 512], BF16, "hT")
                xring = Ring([C.sb(es, [128, 512], F32, "xc") for _ in range(4)])
                sqring = Ring([C.sb(es, [128, 512], F32, "sq") for _ in range(2)])
                tmpring = Ring([C.sb(es, [128, 512], F32, "tmp") for _ in range(2)])
                rstd = C.sb(es, [128, 512], F32, "rstd")
                ss_ps = C.ps(es, [128, 512], F32, "ssps")
                wp = Ring([C.sb(es, [128, KC, 512], BF16, "wp") for _ in range(3)])
                wgt = C.sb(es, [128, KC, 32], BF16, "wgt")
                pp = Ring([C.ps(es, [128, 512], F32, "pp") for _ in range(6)])
                st32 = Ring([C.sb(es, [128, 512], F32, "st32") for _ in range(4)])
                st16 = Ring([C.sb(es, [128, 512], BF16, "st16") for _ in range(3)])
                sm = Ring([C.sb(es, [128, 4], F32, "sm") for _ in range(4)])
                C.dma("gpsimd", wgt.t[:], w_g.rearrange("(kc p) n -> p kc n", p=128), (), [wgt.b])

                full_groups = groups_of(list(range(NF0)))
                rest_groups = groups_of(list(range(REST0, NTT)))
                for tiles, full in [(g, True) for g in full_groups] + [(g, False) for g in rest_groups]:
                    Tn = len(tiles) * 128
                    c0 = tiles[0] * 128
                    norm_modulate(es, xT_in, tiles, 0, 0, 1, hT, xring, sqring, ss_ps, rstd, tmpring)
                    plist = list(range(20)) if full else list(range(2, 8))
                    p = pp.next()
                    for kc in range(KC):
                        C.mm(p.t[0:32, :Tn], wgt.t[:, kc, :], hT.t[:, kc, :Tn], kc == 0, kc == KC - 1, [wgt.b, hT.b], [p.b])
                    st = st32.next()
                    C.cp(st.t[0:32, :Tn], p.t[0:32, :Tn], [p.b], [st.b], eng="scalar")
                    C.dma("sync", gT_s[:, c0:c0 + Tn], st.t[0:32, :Tn], [st.b], ())

                    def load(pi):
                        w = wp.next()
                        C.dma("gpsimd", w.t[:], wview(w_in0, pi * 512, 512), (), [w.b])
                        return w
                    q = [load(plist[0]), load(plist[1])]
                    for ii, pi in enumerate(plist):
                        if ii + 2 < len(plist):
                            q.append(load(plist[ii + 2]))
                        w = q.pop(0)
                        fm = (pi < 4 and full) or (12 <= pi < 16)
                        tm = (2 <= pi < 12) or pi >= 16
                        if fm:
                            for jj in range(4):
                                p = pp.next()
                                for kc in range(KC):
                                    C.mm(p.t[:, :Tn], w.t[:, kc, jj * 128:(jj + 1) * 128], hT.t[:, kc, :Tn],
                                         kc == 0, kc == KC - 1, [w.b, hT.b], [p.b])
                                st = st32.next()
                                ch = pi * 4 + jj
                                if pi < 2:
                                    C.cp(st.t[:, :Tn], p.t[:, :Tn], [p.b], [st.b], eng="scalar")
                                    C.dma("sync", qT_s[ch * 128:(ch + 1) * 128, c0:c0 + Tn], st.t[:, :Tn], [st.b], ())
                                elif pi < 4:
                                    C.cp(st.t[:, :Tn], p.t[:, :Tn], [p.b], [st.b], eng="scalar")
                                    C.dma("sync", kT_s[(ch - 8) * 128:(ch - 7) * 128, c0:c0 + Tn], st.t[:, :Tn], [st.b], ())
                                else:
                                    C.act(st.t[:, :Tn], p.t[:, :Tn], AF.Gelu_apprx_tanh, [p.b], [st.b])
                                    C.dma("sync", guT_s[(ch - 48) * 128:(ch - 47) * 128, c0:c0 + Tn], st.t[:, :Tn], [st.b], ())
                        if tm:
                            for i, t in enumerate(tiles):
                                p = pp.next()
                                for kc in range(KC):
                                    C.mm(p.t[:], hT.t[:, kc, i * 128:(i + 1) * 128], w.t[:, kc, :],
                                         kc == 0, kc == KC - 1, [w.b, hT.b], [p.b])
                                r0 = t * 128
                                if pi < 4:
                                    st = st32.next()
                                    C.cp(st.t[:], p.t[:], [p.b], [st.b], eng="vector")
                                    C.dma("sync", ktm_s[r0:r0 + 128, (pi - 2) * 512:(pi - 1) * 512], st.t[:], [st.b], ())
                                elif pi < 8:
                                    st = st16.next()
                                    C.cp(st.t[:], p.t[:], [p.b], [st.b], eng="vector")
                                    C.dma("sync", v_s[r0:r0 + 128, (pi - 4) * 512:(pi - 3) * 512], st.t[:], [st.b], ())
                                elif pi < 12:
                                    st = st32.next()
                                    C.act(st.t[:], p.t[:], AF.Silu, [p.b], [st.b])
                                    C.dma("sync", sr_s[r0:r0 + 128, (pi - 8) * 512:(pi - 7) * 512], st.t[:], [st.b], ())
                                else:
                                    st = st32.next()
                                    C.act(st.t[:], p.t[:], AF.Gelu_apprx_tanh, [p.b], [st.b])
                                    s2 = st32.next()
                                    C.tt(s2.t[:], st.t[:], st.t[:], ALU.mult, [st.b], [s2.b])
                                    ssm = sm.next()
                                    C.red(ssm.t[:, 0:2], s2.t[:].rearrange("p (g c) -> p g c", g=2), ALU.add, [s2.b], [ssm.b])
                                    C.act(ssm.t[:, 0:2], ssm.t[:, 0:2], AF.Sqrt, [ssm.b, epsc.b], [ssm.b],
                                          bias=epsc.t[:, 0:1], scale=1.0 / 256)
                                    C.recip(ssm.t[:, 0:2], ssm.t[:, 0:2], [ssm.b], [ssm.b])
                                    so = st16.next()
                                    for gg in range(2):
                                        C.ts(so.t[:, gg * 256:(gg + 1) * 256], st.t[:, gg * 256:(gg + 1) * 256],
                                             ssm.t[:, gg:gg + 1], None, ALU.mult, None, [st.b, ssm.b], [so.b])
                                    C.dma("sync", vgn_s[r0:r0 + 128, (pi - 16) * 512:(pi - 15) * 512], so.t[:], [so.b], ())
                S.flush()

        def phase_l0b():
            with ExitStack() as es:
                w2 = C.sb(es, [33, 2048], F32, "w2")
                C.dma("sync", w2.t[:], w2blk, (), [w2.b])
                gaug = Ring([C.sb(es, [33, 128], F32, "gaug") for _ in range(2)])
                for g in gaug.items:
                    C.memset(g.t[:], 1.0, [g.b])
                zps = Ring([C.ps(es, [128, 512], F32, "zps") for _ in range(4)])
                cps = Ring([C.ps(es, [128, 512], F32, "cps") for _ in range(3)])
                esb = Ring([C.sb(es, [128, 2048], F32, "esb") for _ in range(2)])
                spb = Ring([C.sb(es, [128, 2048], F32, "spb") for _ in range(2)])
                Ep = Ring([C.sb(es, [128, 1024], F32, "Ep") for _ in range(2)])
                Em = Ring([C.sb(es, [128, 1024], F32, "Em") for _ in range(2)])
                Dd = Ring([C.sb(es, [128, 1024], F32, "Dd") for _ in range(2)])
                qin = Ring([C.sb(es, [128, 8, 128], F32, "qin") for _ in range(2)])
                kin = Ring([C.sb(es, [128, 8, 128], F32, "kin") for _ in range(2)])
                ktin = Ring([C.sb(es, [128, 1024], F32, "ktin") for _ in range(2)])
                o16 = Ring([C.sb(es, [128, 1024], BF16, "o16") for _ in range(4)])
                eb = Ring([C.sb(es, [128, 8], F32, "eb") for _ in range(4)])
                for t in range(NTT):
                    if ada_late is not None:
                        ada_late.step(2)
                    full = t < NF0
                    c0 = t * 128
                    ga = gaug.next()
                    C.dma("sync", ga.t[0:32, :], gT_s[:, c0:c0 + 128], (), [ga.b])
                    e_ = esb.next()
                    sp = spb.next()
                    dirs = (0, 1) if full else (1,)
                    for d in dirs:
                        for hh in range(2):
                            z = zps.next()
                            cc = d * 1024 + hh * 512
                            C.mm(z.t[:], ga.t[:], w2.t[:, cc:cc + 512], True, True, [ga.b, w2.b], [z.b])
                            C.act(e_.t[:, cc:cc + 512], z.t[:], AF.Exp, [z.b], [e_.b], scale=-1.0)
                        C.act(sp.t[:, d * 1024:(d + 1) * 1024], e_.t[:, d * 1024:(d + 1) * 1024], AF.Ln, [e_.b], [sp.b],
                              bias=1.0)
                    if full:
                        qi = qin.next()
                        ki = kin.next()
                        C.dma("sync", qi.t[:], qT_s[:, c0:c0 + 128].rearrange("(c p) t -> p c t", p=128), (), [qi.b])
                        C.dma("sync", ki.t[:], kT_s[:, c0:c0 + 128].rearrange("(c p) t -> p c t", p=128), (), [ki.b])
                    kt = ktin.next()
                    C.dma("sync", kt.t[:], ktm_s[c0:c0 + 128, :], (), [kt.b])
                    for d in dirs:
                        ep = Ep.next()
                        em = Em.next()
                        for hh in range(2):
                            cp_ = cps.next()
                            for k4 in range(4):
                                kc = hh * 4 + k4
                                C.mm(cp_.t[:, k4 * 128:(k4 + 1) * 128], sp.t[:, d * 1024 + kc * 128:d * 1024 + (kc + 1) * 128],
                                     cst.t[:, 1 + d, :], True, True, [sp.b, cst.b], [cp_.b])
                            C.act(ep.t[:, hh * 512:(hh + 1) * 512], cp_.t[:], AF.Exp, [cp_.b], [ep.b], scale=-1.0 / 16)
                            if full:
                                C.act(em.t[:, hh * 512:(hh + 1) * 512], cp_.t[:], AF.Exp, [cp_.b], [em.b], scale=1.0 / 16)
                        ebt = eb.next()
                        col = 127 if d == 0 else 0
                        C.cp(ebt.t[:], ep.t[:].rearrange("p (c t) -> p c t", t=128)[:, :, col], [ep.b], [ebt.b], eng="vector")
                        C.dma("gpsimd", ebl_s[d][t], ebt.t[:], [ebt.b], ())
                        if full:
                            oq = o16.next()
                            C.stt(oq.t[:], qi.t[:].rearrange("p c t -> p (c t)"), 0.0625, ep.t[:], ALU.mult, ALU.mult,
                                  [qi.b, ep.b], [oq.b])
                            C.dma("gpsimd", qeT_s[d][:, c0:c0 + 128].rearrange("(c p) t -> p c t", p=128),
                                  oq.t[:].rearrange("p (c t) -> p c t", t=128), [oq.b], ())
                            ok = o16.next()
                            C.tt(ok.t[:], ki.t[:].rearrange("p c t -> p (c t)"), em.t[:], ALU.mult, [ki.b, em.b], [ok.b])
                            C.dma("gpsimd", keT_s[d][:, c0:c0 + 128].rearrange("(c p) t -> p c t", p=128),
                                  ok.t[:].rearrange("p (c t) -> p c t", t=128), [ok.b], ())
                        dd = Dd.next()
                        for hh in range(2):
                            cp_ = cps.next()
                            cc = d * 1024 + hh * 512
                            C.mm(cp_.t[:], cst.t[:, 3 + d, :], sp.t[:, cc:cc + 512], True, True, [sp.b, cst.b], [cp_.b])
                            C.act(dd.t[:, hh * 512:(hh + 1) * 512], cp_.t[:], AF.Exp, [cp_.b], [dd.b], scale=-1.0 / 16)
                        okd = o16.next()
                        C.tt(okd.t[:], kt.t[:], dd.t[:], ALU.mult, [kt.b, dd.b], [okd.b])
                        C.dma("gpsimd", kd_s[d][c0:c0 + 128, :], okd.t[:], [okd.b], ())
                S.flush()

        def phase_l0c():
            with ExitStack() as es:
                S32 = [[C.sb(es, [128, 2, 512], F32, "S32") for h in range(4)] for d in range(2)]
                Sbf = [[C.sb(es, [128, 2, 512], BF16, "Sbf") for h in range(4)] for d in range(2)]
                for d in range(2):
                    for h in range(4):
                        C.memset(S32[d][h].t[:], 0.0, [S32[d][h].b])
                        C.memset(Sbf[d][h].t[:], 0.0, [Sbf[d][h].b])
                mask = [C.sb(es, [128, 128], F32, "mask") for d in range(2)]
                C.cp(mask[0].t[:], cst.t[:, 1, :], [cst.b], [mask[0].b])
                C.cp(mask[1].t[:], cst.t[:, 3, :], [cst.b], [mask[1].b])
                qe = Ring([C.sb(es, [128, 8, 128], BF16, "qe") for _ in range(3)])
                ke = Ring([C.sb(es, [128, 8, 128], BF16, "ke") for _ in range(3)])
                kd = Ring([C.sb(es, [128, 1024], BF16, "kd") for _ in range(3)])
                vv = Ring([C.sb(es, [128, 2048], BF16, "vv") for _ in range(3)])
                ebr = Ring([C.sb(es, [128, 8], F32, "ebr") for _ in range(3)])
                attp = Ring([C.ps(es, [128, 128], F32, "attp") for _ in range(2)])
                op_ = Ring([C.ps(es, [128, 512], F32, "op") for _ in range(2)])
                sup = Ring([C.ps(es, [128, 512], F32, "sup") for _ in range(3)])
                atts = Ring([C.sb(es, [128, 128], BF16, "atts") for _ in range(3)])
                ost = Ring([C.sb(es, [128, 512], F32, "ost") for _ in range(4)])

                def step(d, t, with_out):
                    c0 = t * 128
                    if with_out:
                        q_ = qe.next()
                        k_ = ke.next()
                        C.dma("sync", q_.t[:], qeT_s[d][:, c0:c0 + 128].rearrange("(c p) t -> p c t", p=128), (), [q_.b])
                        C.dma("sync", k_.t[:], keT_s[d][:, c0:c0 + 128].rearrange("(c p) t -> p c t", p=128), (), [k_.b])
                    kd_ = kd.next()
                    v_ = vv.next()
                    eb_ = ebr.next()
                    C.dma("sync", kd_.t[:], kd_s[d][c0:c0 + 128, :], (), [kd_.b])
                    C.dma("sync", v_.t[:], v_s[c0:c0 + 128, :], (), [v_.b])
                    C.dma("sync", eb_.t[:], ebl_s[d][t], (), [eb_.b])
                    for h in range(4):
                        if with_out:
                            ap_ = attp.next()
                            for c in range(2):
                                C.mm(ap_.t[:], k_.t[:, 2 * h + c, :], q_.t[:, 2 * h + c, :], c == 0, c == 1, [k_.b, q_.b], [ap_.b])
                            as_ = atts.next()
                            C.tt(as_.t[:], ap_.t[:], mask[d].t[:], ALU.mult, [ap_.b, mask[d].b], [as_.b])
                            o_ = op_.next()
                            for c in range(2):
                                C.mm(o_.t[:], q_.t[:, 2 * h + c, :], Sbf[d][h].t[:, c, :], c == 0, False,
                                     [q_.b, Sbf[d][h].b], [o_.b])
                            C.mm(o_.t[:], as_.t[:], v_.t[:, h * 512:(h + 1) * 512], False, True, [as_.b, v_.b], [o_.b])
                            os_ = ost.next()
                            C.cp(os_.t[:], o_.t[:], [o_.b], [os_.b], eng="scalar")
                            C.dma("gpsimd", o_s[d][c0:c0 + 128, h * 512:(h + 1) * 512], os_.t[:], [os_.b], ())
                        for c in range(2):
                            su = sup.next()
                            C.mm(su.t[:], kd_.t[:, (2 * h + c) * 128:(2 * h + c + 1) * 128], v_.t[:, h * 512:(h + 1) * 512],
                                 True, True, [kd_.b, v_.b], [su.b])
                            C.stt(S32[d][h].t[:, c, :], S32[d][h].t[:, c, :], eb_.t[:, 2 * h + c:2 * h + c + 1], su.t[:],
                                  ALU.mult, ALU.add, [S32[d][h].b, eb_.b, su.b], [S32[d][h].b])
                            C.cp(Sbf[d][h].t[:, c, :], S32[d][h].t[:, c, :], [S32[d][h].b], [Sbf[d][h].b], eng="scalar")

                ctx_t = list(range(CTX0, CTX0 + NT_CTX))
                chainA = [(0, t, True) for t in ctx_t] + [(0, t, True) for t in range(NTF)]
                chainB = ([(1, t, True) for t in reversed(ctx_t)] + [(1, t, False) for t in reversed(range(REST0, NTT))]
                          + [(1, t, True) for t in reversed(range(NTF))])
                for i in range(max(len(chainA), len(chainB))):
                    if ada_late is not None:
                        ada_late.step(2)
                    if i < len(chainB):
                        step(*chainB[i])
                    if i < len(chainA):
                        step(*chainA[i])
                S.flush()

        def phase_l0d():
            with ExitStack() as es:
                gg = C.sb(es, [128, 512], F32, "gg")
                sgn = C.sb(es, [128, 16], F32, "sgn")
                ws = C.sb(es, [128, 8, 128], F32, "wsf")
                wsb = C.sb(es, [128, 8, 128], BF16, "wsb")
                bs = C.sb(es, [128, 8, 128], F32, "bsb")
                C.dma("sync", gg.t[:], gla_gain_bc, (), [gg.b])
                C.dma("sync", sgn.t[:], sgainT, (), [sgn.b])
                C.dma("sync", ws.t[:], wsT, (), [ws.b])
                C.dma("sync", bs.t[:], bsb, (), [bs.b])
                C.cp(wsb.t[:], ws.t[:], [ws.b], [wsb.b])
                oa = Ring([C.sb(es, [128, 2048], F32, "oa") for _ in range(2)])
                ob = Ring([C.sb(es, [128, 2048], F32, "ob") for _ in range(2)])
                sr = Ring([C.sb(es, [128, 2048], F32, "sr") for _ in range(2)])
                sqj = C.sb(es, [128, 512], F32, "sqj")
                vg = Ring([C.sb(es, [128, 2048], BF16, "vg") for _ in range(2)])
                gu = Ring([C.sb(es, [128, 16, 128], F32, "gu") for _ in range(2)])
                mtm = Ring([C.sb(es, [128, 2048], BF16, "mtm") for _ in range(2)])
                mst = Ring([C.sb(es, [128, 32, 128], BF16, "mst") for _ in range(2)])
                ssm = Ring([C.sb(es, [128, 4], F32, "ssm") for _ in range(2)])
                tps = Ring([C.ps(es, [128, 4, 128], BF16, "tps") for _ in range(2)])
                sps = Ring([C.ps(es, [128, 128], F32, "sps") for _ in range(4)])
                t32 = Ring([C.sb(es, [128, 128], F32, "t32") for _ in range(3)])
                for t in range(NF0):
                    if ada_late is not None:
                        ada_late.step(2)
                    c0 = t * 128
                    a_, b_, r_, v_, u_ = oa.next(), ob.next(), sr.next(), vg.next(), gu.next()
                    C.dma("sync", a_.t[:], o_s[0][c0:c0 + 128, :], (), [a_.b])
                    C.dma("sync", b_.t[:], o_s[1][c0:c0 + 128, :], (), [b_.b])
                    C.dma("sync", r_.t[:], sr_s[c0:c0 + 128, :], (), [r_.b])
                    C.dma("sync", v_.t[:], vgn_s[c0:c0 + 128, :], (), [v_.b])
                    C.dma("sync", u_.t[:], guT_s[:, c0:c0 + 128].rearrange("(c p) t -> p c t", p=128), (), [u_.b])
                    C.tt(a_.t[:], a_.t[:], b_.t[:], ALU.add, [a_.b, b_.b], [a_.b])
                    s_ = ssm.next()
                    for h in range(4):
                        C.act(sqj.t[:], a_.t[:, h * 512:(h + 1) * 512], AF.Square, [a_.b], [sqj.b, s_.b], accum=s_.t[:, h:h + 1])
                    C.act(s_.t[:], s_.t[:], AF.Sqrt, [s_.b, epsc.b], [s_.b], bias=epsc.t[:, 0:1], scale=1.0 / 512)
                    C.recip(s_.t[:], s_.t[:], [s_.b], [s_.b])
                    m_ = mtm.next()
                    for h in range(4):
                        sl = slice(h * 512, (h + 1) * 512)
                        C.stt(a_.t[:, sl], a_.t[:, sl], s_.t[:, h:h + 1], gg.t[:], ALU.mult, ALU.mult, [a_.b, s_.b, gg.b], [a_.b])
                    C.tt(m_.t[:], a_.t[:], r_.t[:], ALU.mult, [a_.b, r_.b], [m_.b])
                    ms = mst.next()
                    for q4 in range(4):
                        tp = tps.next()
                        for i in range(4):
                            ch = q4 * 4 + i
                            C.tr(tp.t[:, i, :], m_.t[:, ch * 128:(ch + 1) * 128], ident_bf.t[:], [m_.b, ident_bf.b], [tp.b])
                        C.cp(ms.t[:, q4 * 4:(q4 + 1) * 4, :], tp.t[:], [tp.b], [ms.b], eng="scalar")
                    for c in range(16):
                        g = c // 2
                        sp_ = sps.next()
                        C.mm(sp_.t[:], v_.t[:, c * 128:(c + 1) * 128], wsb.t[:, g, :], True, True, [v_.b, wsb.b], [sp_.b])
                        tt_ = t32.next()
                        C.stt(tt_.t[:], sp_.t[:], sgn.t[:, c:c + 1], bs.t[:, g, :], ALU.mult, ALU.add, [sp_.b, sgn.b, bs.b], [tt_.b])
                        C.tt(ms.t[:, 16 + c, :], tt_.t[:], u_.t[:, c, :], ALU.mult, [tt_.b, u_.b], [ms.b])
                    C.dma("gpsimd", mixT_s[:, c0:c0 + 128].rearrange("(c p) t -> p c t", p=128), ms.t[:], [ms.b], ())
                S.flush()

        def phase_outproj(w_out, l, tiles_all, src_x, dst_x):
            with ExitStack() as es:
                mT = Ring([C.sb(es, [128, KC, 512], BF16, "mT") for _ in range(2)])
                wp = Ring([C.sb(es, [128, KC, 512], BF16, "wp") for _ in range(3)])
                pp = Ring([C.ps(es, [128, 512], F32, "pp") for _ in range(4)])
                xr = Ring([C.sb(es, [128, 512], F32, "xr") for _ in range(4)])
                xo = Ring([C.sb(es, [128, 512], F32, "xo") for _ in range(4)])
                for tiles in groups_of(tiles_all):
                    Tn = len(tiles) * 128
                    c0 = tiles[0] * 128
                    sg = segs(tiles)
                    m_ = mT.next()
                    C.dma("sync", m_.t[:, :, :Tn], mixT_s[:, c0:c0 + Tn].rearrange("(c p) t -> p c t", p=128), (), [m_.b])

                    def load(pi):
                        w = wp.next()
                        C.dma("gpsimd", w.t[:], wview(w_out, pi * 512, 512), (), [w.b])
                        return w
                    q = [load(0), load(1)]
                    for pi in range(8):
                        if pi + 2 < 8:
                            q.append(load(pi + 2))
                        w = q.pop(0)
                        for jj in range(4):
                            ch = pi * 4 + jj
                            p = pp.next()
                            for kc in range(KC):
                                C.mm(p.t[:, :Tn], w.t[:, kc, jj * 128:(jj + 1) * 128], m_.t[:, kc, :Tn],
                                     kc == 0, kc == KC - 1, [w.b, m_.b], [p.b])
                            x_ = xr.next()
                            C.dma("sync", x_.t[:, :Tn], src_x[ch * 128:(ch + 1) * 128, c0:c0 + Tn], (), [x_.b])
                            o_ = xo.next()
                            for (a, b, j) in sg:
                                C.stt(o_.t[:, a:b], p.t[:, a:b], mod(l, 2, ch, j), x_.t[:, a:b], ALU.mult, ALU.add,
                                      [p.b, x_.b, modT.b], [o_.b])
                            C.dma("sync", dst_x[ch * 128:(ch + 1) * 128, c0:c0 + Tn], o_.t[:, :Tn], [o_.b], ())
                S.flush()

        def phase_ffn(l, tiles_all, src_x, mid_x, dst_x, dst_col_off=0):
            halves = [(0, FHC // 2), (FHC // 2, FHC)] if FHC >= 2 else [(0, FHC)]
            HC = max(b - a for a, b in halves)
            with ExitStack() as es:
                hT = C.sb(es, [128, KC, 512], BF16, "hT")
                AT = C.sb(es, [128, HC, 512], BF16, "AT")
                xring = Ring([C.sb(es, [128, 512], F32, "xc") for _ in range(4)])
                sqring = Ring([C.sb(es, [128, 512], F32, "sq") for _ in range(2)])
                tmpring = Ring([C.sb(es, [128, 512], F32, "tmp") for _ in range(2)])
                rstd = C.sb(es, [128, 512], F32, "rstd")
                ss_ps = C.ps(es, [128, 512], F32, "ssps")
                slots = Ring([C.sb(es, [128, 16384], BF16, "slot") for _ in range(3)])
                gp = Ring([C.ps(es, [128, 512], F32, "gp") for _ in range(2)])
                up = Ring([C.ps(es, [128, 512], F32, "up") for _ in range(2)])
                dp = Ring([C.ps(es, [128, 512], F32, "dp") for _ in range(2)])
                sgr = Ring([C.sb(es, [128, 512], F32, "sgr") for _ in range(2)])
                xo = Ring([C.sb(es, [128, 512], F32, "xo") for _ in range(3)])
                for tiles in groups_of(tiles_all):
                    Tn = len(tiles) * 128
                    c0 = tiles[0] * 128
                    sg = segs(tiles)
                    norm_modulate(es, src_x, tiles, l, 3, 4, hT, xring, sqring, ss_ps, rstd, tmpring)
                    for hi, (ha, hb) in enumerate(halves):
                        nh = hb - ha
                        pans = [(c, min(2, hb - c)) for c in range(ha, hb, 2)]

                        def load_gu(pn):
                            cc, n = pn
                            sl = slots.next()
                            gv = sl.t[:, 0:KC * 256].rearrange("p (k n) -> p k n", n=256)
                            uv = sl.t[:, KC * 256:2 * KC * 256].rearrange("p (k n) -> p k n", n=256)
                            C.dma("gpsimd", gv[:, :, :n * 128], wview(ffn_g[l], cc * 128, n * 128), (), [sl.b])
                            C.dma("gpsimd", uv[:, :, :n * 128], wview(ffn_u[l], cc * 128, n * 128), (), [sl.b])
                            return (sl, gv, uv)
                        q = [load_gu(pans[0])] + ([load_gu(pans[1])] if len(pans) > 1 else [])
                        for ii, (cc, n) in enumerate(pans):
                            if ii + 2 < len(pans):
                                q.append(load_gu(pans[ii + 2]))
                            sl, gv, uv = q.pop(0)
                            for jj in range(n):
                                g_ = gp.next()
                                u_ = up.next()
                                for kc in range(KC):
                                    C.mm(g_.t[:, :Tn], gv[:, kc, jj * 128:(jj + 1) * 128], hT.t[:, kc, :Tn],
                                         kc == 0, kc == KC - 1, [sl.b, hT.b], [g_.b])
                                for kc in range(KC):
                                    C.mm(u_.t[:, :Tn], uv[:, kc, jj * 128:(jj + 1) * 128], hT.t[:, kc, :Tn],
                                         kc == 0, kc == KC - 1, [sl.b, hT.b], [u_.b])
                                s_ = sgr.next()
                                C.act(s_.t[:, :Tn], g_.t[:, :Tn], AF.Silu, [g_.b], [s_.b])
                                C.tt(AT.t[:, cc - ha + jj, :Tn], s_.t[:, :Tn], u_.t[:, :Tn], ALU.mult, [s_.b, u_.b], [AT.b])
                        srcx = src_x if hi == 0 else mid_x
                        last = hi == len(halves) - 1
                        dstx = dst_x if last else mid_x

                        def load_d(j):
                            sl = slots.next()
                            dv = sl.t[:, 0:nh * 128].rearrange("p (k n) -> p k n", n=128)
                            C.dma("gpsimd", dv, ffn_d[l][ha * 128:hb * 128, j * 128:(j + 1) * 128]
                                  .rearrange("(kc p) n -> p kc n", p=128), (), [sl.b])
                            return (sl, dv)
                        q = [load_d(0), load_d(1)]
                        for j in range(KC):
                            if j + 2 < KC:
                                q.append(load_d(j + 2))
                            sl, dv = q.pop(0)
                            p = dp.next()
                            for kc in range(nh):
                                C.mm(p.t[:, :Tn], dv[:, kc, :], AT.t[:, kc, :Tn], kc == 0, kc == nh - 1, [sl.b, AT.b], [p.b])
                            x_ = xring.next()
                            C.dma("sync", x_.t[:, :Tn], srcx[j * 128:(j + 1) * 128, c0:c0 + Tn], (), [x_.b])
                            o_ = xo.next()
                            for (a, b, jx) in sg:
                                C.stt(o_.t[:, a:b], p.t[:, a:b], mod(l, 5, j, jx), x_.t[:, a:b], ALU.mult, ALU.add,
                                      [p.b, x_.b, modT.b], [o_.b])
                            if last:
                                C.dma("sync", dstx[j * 128:(j + 1) * 128, c0 + dst_col_off:c0 + dst_col_off + Tn],
                                      o_.t[:, :Tn], [o_.b], ())
                            else:
                                C.dma("sync", dstx[j * 128:(j + 1) * 128, c0:c0 + Tn], o_.t[:, :Tn], [o_.b], ())
                S.flush()

        def phase_l1a():
            with ExitStack() as es:
                hT = C.sb(es, [128, KC, 512], BF16, "hT")
                xring = Ring([C.sb(es, [128, 512], F32, "xc") for _ in range(4)])
                sqring = Ring([C.sb(es, [128, 512], F32, "sq") for _ in range(2)])
                tmpring = Ring([C.sb(es, [128, 512], F32, "tmp") for _ in range(2)])
                rstd = C.sb(es, [128, 512], F32, "rstd")
                ss_ps = C.ps(es, [128, 512], F32, "ssps")
                wp = Ring([C.sb(es, [128, KC, 512], BF16, "wp") for _ in range(3)])
                pp = Ring([C.ps(es, [128, 512], F32, "pp") for _ in range(4)])
                tps = Ring([C.ps(es, [128, 4, 128], BF16, "tps") for _ in range(2)])
                gq = C.sb(es, [128, 128], F32, "gq")
                gk = C.sb(es, [128, 128], F32, "gk")
                C.dma("sync", gq.t[:], qg_bc, (), [gq.b])
                C.dma("sync", gk.t[:], kg_bc, (), [gk.b])
                grep_ = {}
                for nm, g_ in (("q", gq), ("k", gk)):
                    gf = C.sb(es, [128, 512], F32, "gfr")
                    for h in range(4):
                        C.cp(gf.t[:, h * 128:(h + 1) * 128], g_.t[:], [g_.b], [gf.b])
                    grep_[nm] = (None, None, gf)
                rope = [C.sb(es, [128, 2, 256], F32, "rope") for _ in range(4)]
                sq32 = Ring([C.sb(es, [128, 512], F32, "sq32") for _ in range(2)])
                qn = Ring([C.sb(es, [128, 512], F32, "qn") for _ in range(3)])
                r1 = Ring([C.sb(es, [128, 256], F32, "r1") for _ in range(4)])
                qr = Ring([C.sb(es, [128, 512], BF16, "qr") for _ in range(4)])
                sm = Ring([C.sb(es, [128, 4], F32, "sm") for _ in range(4)])
                st16 = Ring([C.sb(es, [128, 4, 128], BF16, "st16") for _ in range(3)])
                sv16 = Ring([C.sb(es, [128, 512], BF16, "sv16") for _ in range(3)])
                fifo = []

                def stage2(o_, pi, r0):
                    tp = tps.next()
                    for h in range(4):
                        C.tr(tp.t[:, h, :], o_.t[:, h * 128:(h + 1) * 128], ident_bf.t[:], [o_.b, ident_bf.b], [tp.b])
                    so = st16.next()
                    C.cp(so.t[:], tp.t[:], [tp.b], [so.b], eng="scalar")
                    if pi < 8:
                        dst = qT1_s[pi * 512:(pi + 1) * 512, r0:r0 + 128]
                    else:
                        dst = kT1_s[(pi - 8) * 512:(pi - 7) * 512, r0:r0 + 128]
                    C.dma("sync", dst.rearrange("(h d) t -> d h t", d=128), so.t[:], [so.b], ())

                for tiles in groups_of(list(range(NF0))):
                    Tn = len(tiles) * 128
                    norm_modulate(es, x2T_s, tiles, 1, 0, 1, hT, xring, sqring, ss_ps, rstd, tmpring)
                    for i, t in enumerate(tiles):
                        if t < NTF:
                            C.dma("sync", rope[i].t[:], rope_cs[t * 128:(t + 1) * 128], (), [rope[i].b])

                    def load(pi):
                        w = wp.next()
                        C.dma("gpsimd", w.t[:], wview(w_in1, pi * 512, 512), (), [w.b])
                        return w
                    q = [load(0), load(1)]
                    for pi in range(12):
                        if pi + 2 < 12:
                            q.append(load(pi + 2))
                        w = q.pop(0)
                        for i, t in enumerate(tiles):
                            if pi < 8 and t >= NT_OWN:
                                continue
                            p = pp.next()
                            for kc in range(KC):
                                C.mm(p.t[:], hT.t[:, kc, i * 128:(i + 1) * 128], w.t[:, kc, :], kc == 0, kc == KC - 1,
                                     [w.b, hT.b], [p.b])
                            r0 = t * 128
                            if pi >= 10:
                                sv = sv16.next()
                                C.cp(sv.t[:], p.t[:], [p.b], [sv.b], eng="scalar")
                                C.dma("sync", v1_s[r0:r0 + 128, (pi - 10) * 512:(pi - 9) * 512], sv.t[:], [sv.b], ())
                                continue
                            kk = 0 if pi < 8 else 1
                            s2 = sq32.next()
                            C.act(s2.t[:], p.t[:], AF.Square, [p.b], [s2.b])
                            ssm = sm.next()
                            C.red(ssm.t[:], s2.t[:].rearrange("p (h d) -> p h d", h=4), ALU.add, [s2.b], [ssm.b])
                            C.act(ssm.t[:], ssm.t[:], AF.Sqrt, [ssm.b, epsc.b], [ssm.b], bias=epsc.t[:, 0:1], scale=1.0 / 128)
                            C.recip(ssm.t[:], ssm.t[:], [ssm.b], [ssm.b])
                            n_ = qn.next()
                            for h in range(4):
                                sl = slice(h * 128, (h + 1) * 128)
                                C.act(n_.t[:, sl], p.t[:, sl], AF.Copy, [p.b, ssm.b], [n_.b], scale=ssm.t[:, h:h + 1])
                            o_ = qr.next()
                            gf = grep_["q" if pi < 8 else "k"][2]
                            if t < NTF:
                                rp = rope[i]
                                C.tt(n_.t[:], n_.t[:], gf.t[:], ALU.mult, [n_.b, gf.b], [n_.b])
                                nv = n_.t[:].rearrange("p (h d) -> p h d", h=4)
                                ov = o_.t[:].rearrange("p (h d) -> p h d", h=4)
                                cosv = rp.t[:, 0, :].rearrange("p (h d) -> p h d", h=4)
                                sinv = rp.t[:, 1, :].rearrange("p (h d) -> p h d", h=4)
                                v4 = lambda x: x.t[:].rearrange("p (h d) -> p h d", h=4)
                                a1, a2, a3, a4 = r1.next(), r1.next(), r1.next(), r1.next()
                                C.tt(v4(a1), nv[:, :, 0:64], cosv, ALU.mult, [n_.b, rp.b], [a1.b])
                                C.tt(v4(a2), nv[:, :, 64:128], sinv, ALU.mult, [n_.b, rp.b], [a2.b])
                                C.tt(ov[:, :, 0:64], v4(a1), v4(a2), ALU.subtract, [a1.b, a2.b], [o_.b])
                                C.tt(v4(a3), nv[:, :, 0:64], sinv, ALU.mult, [n_.b, rp.b], [a3.b])
                                C.tt(v4(a4), nv[:, :, 64:128], cosv, ALU.mult, [n_.b, rp.b], [a4.b])
                                C.tt(ov[:, :, 64:128], v4(a3), v4(a4), ALU.add, [a3.b, a4.b], [o_.b])
                            else:
                                C.tt(o_.t[:], n_.t[:], gf.t[:], ALU.mult, [n_.b, gf.b], [o_.b])
                            fifo.append((o_, pi, r0))
                            if len(fifo) > 2:
                                stage2(*fifo.pop(0))
                    while fifo:
                        stage2(*fifo.pop(0))
                S.flush()

        def phase_l1b():
            with ExitStack() as es:
                kT = C.sb(es, [128, NF0, 8, 128], BF16, "kTall")
                vA = C.sb(es, [128, NF0, 1024], BF16, "vall")
                kb_ = [Buf() for _ in range(NF0)]
                vb_ = [Buf() for _ in range(NF0)]
                for t in range(NF0):
                    C.dma("sync", kT.t[:, t], kT1_s[:, t * 128:(t + 1) * 128].rearrange("(h d) t -> d h t", d=128), (), [kb_[t]])
                    C.dma("sync", vA.t[:, t], v1_s[t * 128:(t + 1) * 128, :], (), [vb_[t]])
                sk = C.sb(es, [128, 32], F32, "sk")
                esk = C.sb(es, [128, 32], F32, "esk")
                negc = C.sb(es, [128, 1], F32, "negc")
                C.memset(negc.t[:], -SOFT_C, [negc.b])
                C.dma("sync", sk.t[:], sink_bc, (), [sk.b])
                C.act(esk.t[:], sk.t[:], AF.Exp, [sk.b, negc.b], [esk.b], bias=negc.t[:, 0:1])
                mk = [C.sb(es, [128, 4, 128], BF16, "mk") for _ in range(2)]
                for h in range(4):
                    C.cp(mk[0].t[:, h, :], cst.t[:, 2, :], [cst.b], [mk[0].b])
                    C.cp(mk[1].t[:, h, :], cst.t[:, 1, :], [cst.b], [mk[1].b])
                qT = Ring([C.sb(es, [128, 32, 128], BF16, "qT") for _ in range(2)])
                sps = Ring([C.ps(es, [128, 512], F32, "sps") for _ in range(4)])
                dps = Ring([C.ps(es, [128, 512], F32, "dps") for _ in range(2)])
                ops_ = Ring([C.ps(es, [128, 512], F32, "ops") for _ in range(2)])
                PT = Ring([C.sb(es, [128, 512], BF16, "PT") for _ in range(4)])
                den = Ring([C.sb(es, [128, 512], F32, "den") for _ in range(2)])
                ast = Ring([C.sb(es, [128, 32, 128], BF16, "ast") for _ in range(2)])
                sc = 128.0 ** -0.5
                LOOK = 3
                for n in range(NT_OWN):
                    q_ = qT.next()
                    C.dma("sync", q_.t[:], qT1_s[:, n * 128:(n + 1) * 128].rearrange("(h d) t -> d h t", d=128), (), [q_.b])
                    blocks = []
                    if n > 0:
                        blocks.append((n - 1, 0))
                    blocks.append((n, None))
                    blocks.append((n + 1, 1))
                    for t in range(CTX0, CTX0 + NT_CTX):
                        blocks.append((t, None))
                    nb = len(blocks)
                    a_ = ast.next()
                    pairs = [(g, bi) for g in range(8) for bi in range(nb)]

                    def score(idx):
                        g, bi = pairs[idx]
                        kt = blocks[bi][0]
                        s_ = sps.next()
                        C.mm(s_.t[:], kT.t[:, kt, g, :], q_.t[:, 4 * g:4 * g + 4, :], True, True, [kb_[kt], q_.b], [s_.b])
                        return s_
                    sq_ = [score(i) for i in range(min(LOOK, len(pairs)))]
                    d_ = o_ = None
                    for idx, (g, bi) in enumerate(pairs):
                        if idx + LOOK < len(pairs):
                            sq_.append(score(idx + LOOK))
                        s_ = sq_.pop(0)
                        kt, mi = blocks[bi]
                        if bi == 0:
                            d_ = dps.next()
                            o_ = ops_.next()
                        p_ = PT.next()
                        C.act(p_.t[:], s_.t[:], AF.Exp, [s_.b, negc.b], [p_.b], bias=negc.t[:, 0:1], scale=sc)
                        if mi is not None:
                            C.tt(p_.t[:], p_.t[:], mk[mi].t[:].rearrange("p h t -> p (h t)"), ALU.mult,
                                 [p_.b, mk[mi].b], [p_.b])
                        C.mm(d_.t[:], ones_bf.t[:], p_.t[:], bi == 0, bi == nb - 1, [p_.b, ones_bf.b], [d_.b])
                        C.mm(o_.t[:], vA.t[:, kt, g * 128:(g + 1) * 128], p_.t[:], bi == 0, bi == nb - 1,
                             [p_.b, vb_[kt]], [o_.b])
                        if bi == nb - 1:
                            dn = den.next()
                            for h in range(4):
                                sl = slice(h * 128, (h + 1) * 128)
                                C.ts(dn.t[:, sl], d_.t[:, sl], esk.t[:, 4 * g + h:4 * g + h + 1], None, ALU.add, None,
                                     [d_.b, esk.b], [dn.b])
                            C.act(dn.t[:], dn.t[:], AF.Ln, [dn.b], [dn.b])
                            C.act(dn.t[:], dn.t[:], AF.Exp, [dn.b], [dn.b], scale=-1.0)
                            C.tt(a_.t[:, 4 * g:4 * g + 4, :].rearrange("p h t -> p (h t)"), o_.t[:], dn.t[:], ALU.mult,
                                 [o_.b, dn.b], [a_.b])
                    C.dma("gpsimd", mixT_s[:, n * 128:(n + 1) * 128].rearrange("(h d) t -> d h t", d=128), a_.t[:], [a_.b], ())
                S.flush()

        ph = cfg.get("PHASES")
        run = lambda name: (ph is None) or (name in ph)
        own = list(range(NT_OWN))
        full0 = list(range(NF0))
        if run("l0a"):
            phase_l0a()
        if cfg["ADA"] and ADA_INTERLEAVE:
            ada_late = AdaStream(ada_es, [(0, j) for j in range(16, 48)] + [(1, j) for j in range(48)])
        if run("l0b"):
            phase_l0b()
        if run("l0c"):
            phase_l0c()
        if run("l0d"):
            phase_l0d()
        if ada_late is not None:
            ada_late.finish([(0, 4), (1, 1), (1, 4)])
            S.flush()
        ada_es.close()
        if DEBUG:
            modT_o = nc.dram_tensor("modT_o", [128, 2 * 6 * KC * 2], F32, kind="ExternalOutput").ap()
            C.dma("sync", modT_o, modT.t[:], [modT.b], ())
            S.flush()
        if run("l0e"):
            phase_outproj(w_out0[:], 0, full0, xT_in, x1T_s)
        if run("l0f"):
            phase_ffn(0, full0, x1T_s, xhT_s, x2T_s)
        if run("l1a"):
            phase_l1a()
        if run("l1b"):
            phase_l1b()
        if run("l1c"):
            phase_outproj(w_out1[:], 1, own, x2T_s, x1T_s)
        if run("l1d"):
            phase_ffn(1, own, x1T_s, xhT_s, yT)
        S.final_wait()
    return nc


def make_consts():
    i = np.arange(128)[:, None]
    t = np.arange(128)[None, :]
    c = np.zeros((128, 8, 128), np.float32)
    c[:, 0] = (i == t)
    c[:, 1] = (i <= t)
    c[:, 2] = (i >= t)
    c[:, 3] = (i > t)
    c[:, 4] = (i < t)
    return c


def rope_tables(seq, grid_w=64, head_dim=128, theta=10000.0):
    rows = seq // grid_w
    row = np.repeat(np.arange(rows, dtype=np.float32), grid_w)
    col = np.tile(np.arange(grid_w, dtype=np.float32), rows)
    n_freq = head_dim // 4
    inv_freq = (np.float32(theta) ** (-np.arange(n_freq, dtype=np.float32) / n_freq)).astype(np.float32)
    ang = np.concatenate([row[:, None] * inv_freq, col[:, None] * inv_freq], axis=-1).astype(np.float32)
    return np.cos(ang).astype(np.float32), np.sin(ang).astype(np.float32)


def host_inputs(cfg, inp, core):
    NT_OWN, NT_HALO, NT_REST, NT_CTX, FH = cfg["NT_OWN"], cfg["NT_HALO"], cfg["NT_REST"], cfg["NT_CTX"], cfg["FH"]
    NTF = NT_OWN + NT_HALO
    b, flip = core // 2, core % 2
    f32 = lambda a: np.ascontiguousarray(a, dtype=np.float32)
    x = np.asarray(inp["x"][b])
    ctx = np.asarray(inp["ctx"][b])
    if flip:
        x = x[::-1]
        ctx = ctx[::-1]
    tok = np.concatenate([x[:NTF * 128], ctx, x[NTF * 128:]], axis=0)
    m = {}
    m["xT"] = f32(tok.T)
    cc = np.stack([np.asarray(inp["c"][b]), np.asarray(inp["c_ctx"])], axis=-1)
    m["cT"] = f32(cc.reshape(KC, 128, 2).transpose(1, 0, 2))
    if cfg["ADA"]:
        m["ada_w"] = inp["_ada_w"]
        m["ada_bT"] = inp["_ada_bT"]
    else:
        m["modT_in"] = inp["_modT"][core]
    m["w_in0"] = inp["_w_in0"]
    wi = np.asarray(inp["even_w_in"][0])
    gf, gb = wi[:, 6144:6160], wi[:, 6160:6176]
    w2f, w2b = np.asarray(inp["even_gate_w2_fwd"][0]), np.asarray(inp["even_gate_w2_bwd"][0])
    b2f, b2b = np.asarray(inp["even_gate_b_fwd"][0]), np.asarray(inp["even_gate_b_bwd"][0])
    if flip:
        gf, gb, w2f, w2b, b2f, b2b = gb, gf, w2b, w2f, b2b, b2f
    m["w_g"] = f32(np.concatenate([gf, gb], axis=1))
    w2 = np.zeros((33, 2048), np.float32)
    w2[0:16, 0:1024] = w2f
    w2[16:32, 1024:2048] = w2b
    w2[32, 0:1024] = b2f
    w2[32, 1024:2048] = b2b
    m["w2blk"] = w2
    m["gla_gain_bc"] = f32(np.broadcast_to(np.asarray(inp["even_gla_norm_gain"][0])[None, :], (128, 512)))
    m["sgainT"] = f32(np.asarray(inp["even_sgu_norm_gain"][0]).reshape(16, 128).T)
    ws = np.asarray(inp["even_sgu_w_s"][0])
    bs = np.asarray(inp["even_sgu_b_s"][0])
    if flip:
        ws = ws[:, ::-1, ::-1]
        bs = bs[:, ::-1]
    m["wsT"] = f32(ws.transpose(2, 0, 1))
    m["bsb"] = f32(np.broadcast_to(bs[None], (128, 8, 128)))
    m["w_out0"] = inp["_w_out0"]
    m["w_in1"] = inp["_w_in1"]
    m["qg_bc"] = f32(np.broadcast_to(np.asarray(inp["odd_q_norm_gain"][0])[None, :], (128, 128)))
    m["kg_bc"] = f32(np.broadcast_to(np.asarray(inp["odd_k_norm_gain"][0])[None, :], (128, 128)))
    m["sink_bc"] = f32(np.broadcast_to(np.asarray(inp["odd_sink"][0])[None, :], (128, 32)))
    m["w_out1"] = inp["_w_out1"]
    m["ffn_g"] = inp["_ffn_g"]
    m["ffn_u"] = inp["_ffn_u"]
    m["ffn_d"] = inp["_ffn_d"]
    cos, sin = inp["_rope"]
    if flip:
        cos, sin = cos[::-1], sin[::-1]
    cs = np.stack([np.tile(cos[:NTF * 128], (1, 4)), np.tile(sin[:NTF * 128], (1, 4))], axis=1)
    m["rope_cs"] = f32(cs)
    m["consts"] = inp["_consts"]
    return m


def prep_shared(cfg, inputs):
    inp = dict(inputs)
    f32 = lambda a: np.ascontiguousarray(a, dtype=np.float32)
    wi = np.asarray(inputs["even_w_in"][0])
    inp["_w_in0"] = f32(np.concatenate([wi[:, 0:6144], wi[:, 6176:10272]], axis=1))
    inp["_w_out0"] = f32(inputs["even_w_out"][0])
    inp["_w_in1"] = f32(inputs["odd_w_in"][0])
    inp["_w_out1"] = f32(inputs["odd_w_out"][0])
    inp["_ffn_g"] = f32(inputs["ffn_w_gate"])
    inp["_ffn_u"] = f32(inputs["ffn_w_up"])
    inp["_ffn_d"] = f32(inputs["ffn_w_down"])
    if cfg["ADA"]:
        inp["_ada_w"] = f32(inputs["ada_w"])
        inp["_ada_bT"] = f32(np.asarray(inputs["ada_b"]).reshape(2, 6, KC, 128).transpose(3, 0, 1, 2))
    seq = (cfg["NT_OWN"] + cfg["NT_HALO"] + cfg["NT_REST"]) * 128
    inp["_rope"] = rope_tables(seq)
    inp["_consts"] = make_consts()
    return inp


def kernel(**inputs):
    cfg = default_cfg()
    B = inputs["x"].shape[0]
    n_cores = 2 * B
    inp = prep_shared(cfg, inputs)
    nc = build_program(cfg)
    in_maps = [host_inputs(cfg, inp, c) for c in range(n_cores)]
    res = run_bass_kernel_spmd(nc, in_maps, core_ids=list(range(n_cores)))
    L = inputs["x"].shape[1]
    half = cfg["NT_OWN"] * 128
    out = np.empty((B, L, D), np.float32)
    for c in range(n_cores):
        y = np.asarray(res.results[c]["yT"]).T
        b, flip = c // 2, c % 2
        if flip:
            out[b, L - half:] = y[::-1]
        else:
            out[b, :half] = y
    return out
```
